# Optimizing a Trainium2 kernel written in Bass

```python
import jax, jax.numpy as jnp
from jax import lax
import numpy as np

D_MODEL = 1024
BATCH = 8
SEQ = 2048
DEPTH = 1

HEAD_DIM = 64
NSA_HEADS = 8
NSA_KV_GROUPS = 2
NSA_HPG = NSA_HEADS // NSA_KV_GROUPS
FOX_HEADS = 8
NSA_WIDTH = NSA_HEADS * HEAD_DIM
FOX_WIDTH = FOX_HEADS * HEAD_DIM
NSA_KV_WIDTH = NSA_KV_GROUPS * HEAD_DIM
CMP_BLOCK = 32
CMP_STRIDE = 16
CMP_HIDDEN = 128
SEL_BLOCK = 64
N_SELECT = 16
WINDOW = 512
Q_BLOCK = 128
SEL_Q_BLOCK = 32
D_FF = 4 * D_MODEL
NORM_EPS = 1e-6
NEG_INF = -1e30
FORCE_SCORE = 1e9
IN_SIZES = (NSA_WIDTH,) + (NSA_KV_WIDTH,) * 6 + (3 * NSA_HEADS, FOX_WIDTH, FOX_WIDTH, FOX_WIDTH, FOX_HEADS)
IN_WIDTH = sum(IN_SIZES)
IN_OFFSETS = tuple(int(o) for o in np.cumsum(IN_SIZES)[:-1])

kernel_name = "hybrid_nsa_fox_gated_block"


def rms_norm(x, g):
    xf = x.astype(jnp.float32)
    y = xf * lax.rsqrt(jnp.mean(xf * xf, axis=-1, keepdims=True) + NORM_EPS)
    return (y * g.astype(jnp.float32)).astype(x.dtype)


def alibi_slopes(n):
    return jnp.exp2(-8.0 * jnp.arange(1, n + 1, dtype=jnp.float32) / n)


def masked_softmax(s, valid):
    p = jax.nn.softmax(jnp.where(valid, s, NEG_INF), axis=-1)
    return jnp.where(valid, p, 0.0)


def compress_blocks(kv, pos, w1, w2):
    b, g, t, dk = kv.shape
    nc = (t - CMP_BLOCK) // CMP_STRIDE + 1
    idx = np.arange(nc)[:, None] * CMP_STRIDE + np.arange(CMP_BLOCK)[None, :]
    blocks = kv[:, :, idx] + pos.astype(kv.dtype)
    flat = blocks.reshape(b, g, nc, CMP_BLOCK * dk)
    return jax.nn.gelu(flat @ w1) @ w2


def nsa_attention(q, kc, vc, ks, vs, kw, vw, gates, pos_k, w1_k, w2_k, pos_v, w1_v, w2_v):
    b, g, hg, t, dk = q.shape
    scale = dk ** -0.5
    slopes = alibi_slopes(NSA_HEADS).reshape(g, hg)
    tq_all = jnp.arange(t)

    kcmp = compress_blocks(kc, pos_k, w1_k, w2_k)
    vcmp = compress_blocks(vc, pos_v, w1_v, w2_v)
    nc = kcmp.shape[2]
    cmp_start = jnp.arange(nc) * CMP_STRIDE
    cmp_end = cmp_start + CMP_BLOCK - 1
    dist_c = (tq_all[:, None] - cmp_end[None, :]).astype(jnp.float32)
    s_c = jnp.einsum('bghtd,bgnd->bghtn', q, kcmp).astype(jnp.float32) * scale - slopes[:, :, None, None] * dist_c
    p_cmp = masked_softmax(s_c, dist_c >= 0)
    o_cmp = jnp.einsum('bghtn,bgnd->bghtd', p_cmp.astype(vcmp.dtype), vcmp)

    ns = t // SEL_BLOCK
    sel_start = jnp.arange(ns) * SEL_BLOCK
    overlap = ((cmp_start[:, None] <= sel_start[None, :] + SEL_BLOCK - 1) & (cmp_end[:, None] >= sel_start[None, :])).astype(jnp.float32)
    imp = jnp.einsum('bghtn,nj->bgtj', p_cmp, overlap)
    cur = tq_all // SEL_BLOCK
    jj = jnp.arange(ns)
    forced = (jj[None, :] == 0) | (jj[None, :] == cur[:, None]) | (jj[None, :] == cur[:, None] - 1)
    valid_blk = sel_start[None, :] <= tq_all[:, None]
    score = jnp.where(forced, FORCE_SCORE, jnp.where(valid_blk, imp, -1.0))
    n_sel = min(N_SELECT, ns)
    _, sel_idx = lax.top_k(score, n_sel)

    ks_blocks = ks.reshape(b, g, ns, SEL_BLOCK, dk)
    vs_blocks = vs.reshape(b, g, ns, SEL_BLOCK, dk)
    nq = t // SEL_Q_BLOCK
    q_ch = jnp.moveaxis(q.reshape(b, g, hg, nq, SEL_Q_BLOCK, dk), 3, 0)
    idx_ch = jnp.moveaxis(sel_idx.reshape(b, g, nq, SEL_Q_BLOCK, n_sel), 2, 0)
    bi = jnp.arange(b)[:, None, None, None]
    gi = jnp.arange(g)[None, :, None, None]

    def sel_chunk(args):
        c, qc, ic = args
        kg = ks_blocks[bi, gi, ic]
        vg = vs_blocks[bi, gi, ic]
        tq = c * SEL_Q_BLOCK + jnp.arange(SEL_Q_BLOCK)
        pos = ic[..., None] * SEL_BLOCK + jnp.arange(SEL_BLOCK)
        dist = (tq[:, None, None] - pos).astype(jnp.float32)
        s = jnp.einsum('bghcd,bgcnkd->bghcnk', qc, kg).astype(jnp.float32) * scale - slopes[:, :, None, None, None] * dist[:, :, None]
        s = s.reshape(b, g, hg, SEL_Q_BLOCK, n_sel * SEL_BLOCK)
        valid = (dist >= 0).reshape(b, g, 1, SEL_Q_BLOCK, n_sel * SEL_BLOCK)
        p = masked_softmax(s, valid)
        return jnp.einsum('bghcm,bgcmd->bghcd', p.astype(vg.dtype), vg.reshape(b, g, SEL_Q_BLOCK, n_sel * SEL_BLOCK, dk))

    o_sel = lax.map(sel_chunk, (jnp.arange(nq), q_ch, idx_ch))
    o_sel = jnp.moveaxis(o_sel, 0, 3).reshape(b, g, hg, t, dk)

    span = Q_BLOCK + WINDOW
    kw_pad = jnp.pad(kw, ((0, 0), (0, 0), (WINDOW, 0), (0, 0)))
    vw_pad = jnp.pad(vw, ((0, 0), (0, 0), (WINDOW, 0), (0, 0)))
    nb = t // Q_BLOCK
    q_blk = jnp.moveaxis(q.reshape(b, g, hg, nb, Q_BLOCK, dk), 3, 0)

    def win_chunk(args):
        c, qc = args
        start = c * Q_BLOCK
        kb = lax.dynamic_slice_in_dim(kw_pad, start, span, axis=2)
        vb = lax.dynamic_slice_in_dim(vw_pad, start, span, axis=2)
        tq = start + jnp.arange(Q_BLOCK)
        sk = start - WINDOW + jnp.arange(span)
        d = tq[:, None] - sk[None, :]
        valid = (sk[None, :] >= 0) & (d >= 0) & (d < WINDOW)
        s = jnp.einsum('bghqd,bgkd->bghqk', qc, kb).astype(jnp.float32) * scale - slopes[:, :, None, None] * d.astype(jnp.float32)
        p = masked_softmax(s, valid)
        return jnp.einsum('bghqk,bgkd->bghqd', p.astype(vb.dtype), vb)

    o_win = lax.map(win_chunk, (jnp.arange(nb), q_blk))
    o_win = jnp.moveaxis(o_win, 0, 3).reshape(b, g, hg, t, dk)

    return gates[0] * o_cmp + gates[1] * o_sel + gates[2] * o_win


def forgetting_attention(q, k, v, f_logit):
    b, h, t, dk = q.shape
    scale = dk ** -0.5
    csum = jnp.cumsum(jax.nn.log_sigmoid(f_logit.astype(jnp.float32)), axis=-1)
    nb = t // Q_BLOCK
    q_blk = jnp.moveaxis(q.reshape(b, h, nb, Q_BLOCK, dk), 2, 0)
    c_blk = jnp.moveaxis(csum.reshape(b, h, nb, Q_BLOCK), 2, 0)
    tk = jnp.arange(t)

    def blk(args):
        c, qc, cq = args
        tq = c * Q_BLOCK + jnp.arange(Q_BLOCK)
        valid = tk[None, :] <= tq[:, None]
        s = jnp.einsum('bhqd,bhkd->bhqk', qc, k).astype(jnp.float32) * scale + cq[..., None] - csum[:, :, None, :]
        p = masked_softmax(s, valid)
        return jnp.einsum('bhqk,bhkd->bhqd', p.astype(v.dtype), v)

    o = lax.map(blk, (jnp.arange(nb), q_blk, c_blk))
    return jnp.moveaxis(o, 0, 2).reshape(b, h, t, dk)


def setup_inputs(seed: int = 0) -> dict:
    key = jax.random.key(seed)
    ks = jax.random.split(key, 20)
    L = DEPTH
    nrm = lambda k, shape, fan: jax.random.normal(k, shape, jnp.float32) * fan ** -0.5
    gain = lambda k, shape: 1.0 + 0.01 * jax.random.normal(k, shape, jnp.float32)
    return {
        "x": jax.random.normal(ks[0], (BATCH, SEQ, D_MODEL), jnp.float32),
        "norm_mix": gain(ks[1], (L, D_MODEL)),
        "w_in": nrm(ks[2], (L, D_MODEL, IN_WIDTH), D_MODEL),
        "cmp_pos_k": 0.1 * jax.random.normal(ks[3], (L, CMP_BLOCK, HEAD_DIM), jnp.float32),
        "cmp_w1_k": nrm(ks[4], (L, CMP_BLOCK * HEAD_DIM, CMP_HIDDEN), CMP_BLOCK * HEAD_DIM),
        "cmp_w2_k": nrm(ks[5], (L, CMP_HIDDEN, HEAD_DIM), CMP_HIDDEN),
        "cmp_pos_v": 0.1 * jax.random.normal(ks[6], (L, CMP_BLOCK, HEAD_DIM), jnp.float32),
        "cmp_w1_v": nrm(ks[7], (L, CMP_BLOCK * HEAD_DIM, CMP_HIDDEN), CMP_BLOCK * HEAD_DIM),
        "cmp_w2_v": nrm(ks[8], (L, CMP_HIDDEN, HEAD_DIM), CMP_HIDDEN),
        "fox_f_bias": jax.random.uniform(ks[9], (L, FOX_HEADS), jnp.float32, 1.0, 6.0),
        "w_branch_nsa": nrm(ks[10], (L, NSA_WIDTH, D_MODEL), NSA_WIDTH),
        "w_branch_fox": nrm(ks[11], (L, FOX_WIDTH, D_MODEL), FOX_WIDTH),
        "w_merge_gate": nrm(ks[12], (L, D_MODEL, 2 * D_MODEL), D_MODEL),
        "b_merge_gate": 0.01 * jax.random.normal(ks[13], (L, 2 * D_MODEL), jnp.float32),
        "w_out": nrm(ks[14], (L, D_MODEL, D_MODEL), D_MODEL),
        "norm_mlp": gain(ks[15], (L, D_MODEL)),
        "w_up": nrm(ks[16], (L, D_MODEL, D_FF), D_MODEL),
        "w_down": nrm(ks[17], (L, D_FF, D_MODEL), D_FF),
        "norm_final": gain(ks[18], (D_MODEL,)),
    }


def reference(x, norm_mix, w_in, cmp_pos_k, cmp_w1_k, cmp_w2_k, cmp_pos_v, cmp_w1_v, cmp_w2_v, fox_f_bias, w_branch_nsa, w_branch_fox, w_merge_gate, b_merge_gate, w_out, norm_mlp, w_up, w_down, norm_final):
    b, t, _ = x.shape
    g, hg = NSA_KV_GROUPS, NSA_HPG
    h = x
    for i in range(DEPTH):
        u = rms_norm(h, norm_mix[i])
        parts = jnp.split(u @ w_in[i], IN_OFFSETS, axis=-1)
        q_n, kc, vc, ks_, vs_, kw, vw, gate_n, q_f, k_f, v_f, f_l = parts
        q_n = q_n.reshape(b, t, NSA_HEADS, HEAD_DIM).transpose(0, 2, 1, 3).reshape(b, g, hg, t, HEAD_DIM)
        kvs = [a.reshape(b, t, g, HEAD_DIM).transpose(0, 2, 1, 3) for a in (kc, vc, ks_, vs_, kw, vw)]
        gates = jax.nn.sigmoid(gate_n.astype(jnp.float32)).astype(u.dtype)
        gates = gates.reshape(b, t, 3, NSA_HEADS).transpose(2, 0, 3, 1).reshape(3, b, g, hg, t)[..., None]
        o_n = nsa_attention(q_n, *kvs, gates, cmp_pos_k[i], cmp_w1_k[i], cmp_w2_k[i], cmp_pos_v[i], cmp_w1_v[i], cmp_w2_v[i])
        o_n = o_n.reshape(b, NSA_HEADS, t, HEAD_DIM).transpose(0, 2, 1, 3).reshape(b, t, NSA_WIDTH)
        to_heads = lambda a: a.reshape(b, t, FOX_HEADS, HEAD_DIM).transpose(0, 2, 1, 3)
        f_logit = (f_l + fox_f_bias[i]).transpose(0, 2, 1)
        o_f = forgetting_attention(to_heads(q_f), to_heads(k_f), to_heads(v_f), f_logit)
        o_f = o_f.transpose(0, 2, 1, 3).reshape(b, t, FOX_WIDTH)
        mg = jax.nn.sigmoid((u @ w_merge_gate[i] + b_merge_gate[i]).astype(jnp.float32)).astype(u.dtype)
        g_a, g_b = jnp.split(mg, 2, axis=-1)
        y = g_a * (o_n @ w_branch_nsa[i]) + g_b * (o_f @ w_branch_fox[i])
        h = h + y @ w_out[i]
        v = rms_norm(h, norm_mlp[i])
        h = h + jnp.square(jax.nn.relu(v @ w_up[i])) @ w_down[i]
    return rms_norm(h, norm_final)
```

```python
import numpy as np
from contextlib import ExitStack
import concourse.bass as bass
import concourse.mybir as mybir
from concourse.bass_utils import run_bass_kernel_spmd

F32 = mybir.dt.float32
BF16 = mybir.dt.bfloat16
AF = mybir.ActivationFunctionType
ALU = mybir.AluOpType

ENGS = ['pe', 'act', 'dve', 'pool', 'sp']
NDS = 40
NDS_SW = 8
NEG = -30000.0
T = 2048
D = 1024
NSLOT = 4


class Prog:
    def __init__(self, nc, ctx):
        self.nc = nc
        self.ops = {e: [] for e in ENGS}
        self.esem = {e: ctx.enter_context(nc.semaphore("es_" + e)) for e in ENGS}
        self.ecnt = {e: 0 for e in ENGS}
        self.dsems = [ctx.enter_context(nc.semaphore("ds%d" % i)) for i in range(NDS)]
        self.dcnt = [0] * NDS
        self.dnext = 0
        self.dnext_sw = 0
        self.seen = {e: {} for e in ENGS}
        self.lastw = {}
        self.reads = {}
        self.pending = {e: False for e in ENGS}

    def _deps(self, eng, reads, writes, extra=()):
        deps = list(extra)
        for k in reads:
            t = self.lastw.get(k)
            if t is not None:
                deps.append(t)
        for k in writes:
            t = self.lastw.get(k)
            if t is not None:
                deps.append(t)
            deps.extend(self.reads.get(k, ()))
        waits = {}
        for (s, v) in deps:
            if s == ('e', eng) and eng == 'pe':
                continue
            if self.seen[eng].get(s, 0) >= v:
                continue
            if waits.get(s, 0) < v:
                waits[s] = v
        for s, v in waits.items():
            self.seen[eng][s] = v
        return list(waits.items())

    def _commit(self, token, reads, writes):
        for k in writes:
            self.lastw[k] = token
            self.reads[k] = []
        for k in reads:
            self.reads.setdefault(k, []).append(token)

    def emit(self, eng, fn, reads=(), writes=(), inc=True):
        waits = self._deps(eng, reads, writes)
        for (s, v) in waits:
            if s == ('e', eng):
                assert v <= self.ecnt[eng], "self-wait on un-inc'd instruction"
        if inc:
            self.ecnt[eng] += 1
            token = (('e', eng), self.ecnt[eng])
            self.ops[eng].append((waits, fn, ('e', eng)))
            self.pending[eng] = False
        else:
            token = (('e', eng), self.ecnt[eng] + 1)
            self.ops[eng].append((waits, fn, None))
            self.pending[eng] = True
        self._commit(token, reads, writes)
        return token

    def dma(self, eng, out, in_, reads=(), writes=(), **kw):
        if eng == 'pool':
            i = self.dnext_sw
            self.dnext_sw = (self.dnext_sw + 1) % NDS_SW
        else:
            i = NDS_SW + self.dnext
            self.dnext = (self.dnext + 1) % (NDS - NDS_SW)
        extra = []
        if self.dcnt[i] > 0:
            extra.append((('d', i), self.dcnt[i]))
        waits = self._deps(eng, reads, writes, extra)
        self.dcnt[i] += 16
        token = (('d', i), self.dcnt[i])
        fn = (lambda e, out=out, in_=in_, kw=kw: e.dma_start(out=out, in_=in_, **kw))
        self.ops[eng].append((waits, fn, ('d', i)))
        self._commit(token, reads, writes)
        return token

    def barrier(self):
        for e in ENGS:
            assert not self.pending[e], e
        allw = []
        for e in ENGS:
            if self.ecnt[e] > 0:
                allw.append((('e', e), self.ecnt[e]))
        for i in range(NDS):
            if self.dcnt[i] > 0:
                allw.append((('d', i), self.dcnt[i]))
        for e in ENGS:
            waits = [(s, v) for (s, v) in allw if self.seen[e].get(s, 0) < v]
            for s, v in waits:
                self.seen[e][s] = v
            self.ops[e].append((waits, None, None))
        self.lastw = {}
        self.reads = {}

    def _sem(self, s):
        return self.esem[s[1]] if s[0] == 'e' else self.dsems[s[1]]

    def replay(self):
        nc = self.nc
        hmap = {'pe': 'tensor', 'act': 'scalar', 'dve': 'vector', 'pool': 'gpsimd', 'sp': 'sync'}
        with nc.Block() as block:
            for e in ENGS:
                ops = self.ops[e]

                def body(engine, ops=ops):
                    for (waits, fn, inc) in ops:
                        for (s, v) in waits:
                            engine.wait_ge(self._sem(s), v)
                        if fn is None:
                            continue
                        ins = fn(engine)
                        if inc is not None:
                            ins.then_inc(self._sem(inc), 1 if inc[0] == 'e' else 16)
                getattr(block, hmap[e])(body)


BLK = {}
_names = (['W1K', 'W1V', 'KV', 'TM', 'QN'] + ['FOX%d' % p for p in range(4)] + ['MG%d' % j for j in range(4)]
          + ['BR0', 'BR1', 'OUT0', 'OUT1'] + ['UP%d' % f for f in range(8)] + ['DOWN%d' % f for f in range(8)])
for _i, _n in enumerate(_names):
    BLK[_n] = _i
NBLK = len(_names)

STREAM = ['KV', 'TM', 'QN', 'W1K', 'W1V', 'FOX0', 'FOX1', 'FOX2', 'FOX3']
for _half in range(2):
    STREAM += ['MG0', 'BR0', 'MG1', 'MG2', 'BR1', 'MG3', 'OUT0', 'OUT1']
STREAM += ['UP0']
for _f in range(8):
    if _f + 1 < 8:
        STREAM.append('UP%d' % (_f + 1))
    STREAM.append('DOWN%d' % _f)


def _std_block(W):
    W = np.asarray(W, np.float32)
    if W.shape[1] < 512:
        Wp = np.zeros((W.shape[0], 512), np.float32)
        Wp[:, :W.shape[1]] = W
        W = Wp
    return W.reshape(8, 128, 512).transpose(1, 0, 2).reshape(128, 4096)


def _w1_block(w1):
    a = np.asarray(w1, np.float32).reshape(32, 64, 128).transpose(1, 0, 2)
    return np.concatenate([a, a], 0).reshape(128, 4096)


def host_weights(inp):
    w_in = np.asarray(inp['w_in'][0], np.float32)
    q_n, kc, vc, ks, vs, kw, vw = (w_in[:, 0:512], w_in[:, 512:640], w_in[:, 640:768], w_in[:, 768:896],
                                  w_in[:, 896:1024], w_in[:, 1024:1152], w_in[:, 1152:1280])
    gate, q_f, k_f, v_f, f_l = (w_in[:, 1280:1304], w_in[:, 1304:1816], w_in[:, 1816:2328], w_in[:, 2328:2840],
                                w_in[:, 2840:2848])
    wall = np.zeros((NBLK, 128, 4096), np.float32)
    wall[BLK['W1K']] = _w1_block(inp['cmp_w1_k'][0])
    wall[BLK['W1V']] = _w1_block(inp['cmp_w1_v'][0])
    wall[BLK['KV']] = _std_block(np.concatenate([kc, vc, ks, kw], 1))
    wall[BLK['TM']] = _std_block(np.concatenate([vs, vw, gate, f_l], 1))
    wall[BLK['QN']] = _std_block(q_n)
    for p in range(4):
        sl = slice(128 * p, 128 * p + 128)
        wall[BLK['FOX%d' % p]] = _std_block(np.concatenate([q_f[:, sl], k_f[:, sl], v_f[:, sl]], 1))
    w_mg = np.asarray(inp['w_merge_gate'][0], np.float32)
    for j in range(4):
        cols = [w_mg[:, 128 * (2 * j):128 * (2 * j) + 128], w_mg[:, 128 * (2 * j + 1):128 * (2 * j + 1) + 128],
                w_mg[:, 1024 + 128 * (2 * j):1024 + 128 * (2 * j) + 128],
                w_mg[:, 1024 + 128 * (2 * j + 1):1024 + 128 * (2 * j + 1) + 128]]
        wall[BLK['MG%d' % j]] = _std_block(np.concatenate(cols, 1))
    w_br = np.concatenate([np.asarray(inp['w_branch_nsa'][0], np.float32), np.asarray(inp['w_branch_fox'][0], np.float32)], 0)
    w_out = np.asarray(inp['w_out'][0], np.float32)
    for j in range(2):
        wall[BLK['BR%d' % j]] = _std_block(w_br[:, 512 * j:512 * j + 512])
        wall[BLK['OUT%d' % j]] = _std_block(w_out[:, 512 * j:512 * j + 512])
    w_up = np.asarray(inp['w_up'][0], np.float32)
    w_down = np.asarray(inp['w_down'][0], np.float32)
    for f in range(8):
        wall[BLK['UP%d' % f]] = _std_block(w_up[:, 512 * f:512 * f + 512])
        wall[BLK['DOWN%d' % f]] = w_down[512 * f:512 * f + 512, :].reshape(4, 128, 1024).transpose(1, 0, 2).reshape(128, 4096)
    gfin = np.ascontiguousarray(np.broadcast_to(np.asarray(inp['norm_final'], np.float32)[None, :], (128, 1024)))
    prm = np.zeros((128, 240), np.float32)
    prm[:, 0:8] = np.asarray(inp['norm_mix'][0], np.float32).reshape(8, 128).T
    prm[:, 8:16] = np.asarray(inp['norm_mlp'][0], np.float32).reshape(8, 128).T
    prm[:, 16:24] = np.asarray(inp['norm_final'], np.float32).reshape(8, 128).T
    prm[:, 24:40] = np.asarray(inp['b_merge_gate'][0], np.float32).reshape(16, 128).T
    prm[:, 40:48] = np.asarray(inp['fox_f_bias'][0], np.float32)[None, :]
    prm[0:64, 48:80] = np.asarray(inp['cmp_pos_k'][0], np.float32).T
    prm[0:64, 80:112] = np.asarray(inp['cmp_pos_v'][0], np.float32).T
    prm[:, 112:176] = np.asarray(inp['cmp_w2_k'][0], np.float32)
    prm[:, 176:240] = np.asarray(inp['cmp_w2_v'][0], np.float32)
    return wall, prm, gfin


def host_consts():
    C = {}
    p = np.arange(128)
    t = np.arange(T)
    hi = (128 * (t // 128)).astype(np.float32)
    lo = (t % 128).astype(np.float32)
    slopes = (2.0 ** (-8.0 * np.arange(1, 9) / 8)).astype(np.float32)
    C['c_ident'] = np.eye(128, dtype=np.float32)
    idf = np.zeros((128, 512), np.float32)
    idf[:, 0:128] = np.eye(128, dtype=np.float32)
    idf[:, 128:256] = 1.0
    idf[:, 256:384] = (p[:, None] <= p[None, :])
    idf[:, 384:512] = ((p[:, None] // 16 == p[None, :] // 16) & (p[:, None] % 16 < p[None, :] % 16))
    C['c_identf'] = idf
    tri = np.zeros((128, 640), np.float32)
    tri[:, 0:128] = np.where(p[:, None] <= p[None, :], 0.0, NEG)
    tri[:, 128:256] = np.where(p[None, :] < p[:, None], 0.0, NEG)
    tri[:, 256:384] = (p[:, None] <= p[None, :]).astype(np.float32)
    tri[:, 384:512] = (p[:, None] <= p[None, :])
    tri[:, 512:640] = (p[None, :] < p[:, None])
    C['c_tri'] = tri
    n = np.arange(128)
    ce = 16 * n + 31
    cm = np.where((t[None, :] >= ce[:, None]) & (n[:, None] < 127), 0.0, NEG).astype(np.float32)
    C['c_cmpmask'] = cm
    tt = (np.arange(16)[None, :] * 128 + p[:, None])
    jj = np.arange(32)
    cur = tt // 64
    forced = (jj[None, None, :] == 0) | (jj[None, None, :] == cur[:, :, None]) | (jj[None, None, :] == cur[:, :, None] - 1)
    valid = (jj[None, None, :] * 64) <= tt[:, :, None]
    V = (valid & ~forced).astype(np.float32)
    Cc = np.where(forced, 100.0 + jj[None, None, :], np.where(valid, 0.0, -1.0 - 0.01 * jj[None, None, :])).astype(np.float32)
    C['c_topk'] = np.concatenate([V.reshape(128, 512), Cc.reshape(128, 512)], 1)
    cs = 16 * n
    ov = ((cs[:, None] <= jj[None, :] * 64 + 63) & (ce[:, None] >= jj[None, :] * 64) & (n[:, None] < 127)).astype(np.float32)
    C['c_overlap'] = ov
    alq = np.zeros((8, 4, T), np.float32)
    for h in range(8):
        alq[h, 0] = -slopes[h] * hi
        alq[h, 1] = -slopes[h] * lo
        alq[h, 2] = slopes[h]
        alq[h, 3] = slopes[h]
    C['c_alq'] = alq
    alk = np.zeros((4, T), np.float32)
    alk[0] = 1.0
    alk[1] = 1.0
    alk[2] = hi
    alk[3] = lo
    C['c_alk'] = alk
    alkc = np.zeros((4, 128), np.float32)
    alkc[0, :127] = 1.0
    alkc[1, :127] = 1.0
    alkc[2, :127] = (128 * (ce[:127] // 128))
    alkc[3, :127] = (ce[:127] % 128)
    C['c_alkc'] = alkc
    E = np.zeros((32, T), np.float32)
    E[t // 64, t] = 1.0
    C['c_E'] = E
    return C


CONST_SHAPES = {'c_ident': [128, 128], 'c_tri': [128, 640], 'c_cmpmask': [128, T], 'c_topk': [128, 1024],
                'c_overlap': [128, 32], 'c_identf': [128, 512], 'c_alq': [8, 4, T], 'c_alk': [4, T], 'c_alkc': [4, 128], 'c_E': [32, T]}


def build_program(debug=(), stop_after=None):
    nc = bass.Bass("TRN2", target_bir_lowering=False)
    x_d = nc.dram_tensor("x", [T, D], F32, kind="ExternalInput").ap()
    wall_d = nc.dram_tensor("wall", [NBLK, 128, 4096], F32, kind="ExternalInput").ap()
    prm_d = nc.dram_tensor("prm", [128, 240], F32, kind="ExternalInput").ap()
    gfin_d = nc.dram_tensor("gfin", [128, 1024], F32, kind="ExternalInput").ap()
    cd = {}
    for name, shp in CONST_SHAPES.items():
        dt_ = F32 if name in ('c_topk', 'c_identf') else BF16
        cd[name] = nc.dram_tensor(name, shp, dt_, kind="ExternalInput").ap()
    out_d = nc.dram_tensor("out", [T, D], F32, kind="ExternalOutput").ap()
    DBG_SHAPES = {'uT': ([128, 8 * T], BF16), 'kcT': ([128, T], BF16), 'ksaug0': ([128, T], BF16), 'gates': ([128, 16 * 24], F32),
                  'vsaug': ([128, 16 * 2 * 65], BF16), 'kcmpaug0': ([128, 128], BF16), 'vcmpaug0': ([128, 104], BF16),
                  'qn0': ([100, T], BF16), 'otm': ([128, 16 * 1024], BF16), 'impacc': ([128, 16 * 2 * 32], F32),
                  'selb': ([128, 64 + 1024], BF16), 'cneg': ([128, 128], F32), 'fq0': ([70, T], BF16), 'fk0': ([70, T], BF16),
                  'oT': ([128, 8 * T], BF16), 'hT': ([128, 8 * T], F32), 'yT': ([128, 8 * 1024], BF16), 'vT': ([128, 8 * T], BF16)}
    dbg_d = {}
    for name in debug:
        shp, dt_ = DBG_SHAPES[name]
        dbg_d[name] = nc.dram_tensor("dbg_" + name, shp, dt_, kind="ExternalOutput").ap()

    with ExitStack() as ctx:
        P = Prog(nc, ctx)

        def E(eng, method, reads=(), writes=(), inc=True, **kw):
            return P.emit(eng, (lambda e, method=method, kw=kw: getattr(e, method)(**kw)), reads, writes, inc)

        def sbt(name, shape, dt_):
            return ctx.enter_context(nc.sbuf_tensor(name, shape, dt_))

        def dump(name, ap2d, reads=()):
            if name in dbg_d:
                P.dma('sp', dbg_d[name], ap2d, reads=reads)

        uT_t = sbt("uT", [128, 8 * T], BF16)
        uT = uT_t[:, :].rearrange("p (c t) -> p c t", c=8)
        arena = sbt("arena", [128, NSLOT * 4096], BF16)
        identb = sbt("identb", [128, 128], BF16)
        identf = sbt("identf", [128, 512], F32)
        trib = sbt("trib", [128, 640], BF16)
        prm = sbt("prm_s", [128, 240], F32)
        RG_BYTES = 142336
        RG = sbt("RG", [128, RG_BYTES // 2], BF16)
        banks = [ctx.enter_context(nc.psum_tensor("bank%d" % i, [128, 512], F32)) for i in range(8)]

        class Carver:
            def __init__(self):
                self.off = 0

            def alloc(self, dims, dt_):
                n = 1
                for d_ in dims:
                    n *= d_
                nb = n * (4 if dt_ == F32 else 2)
                nb = (nb + 63) // 64 * 64
                assert self.off + nb <= RG_BYTES, (self.off, nb)
                ap = RG[:, self.off // 2:(self.off + nb) // 2]
                self.off += nb
                if dt_ == F32:
                    ap = ap.bitcast(F32)
                ap = ap[:, 0:n]
                if len(dims) == 2:
                    ap = ap.rearrange("p (a b) -> p a b", a=dims[0])
                elif len(dims) == 3:
                    ap = ap.rearrange("p (a b c) -> p a b c", a=dims[0], b=dims[1])
                return ap

        class WStream:
            def __init__(self):
                self.loaded = 0
                self.released = set()

            def slot_ap(self, pos):
                s = pos % NSLOT
                return arena[:, s * 4096:(s + 1) * 4096]

            def pump(self):
                while self.loaded < len(STREAM) and (self.loaded < NSLOT or (self.loaded - NSLOT) in self.released):
                    pos = self.loaded
                    P.dma('pool', self.slot_ap(pos), wall_d[BLK[STREAM[pos]]], writes=[('w', pos % NSLOT)])
                    self.loaded += 1

            def acquire(self, name):
                pos = 0
                while pos in self.released or STREAM[pos] != name:
                    pos += 1
                assert pos < self.loaded, (name, pos, self.loaded)
                return pos, self.slot_ap(pos), ('w', pos % NSLOT)

            def release(self, pos):
                self.released.add(pos)
                self.pump()

        W = WStream()

        P.dma('sp', identb[:, :], cd['c_ident'], writes=['identb'])
        P.dma('sp', prm[:, :], prm_d, writes=['prm'])

        cv = Carver()
        ksaug = [cv.alloc([T], BF16) for _ in range(2)]
        kwaug = [cv.alloc([T], BF16) for _ in range(2)]
        qn = [cv.alloc([T], BF16) for _ in range(4)]
        fq = [cv.alloc([T], BF16) for _ in range(2)]
        fk = [cv.alloc([T], BF16) for _ in range(2)]
        fv = cv.alloc([16, 2, 65], BF16)
        vsaug = cv.alloc([16, 2, 65], BF16)
        vwaug = cv.alloc([16, 2, 65], BF16)
        gates = cv.alloc([16, 24], F32)
        spx = cv.alloc([128], F32)
        cneg = cv.alloc([128], F32)
        tsum = cv.alloc([128], F32)
        cref = cv.alloc([128], F32)
        r1 = cv.alloc([128], F32)
        a123 = [cv.alloc([128], BF16) for _ in range(3)]
        otm_off = cv.off
        otm = cv.alloc([16, 1024], BF16)
        selb = cv.alloc([64 + 1024], BF16)
        impacc = cv.alloc([16, 2, 32], F32)
        kcmpaug = [cv.alloc([128], BF16) for _ in range(2)]
        vcmpaug = [cv.alloc([104], BF16) for _ in range(2)]
        NPT = 6
        PT = [cv.alloc([512], BF16) for _ in range(NPT)]
        cmpmask = cv.alloc([T], BF16)
        topkVC = cv.alloc([1024], F32)
        score = cv.alloc([16, 32], F32)
        sc2 = cv.alloc([32], F32)
        mx8 = cv.alloc([8], F32)
        rec = cv.alloc([4], F32)
        fsc = cv.alloc([4], F32)
        tmpo = cv.alloc([4, 64], F32)
        tmpi = cv.alloc([4, 32], F32)
        w2b = cv.alloc([128], BF16)
        posb = cv.alloc([64], BF16)
        cbias = cv.alloc([2], F32)
        cbias2 = cv.alloc([2], F32)
        gxs = [[cv.alloc([128], F32) for _ in range(4)] for _ in range(2)]
        ghb4 = [cv.alloc([128], BF16) for _ in range(4)]
        kcr = cv.alloc([16, 128], BF16)
        vcr = cv.alloc([16, 128], BF16)
        kcT = fq[0]
        vcT = fq[1]
        otm_flat = RG[:, otm_off // 2:otm_off // 2 + 16384]
        NXS = 4
        xs = [otm_flat[:, 2048 * j:2048 * (j + 1)].bitcast(F32) for j in range(NXS)]
        xn = [otm_flat[:, 8192 + 1024 * j:8192 + 1024 * (j + 1)] for j in range(NXS)]
        ss = otm_flat[:, 12288:12320].bitcast(F32)
        rstd = otm_flat[:, 12320:12352].bitcast(F32)

        for i0 in range(NXS):
            P.dma('sp', xs[i0], x_d[i0 * 128:(i0 + 1) * 128, :], writes=[('xs', i0)])
        P.lastw[('w', 0)] = P.lastw[('xs', 1)]
        W.pump()
        P.dma('sp', identf[:, :], cd['c_identf'], writes=['identf'])
        P.dma('sp', trib[:, :], cd['c_tri'], writes=['trib'])
        P.dma('sp', cmpmask, cd['c_cmpmask'], writes=['cmpmask'])
        P.dma('sp', topkVC, cd['c_topk'], writes=['topkVC'])
        init_ms = []

        def ms(key, ap, val):
            init_ms.append(lambda: E('dve', 'memset', writes=[key], ap=ap, constant=val))
        ms('vsaug', vsaug, 1.0)
        ms('vwaug', vwaug, 1.0)
        for g in range(2):
            for (t_, nm) in ((ksaug[g], 'ks'), (kwaug[g], 'kw')):
                P.dma('sp', t_[96:100, :], cd['c_alk'], writes=[(nm, g, 'aug')])
            P.dma('sp', ksaug[g][64:96, :], cd['c_E'], writes=[('ks', g, 'E')])
            ms(('kw', g, 'E'), kwaug[g][64:96, :], 0.0)
            ms(('kcmp', g), kcmpaug[g][:, :], 0.0)
            ms(('vcmp', g), vcmpaug[g][:, :], 0.0)
        for hl in range(4):
            ms(('qn', hl, 'sel'), qn[hl][64:96, :], 0.0)
        ms('fv', fv, 1.0)
        ms('selb', selb, 0.0)
        E('dve', 'tensor_copy', reads=['prm'], writes=['w2b'], out=w2b, in_=prm[:, 112:240])
        E('dve', 'tensor_copy', reads=['prm'], writes=['posb'], out=posb[0:64, :], in_=prm[0:64, 48:112])

        pj = [0]

        def proj_bank():
            pj[0] += 1
            return 6 + (pj[0] % 2)

        def proj_fm_tg(slot, wkey, col0, evac, tg, ukeys=()):
            sv = slot.rearrange("p (k n) -> p k n", k=8)
            bi = proj_bank()
            for k in range(8):
                E('pe', 'matmul', reads=[wkey] + list(ukeys), writes=[('bank', bi)], inc=(k == 7),
                  out=banks[bi][:, :], lhsT=sv[:, k, col0:col0 + 128], rhs=uT[:, k, tg * 512:(tg + 1) * 512],
                  start=(k == 0), stop=(k == 7))
            evac(tg, bi)

        def proj_fm(slot, wkey, col0, evac):
            for tg in range(4):
                proj_fm_tg(slot, wkey, col0, evac, tg)

        def evac_plain(dst, key):
            def f(tg, bi):
                E('dve', 'tensor_copy', reads=[('bank', bi)], writes=[(key, tg)], out=dst[:, tg * 512:(tg + 1) * 512], in_=banks[bi][:, :])
            return f

        def evac_split(dsts, key, scale=None):
            def f(tg, bi):
                for a in range(2):
                    if scale is None:
                        E('dve', 'tensor_copy', reads=[('bank', bi)], writes=[(key, a, 'q', tg)],
                          out=dsts[a][0:64, tg * 512:(tg + 1) * 512], in_=banks[bi][a * 64:(a + 1) * 64, :])
                    else:
                        E('dve', 'tensor_scalar', reads=[('bank', bi)], writes=[(key, a, 'q', tg)],
                          out=dsts[a][0:64, tg * 512:(tg + 1) * 512], in0=banks[bi][a * 64:(a + 1) * 64, :],
                          scalar1=scale, scalar2=None, op0=ALU.mult)
            return f

        pos_kv, slot_kv, key_kv = W.acquire('KV')
        pos_tm, slot_tm, key_tm = W.acquire('TM')
        pos_qn, slot_qn, key_qn = W.acquire('QN')
        pos_w1k, slot_w1k, key_w1k = W.acquire('W1K')
        svtm = slot_tm.rearrange("p (k n) -> p k n", k=8)

        E('dve', 'memset', writes=['ss'], ap=ss, constant=0.0)

        def stageA(i):
            b = i % NXS
            if i >= NXS:
                P.dma('sp', xs[b], x_d[i * 128:(i + 1) * 128, :], writes=[('xs', b)])
            E('act', 'activation', reads=[('xs', b), 'ss'], writes=[('xn', b), ('ss', i)],
              out=xn[b], in_=xs[b], func=AF.Square, accum_out=ss[:, i:i + 1])
            E('dve', 'tensor_scalar', reads=[('ss', i)], writes=[('rstd', i)], out=rstd[:, i:i + 1], in0=ss[:, i:i + 1],
              scalar1=1.0 / D, scalar2=1e-6, op0=ALU.mult, op1=ALU.add)

        def stageB(i):
            b = i % NXS
            tb = 3 + (i % 3)
            E('act', 'activation', reads=[('rstd', i)], writes=[('rstd', i)], out=rstd[:, i:i + 1], in_=rstd[:, i:i + 1], func=AF.Ln)
            E('act', 'activation', reads=[('rstd', i)], writes=[('rstd', i)], out=rstd[:, i:i + 1], in_=rstd[:, i:i + 1], func=AF.Exp, scale=-0.5)
            E('act', 'activation', reads=[('xs', b), ('rstd', i)], writes=[('xn', b)],
              out=xn[b], in_=xs[b], func=AF.Copy, scale=rstd[:, i:i + 1])
            ptb = banks[tb][:, :].bitcast(BF16)
            for c in range(8):
                E('pe', 'transpose', reads=[('xn', b), 'identb'], writes=[('bank', tb)], inc=(c == 7),
                  out=ptb[:, c * 128:(c + 1) * 128], in_=xn[b][:, c * 128:(c + 1) * 128], identity=identb[:, :])
            E('dve', 'tensor_tensor', reads=[('bank', tb), 'prm'], writes=[('uT', i)],
              out=uT[:, :, i * 128:(i + 1) * 128], in0=ptb.rearrange("p (c t) -> p c t", c=8),
              in1=prm[:, 0:8].unsqueeze(2).to_broadcast([128, 8, 128]), op=ALU.mult)

        def projs(tg, j):
            uk = [('uT', q_) for q_ in range(4 * tg, 4 * tg + 4)]
            if j == 0:
                proj_fm_tg(slot_kv, key_kv, 0, evac_plain(kcT, 'kcT'), tg, uk)
            elif j == 1:
                proj_fm_tg(slot_kv, key_kv, 128, evac_plain(vcT, 'vcT'), tg, uk)
            elif j == 2:
                proj_fm_tg(slot_kv, key_kv, 256, evac_split(ksaug, 'ks'), tg, uk)
            else:
                proj_fm_tg(slot_kv, key_kv, 384, evac_split(kwaug, 'kw'), tg, uk)
            for it in range(4 * tg + j, 4 * tg + j + 1):
                bi = proj_bank()
                for k in range(8):
                    E('pe', 'matmul', reads=[key_tm, ('uT', it)], writes=[('bank', bi)], inc=(k == 7),
                      out=banks[bi][:, 0:288], lhsT=uT[:, k, it * 128:(it + 1) * 128], rhs=svtm[:, k, 0:288],
                      start=(k == 0), stop=(k == 7))
                E('dve', 'tensor_copy', reads=[('bank', bi), 'vsaug'], writes=[('vsaug', it)],
                  out=vsaug[:, it, :, 0:64], in_=banks[bi][:, 0:128].rearrange("p (g d) -> p g d", g=2))
                E('dve', 'tensor_copy', reads=[('bank', bi), 'vwaug'], writes=[('vwaug', it)],
                  out=vwaug[:, it, :, 0:64], in_=banks[bi][:, 128:256].rearrange("p (g d) -> p g d", g=2))
                E('dve', 'tensor_copy', reads=[('bank', bi)], writes=[('gates', it)], out=gates[:, it, :], in_=banks[bi][:, 256:280])
                E('dve', 'tensor_tensor', reads=[('bank', bi), 'prm'], writes=[('spx', it)],
                  out=spx.rearrange("p (h i) -> p h i", h=8)[:, :, it], in0=banks[bi][:, 280:288], in1=prm[:, 40:48], op=ALU.add)

        stageA(0)
        for i in range(16):
            if i + 1 < 16:
                stageA(i + 1)
            stageB(i)
            if init_ms:
                init_ms.pop(0)()
            if i >= 4:
                projs((i - 4) // 4, i % 4)
        while init_ms:
            init_ms.pop(0)()
        for j in range(4):
            projs(3, j)
        W.release(pos_kv)
        W.release(pos_tm)
        pos_w1v, slot_w1v, key_w1v = W.acquire('W1V')
        allg = [('gates', i) for i in range(16)]
        gflat = gates.rearrange("p a b -> p (a b)")
        E('act', 'activation', reads=allg, writes=['gates'], out=gflat, in_=gflat, func=AF.Exp, scale=-1.0)
        E('dve', 'tensor_scalar', reads=['gates'], writes=['gates'], out=gflat, in0=gflat, scalar1=1.0, scalar2=None, op0=ALU.add)
        E('dve', 'reciprocal', reads=['gates'], writes=['gates'], out=gflat, in_=gflat)
        allspx = [('spx', i) for i in range(16)]
        E('act', 'activation', reads=allspx, writes=['spx'], out=spx, in_=spx, func=AF.Exp, scale=-1.0)
        E('act', 'activation', reads=['spx'], writes=['spx'], out=spx, in_=spx, func=AF.Ln, bias=1.0)

        w1kv = slot_w1k.rearrange("p (i m) -> p i m", i=32)
        w1vv = slot_w1v.rearrange("p (i m) -> p i m", i=32)
        allkc = [('kcT', tg) for tg in range(4)]
        allvc = [('vcT', tg) for tg in range(4)]
        E('dve', 'tensor_copy', reads=allkc, writes=['kcr'], out=kcr, in_=kcT.rearrange("p (j r) -> p r j", r=16))
        E('dve', 'tensor_copy', reads=allvc, writes=['vcr'], out=vcr, in_=vcT.rearrange("p (j r) -> p r j", r=16))
        bi = proj_bank()
        for j, (wv, wkey) in enumerate(((w1kv, key_w1k), (w1vv, key_w1v))):
            for i in range(32):
                E('pe', 'matmul', reads=[wkey, 'posb'], writes=[('bank', bi)], inc=(i == 31),
                  out=banks[bi][:, j:j + 1], lhsT=wv[0:64, i, :], rhs=posb[0:64, j * 32 + i:j * 32 + i + 1],
                  start=(i == 0), stop=(i == 31))
        E('dve', 'tensor_copy', reads=[('bank', bi)], writes=['cbias'], out=cbias, in_=banks[bi][:, 0:2])
        chains = [(g, j) for g in range(2) for j in range(2)]
        for ci, (g, j) in enumerate(chains):
            wv, wkey, srcr, skey = ((w1kv, key_w1k, kcr, 'kcr'), (w1vv, key_w1v, vcr, 'vcr'))[j]
            for i in range(32):
                E('pe', 'matmul', reads=[wkey, skey], writes=[('bank', ci)], inc=(i == 31),
                  out=banks[ci][:, 0:127], lhsT=wv[g * 64:(g + 1) * 64, i, :],
                  rhs=srcr[g * 64:(g + 1) * 64, i % 16, (i // 16):(i // 16) + 127], start=(i == 0), stop=(i == 31))
        for hl in range(4):
            P.dma('sp', qn[hl][96:100, :], cd['c_alq'][hl], writes=[('qn', hl, 'alibi')])
        proj_fm(slot_qn, key_qn, 0, evac_split([qn[0], qn[1]], ('qnq', 0), scale=0.125))
        for ci, (g, j) in enumerate(chains):
            cs_ = ci % 2
            gx = gxs[cs_]
            ghb = ghb4[ci]
            gk = ['gx%d_%d' % (q_, cs_) for q_ in range(4)] + ['ghb%d' % ci]
            bi = ci
            E('act', 'activation', reads=[('bank', bi), 'cbias'], writes=[gk[0]], out=gx[0][:, 0:127], in_=banks[bi][:, 0:127],
              func=AF.Identity, bias=cbias[:, j:j + 1])
            E('dve', 'tensor_tensor', reads=[gk[0]], writes=[gk[1]], out=gx[1][:, 0:127], in0=gx[0][:, 0:127], in1=gx[0][:, 0:127], op=ALU.mult)
            E('dve', 'tensor_scalar', reads=[gk[1]], writes=[gk[1]], out=gx[1][:, 0:127], in0=gx[1][:, 0:127],
              scalar1=0.044715, scalar2=1.0, op0=ALU.mult, op1=ALU.add)
            E('dve', 'tensor_tensor', reads=[gk[1], gk[0]], writes=[gk[2]], out=gx[2][:, 0:127], in0=gx[1][:, 0:127], in1=gx[0][:, 0:127], op=ALU.mult)
            E('act', 'activation', reads=[gk[2]], writes=[gk[3]], out=gx[3][:, 0:127], in_=gx[2][:, 0:127], func=AF.Exp, scale=-1.5957691216057308)
            E('dve', 'tensor_scalar', reads=[gk[3]], writes=[gk[3]], out=gx[3][:, 0:127], in0=gx[3][:, 0:127], scalar1=1.0, scalar2=None, op0=ALU.add)
            E('dve', 'reciprocal', reads=[gk[3]], writes=[gk[3]], out=gx[3][:, 0:127], in_=gx[3][:, 0:127])
            E('dve', 'tensor_tensor', reads=[gk[3], gk[0]], writes=[gk[4]], out=ghb[:, 0:127], in0=gx[3][:, 0:127], in1=gx[0][:, 0:127], op=ALU.mult)
        proj_fm(slot_qn, key_qn, 128, evac_split([qn[2], qn[3]], ('qnq', 1), scale=0.125))
        for ci, (g, j) in enumerate(chains):
            ghb = ghb4[ci]
            bo = proj_bank()
            if j == 0:
                E('pe', 'matmul', reads=['ghb%d' % ci, 'w2b'], writes=[('bank', bo)], out=banks[bo][0:64, 0:127], lhsT=w2b[:, 0:64],
                  rhs=ghb[:, 0:127], start=True, stop=True)
                E('dve', 'tensor_copy', reads=[('bank', bo), ('kcmp', g)], writes=[('kcmp', g)], out=kcmpaug[g][0:64, 0:127], in_=banks[bo][0:64, 0:127])
            else:
                E('pe', 'matmul', reads=['ghb%d' % ci, 'w2b'], writes=[('bank', bo)], out=banks[bo][0:127, 0:64], lhsT=ghb[:, 0:127],
                  rhs=w2b[:, 64:128], start=True, stop=True)
                E('dve', 'tensor_copy', reads=[('bank', bo), ('vcmp', g)], writes=[('vcmp', g)], out=vcmpaug[g][0:127, 0:64], in_=banks[bo][0:127, 0:64])
        for g in range(2):
            P.dma('sp', kcmpaug[g][96:100, :], cd['c_alkc'], reads=[('kcmp', g)], writes=[('kcmp', g)])
            E('dve', 'memset', reads=[('vcmp', g)], writes=[('vcmp', g)], ap=vcmpaug[g][0:127, 64:65], constant=1.0)
            P.dma('sp', vcmpaug[g][:, 65:97], cd['c_overlap'], reads=[('vcmp', g)], writes=[('vcmp', g)])
        bi = proj_bank()
        E('pe', 'transpose', reads=['spx', 'identf'], writes=[('bank', bi)], out=banks[bi][:, 0:128], in_=spx, identity=identf[:, 0:128])
        E('dve', 'tensor_copy', reads=[('bank', bi)], writes=['tsum'], out=tsum, in_=banks[bi][:, 0:128])
        E('dve', 'tensor_tensor_scan', reads=['tsum', 'identf'], writes=['cref'], out=cref, data0=identf[:, 128:256], data1=tsum,
          initial=0.0, op0=ALU.mult, op1=ALU.add)
        bi = proj_bank()
        E('pe', 'matmul', reads=['cref', 'identf'], writes=[('bank', bi)], out=banks[bi][:, 0:1], lhsT=identf[:, 384:512], rhs=cref[:, 127:128],
          start=True, stop=True)
        E('dve', 'tensor_copy', reads=[('bank', bi)], writes=['cbase'], out=cbias2[:, 0:1], in_=banks[bi][:, 0:1])
        E('dve', 'tensor_scalar', reads=['cref', 'cbase'], writes=['cneg'], out=cneg, in0=cref, scalar1=cbias2[:, 0:1], scalar2=None, op0=ALU.add)
        E('dve', 'tensor_copy', reads=['cneg'], writes=['a0'], out=a123[0], in_=cneg)
        E('dve', 'tensor_tensor', reads=['cneg', 'a0'], writes=['r1'], out=r1, in0=cneg, in1=a123[0], op=ALU.subtract)
        E('dve', 'tensor_copy', reads=['r1'], writes=['a1'], out=a123[1], in_=r1)
        E('dve', 'tensor_tensor', reads=['r1', 'a1'], writes=['r1'], out=r1, in0=r1, in1=a123[1], op=ALU.subtract)
        E('dve', 'tensor_copy', reads=['r1'], writes=['a2'], out=a123[2], in_=r1)
        W.release(pos_w1k)
        W.release(pos_w1v)
        for a_ in range(2):
            E('dve', 'memset', reads=['kcr', 'vcr'], writes=[(('fq', 0), a_, 'aug')], ap=fq[a_][64:70, :], constant=1.0)
            E('dve', 'memset', writes=[(('fk', 0), a_, 'aug')], ap=fk[a_][64:70, :], constant=-1.0)
        if debug:
            E('dve', 'memset', writes=['otm_init'], ap=otm, constant=0.0)
            E('dve', 'memset', writes=['imp_init'], ap=impacc, constant=0.0)
        dump('uT', uT_t[:, :])
        dump('kcmpaug0', kcmpaug[0])
        dump('vcmpaug0', vcmpaug[0])
        if stop_after == '1B':
            P.barrier()
            P.replay()
            return nc

        tiles = []
        ctr = {'S': 0, 'PT': 0, 'O': 0}
        NOB = 2
        Obv = [banks[4 + b][:, :].rearrange("p (s c) -> p s c", s=4) for b in range(NOB)]

        def add_unit(ktiles, kaug, kkeys_fn, qaug, qkeys, K, tg, v_fn, vkeys_fn, nv, first_fn, last_fn, post, pre=None, kparts=128):
            ob = ctr['O'] % NOB
            ctr['O'] += 1
            n = len(ktiles)
            for idx, (i, col0, ncols, mask) in enumerate(ktiles):
                d = {'pre': [], 'post': [], 'first': (idx == 0)}
                if idx == 0 and pre is not None:
                    d['pre'].append(pre)

                def S(i=i, col0=col0, ncols=ncols, mask=mask, d=d):
                    sb_ = ctr['S'] % 4
                    ctr['S'] += 1
                    d['sb'] = sb_
                    E('pe', 'matmul', reads=list(kkeys_fn(i)) + list(qkeys), writes=[('bank', sb_)], inc=(mask is None),
                      out=banks[sb_][0:kparts, 0:ncols], lhsT=kaug(i), rhs=qaug[0:K, tg * 512 + col0: tg * 512 + col0 + ncols],
                      start=True, stop=(mask is None))
                    if mask is not None:
                        if mask[0] == 'cmp':
                            E('pe', 'matmul', reads=['identb', 'cmpmask'], writes=[('bank', sb_)],
                              out=banks[sb_][:, 0:512], lhsT=identb[:, :], rhs=cmpmask[:, tg * 512:(tg + 1) * 512], start=False, stop=True)
                        else:
                            lc = mask[1]
                            mo = 0 if mask[0] == 'ge' else 128
                            E('pe', 'matmul', reads=['identb', 'trib'], writes=[('bank', sb_)],
                              out=banks[sb_][:, lc:lc + 128], lhsT=identb[:, :], rhs=trib[:, mo:mo + 128], start=False, stop=True)

                def EXP(ncols=ncols, d=d, mask=mask):
                    pt_ = ctr['PT'] % NPT
                    ctr['PT'] += 1
                    d['pt'] = pt_
                    E('act', 'activation', reads=[('bank', d['sb'])], writes=[('PT', pt_)],
                      out=PT[pt_][0:kparts, 0:ncols], in_=banks[d['sb']][0:kparts, 0:ncols], func=AF.Exp)

                def PV(i=i, col0=col0, ncols=ncols, d=d):
                    s0 = col0 // 128
                    s1 = (col0 + ncols) // 128
                    for s in range(s0, s1):
                        E('pe', 'matmul', reads=[('PT', d['pt'])] + list(vkeys_fn(i)), writes=[('O', ob)], inc=(s == s1 - 1),
                          out=Obv[ob][:, s, 0:nv], lhsT=PT[d['pt']][0:kparts, (s - s0) * 128:(s - s0 + 1) * 128], rhs=v_fn(i),
                          start=(d['first'] and s == s0), stop=(i == last_fn(s)), skip_group_check=True)
                d['S'] = S
                d['EXP'] = EXP
                d['PV'] = PV
                if idx == n - 1:
                    d['post'].append(lambda ob=ob: post(ob))
                tiles.append(d)

        def run_pipeline(depth=3):
            n = len(tiles)
            for t_ in range(min(depth, n)):
                for f in tiles[t_]['pre']:
                    f()
                tiles[t_]['S']()
            for t_ in range(n):
                tiles[t_]['EXP']()
                if t_ + depth < n:
                    for f in tiles[t_ + depth]['pre']:
                        f()
                    tiles[t_ + depth]['S']()
                tiles[t_]['PV']()
                for f in tiles[t_]['post']:
                    f()
            del tiles[:]

        def epi_rec(ob):
            E('dve', 'tensor_scalar', reads=[('O', ob)], writes=['rec'], out=rec, in0=Obv[ob][:, :, 64],
              scalar1=1e-30, scalar2=None, op0=ALU.max)
            E('dve', 'reciprocal', reads=['rec'], writes=['rec'], out=rec, in_=rec)

        def nsa_post(br, h, g, hl, tg, first_branch):
            def f(ob):
                epi_rec(ob)
                E('dve', 'tensor_tensor', reads=['rec', 'gates'] + [('gates', i) for i in range(4 * tg, 4 * tg + 4)], writes=['fsc'],
                  out=fsc, in0=rec, in1=gates[:, 4 * tg:4 * tg + 4, br * 8 + h], op=ALU.mult)
                dst = otm[:, 4 * tg:4 * tg + 4, h * 64:(h + 1) * 64]
                okey = ('otm', h, tg)
                if first_branch:
                    E('dve', 'tensor_tensor', reads=[('O', ob), 'fsc'], writes=[okey], out=dst, in0=Obv[ob][:, :, 0:64],
                      in1=fsc.unsqueeze(2).to_broadcast([128, 4, 64]), op=ALU.mult)
                else:
                    E('dve', 'tensor_tensor', reads=[('O', ob), 'fsc'], writes=['tmpo'], out=tmpo, in0=Obv[ob][:, :, 0:64],
                      in1=fsc.unsqueeze(2).to_broadcast([128, 4, 64]), op=ALU.mult)
                    E('dve', 'tensor_tensor', reads=['tmpo', okey], writes=[okey], out=dst, in0=dst, in1=tmpo, op=ALU.add)
                if br == 0:
                    idst = impacc[:, 4 * tg:4 * tg + 4, g, :]
                    ikey = ('imp', g, tg)
                    if hl == 0:
                        E('dve', 'tensor_tensor', reads=[('O', ob), 'rec'], writes=[ikey], out=idst, in0=Obv[ob][:, :, 65:97],
                          in1=rec.unsqueeze(2).to_broadcast([128, 4, 32]), op=ALU.mult)
                    else:
                        E('dve', 'tensor_tensor', reads=[('O', ob), 'rec'], writes=['tmpi'], out=tmpi, in0=Obv[ob][:, :, 65:97],
                          in1=rec.unsqueeze(2).to_broadcast([128, 4, 32]), op=ALU.mult)
                        E('dve', 'tensor_tensor', reads=['tmpi', ikey], writes=[ikey], out=idst, in0=idst, in1=tmpi, op=ALU.add)
            return f

        def fox_post(h, tg):
            def f(ob):
                epi_rec(ob)
                E('dve', 'tensor_tensor', reads=[('O', ob), 'rec'], writes=[('otm', 8 + h, tg)],
                  out=otm[:, 4 * tg:4 * tg + 4, 512 + h * 64:512 + (h + 1) * 64], in0=Obv[ob][:, :, 0:64],
                  in1=rec.unsqueeze(2).to_broadcast([128, 4, 64]), op=ALU.mult)
            return f

        def causal_ktiles(tg):
            kt = []
            for i in range(4 * tg + 4):
                r = i - 4 * tg
                if r < 0:
                    kt.append((i, 0, 512, None))
                else:
                    kt.append((i, 128 * r, 512 - 128 * r, ('ge', 0)))
            return kt

        def window_ktiles(tg):
            kt = []
            for i in range(max(4 * tg - 4, 0), 4 * tg + 4):
                r = i - 4 * tg
                if r >= 0:
                    kt.append((i, 128 * r, 512 - 128 * r, ('ge', 0)))
                else:
                    nc_ = 128 * (r + 5)
                    kt.append((i, 0, nc_, ('lt', nc_ - 128)))
            return kt

        topkV = topkVC[:, 0:512].rearrange("p (a b) -> p a b", a=16)
        topkC = topkVC[:, 512:1024].rearrange("p (a b) -> p a b", a=16)
        selbv = selb[:, 64:64 + 1024].rearrange("p (g a b) -> p g a b", g=2, a=16)

        def topk_prep(g):
            allimp = [('imp', g, tg) for tg in range(4)]
            E('dve', 'tensor_tensor', reads=allimp + ['topkVC'], writes=[('score', g)], out=score, in0=impacc[:, :, g, :], in1=topkV, op=ALU.mult)
            E('dve', 'tensor_tensor', reads=[('score', g), 'topkVC'], writes=[('score', g)], out=score, in0=score, in1=topkC, op=ALU.add)

        def topk_tile(g, i):
            E('dve', 'max', reads=[('score', g)], writes=['mx8'], out=mx8, in_=score[:, i, :])
            E('dve', 'match_replace', reads=[('score', g), 'mx8'], writes=['sc2'], out=sc2, in_to_replace=mx8, in_values=score[:, i, :], imm_value=-1e30)
            E('dve', 'max', reads=['sc2'], writes=['mx8'], out=mx8, in_=sc2)
            E('dve', 'tensor_scalar', reads=[('score', g), 'mx8'], writes=[('selb', g, i)], out=selbv[:, g, i, :], in0=score[:, i, :],
              scalar1=mx8[:, 7:8], scalar2=NEG, op0=ALU.is_lt, op1=ALU.mult)

        def sel_rows_group(g):
            for tg in range(4):
                bi = proj_bank()
                ptb = banks[bi][:, :].bitcast(BF16)
                for s in range(4):
                    i = 4 * tg + s
                    off = (g * 16 + i) * 32
                    E('pe', 'transpose', reads=[('selb', g, i), 'identb'], writes=[('bank', bi)], inc=(s == 3),
                      out=ptb[0:96, s * 128:(s + 1) * 128], in_=selb[:, off:off + 96], identity=identb[:, :])
                for hl in range(4):
                    E('dve', 'tensor_copy', reads=[('bank', bi)], writes=[('qn', hl, 'sel', tg)],
                      out=qn[hl][64:96, tg * 512:(tg + 1) * 512], in_=ptb[64:96, 0:512])

        FQ = [fq, [qn[0], qn[1]]]
        FK = [fk, [qn[2], qn[3]]]
        FV = [fv, vsaug]

        def fox_setup_pieces(p):
            st = p % 2
            holder = {}
            pieces = []

            def acq():
                holder['w'] = W.acquire('FOX%d' % p)
            for tg in range(4):
                def pc(tg=tg):
                    if 'w' not in holder:
                        acq()
                    pos_f, slot_f, key_f = holder['w']
                    proj_fm_tg(slot_f, key_f, 0, evac_split(FQ[st], ('fq', st), scale=0.125), tg)
                    proj_fm_tg(slot_f, key_f, 128, evac_split(FK[st], ('fk', st)), tg)
                pieces.append(pc)
            for tg in range(4):
                def pv(tg=tg):
                    pos_f, slot_f, key_f = holder['w']
                    svf = slot_f.rearrange("p (k n) -> p k n", k=8)
                    for i in range(4 * tg, 4 * tg + 4):
                        bi = proj_bank()
                        for k in range(8):
                            E('pe', 'matmul', reads=[key_f], writes=[('bank', bi)], inc=(k == 7),
                              out=banks[bi][:, 0:128], lhsT=uT[:, k, i * 128:(i + 1) * 128], rhs=svf[:, k, 256:384],
                              start=(k == 0), stop=(k == 7))
                        E('dve', 'tensor_copy', reads=[('bank', bi)], writes=[(('fv', st), i)],
                          out=FV[st][:, i, :, 0:64], in_=banks[bi][:, 0:128].rearrange("p (g d) -> p g d", g=2))
                    if tg == 3:
                        W.release(pos_f)
                pieces.append(pv)
            for a in range(2):
                def pa(a=a):
                    h = 2 * p + a
                    for j in range(3):
                        P.dma('sp', FQ[st][a][64 + j:65 + j, :], a123[j][16 * h:16 * h + 16, :], reads=['a%d' % j], writes=[(('fq', st), a, 'aug')])
                        P.dma('sp', FK[st][a][67 + j:68 + j, :], a123[j][16 * h:16 * h + 16, :], reads=['a%d' % j], writes=[(('fk', st), a, 'aug')])
                pieces.append(pa)
            return pieces


        for g in range(2):
            def pre_group(g=g):
                if g == 0:
                    return
                for hl in range(4):
                    P.dma('sp', qn[hl][96:100, :], cd['c_alq'][g * 4 + hl], writes=[('qn', hl, 'alibi')])
                for c in range(2):
                    proj_fm(slot_qn, key_qn, (2 * g + c) * 128, evac_split([qn[2 * c], qn[2 * c + 1]], ('qnq', c), scale=0.125))
                if g == 1:
                    W.release(pos_qn)

            def qkeys_of(hl, tg, sel):
                ks_ = [(('qnq', hl // 2), hl % 2, 'q', tg), ('qn', hl, 'alibi'), ('qn', hl, 'sel')]
                ks_.append(('qn', hl, 'sel', tg))
                return ks_
            def win_units(hls, first_branch, with_pre):
                ends = []
                for hl in hls:
                    h = g * 4 + hl
                    for tg in range(4):
                        add_unit(window_ktiles(tg), (lambda i, g=g: kwaug[g][0:100, i * 128:(i + 1) * 128]),
                                 (lambda i, g=g: [('kw', g, 'aug'), ('kw', g, 'E')] + [('kw', g, 'q', i // 4)]),
                                 qn[hl], qkeys_of(hl, tg, False), 100, tg, (lambda i, g=g: vwaug[:, i, g, :]), (lambda i: [('vwaug', i), 'vwaug']), 65,
                                 (lambda s, tg=tg: max(4 * tg + s - 4, 0)), (lambda s, tg=tg: 4 * tg + s), nsa_post(2, h, g, hl, tg, first_branch),
                                 pre=(pre_group if (with_pre and hl == hls[0] and tg == 0) else None))
                        ends.append(len(tiles) - 1)
                return ends
            win_units((0, 1), True, True)
            for hl in range(4):
                h = g * 4 + hl
                for tg in range(4):
                    add_unit([(0, 0, 512, ('cmp',))], (lambda i, g=g: kcmpaug[g][0:100, 0:128]), (lambda i, g=g: [('kcmp', g)]),
                             qn[hl], qkeys_of(hl, tg, False), 100, tg, (lambda i, g=g: vcmpaug[g][:, 0:97]), (lambda i, g=g: [('vcmp', g)]), 97,
                             (lambda s: 0), (lambda s: 0), nsa_post(0, h, g, hl, tg, hl >= 2))
            tiles[-1]['post'].append(lambda g=g: (topk_prep(g), topk_tile(g, 0)))
            win_unit_ends = win_units((2, 3), False, False)
            for ui in range(15):
                tiles[win_unit_ends[ui // 2]]['post'].append(lambda g=g, ui=ui: topk_tile(g, ui + 1))
            for hl in range(4):
                h = g * 4 + hl
                for tg in range(4):
                    add_unit(causal_ktiles(tg), (lambda i, g=g: ksaug[g][0:100, i * 128:(i + 1) * 128]),
                             (lambda i, g=g: [('ks', g, 'aug'), ('ks', g, 'E')] + [('ks', g, 'q', i // 4)]),
                             qn[hl], qkeys_of(hl, tg, True), 100, tg, (lambda i, g=g: vsaug[:, i, g, :]), (lambda i: [('vsaug', i), 'vsaug']), 65,
                             (lambda s: 0), (lambda s, tg=tg: 4 * tg + s), nsa_post(1, h, g, hl, tg, False),
                             pre=((lambda g=g: sel_rows_group(g)) if (hl == 0 and tg == 0) else None))
            if g == 1:
                pcs0 = fox_setup_pieces(0)
                base0 = len(tiles) - 3 * len(pcs0) - 4
                for j, pc_ in enumerate(pcs0):
                    tiles[base0 + 3 * j]['pre'].append(pc_)
            run_pipeline()
            if g == 0:
                dump('qn0', qn[0][0:100, :], reads=[(('qnq', 0), 0, 'q', tg) for tg in range(4)] + [('qn', 0, 'alibi'), ('qn', 0, 'sel')] + [('qn', 0, 'sel', tg) for tg in range(4)])
            if stop_after == 'NSA0':
                break
        dump('impacc', impacc.rearrange("p a b c -> p (a b c)"), reads=[('imp', g, tg) for g in range(2) for tg in range(4)])
        dump('selb', selb, reads=[('selb', g, i) for g in range(2) for i in range(16)])
        if stop_after in ('NSA0', 'NSA'):
            P.barrier()
            dump('otm', otm.rearrange("p a b -> p (a b)"))
            P.barrier()
            P.replay()
            return nc

        P.barrier()
        for a_ in range(2):
            E('dve', 'memset', writes=[(('fq', 1), a_, 'aug')], ap=FQ[1][a_][64:70, :], constant=1.0)
            E('dve', 'memset', writes=[(('fk', 1), a_, 'aug')], ap=FK[1][a_][64:70, :], constant=-1.0)
        pair_first_tile = []
        for p in range(4):
            st = p % 2
            pair_first_tile.append(len(tiles))
            for a in range(2):
                h = 2 * p + a
                for tg in range(4):
                    add_unit(causal_ktiles(tg), (lambda i, a=a, st=st: FK[st][a][0:70, i * 128:(i + 1) * 128]),
                             (lambda i, a=a, st=st: [(('fk', st), a, 'q', i // 4), (('fk', st), a, 'aug')]),
                             FQ[st][a], [(('fq', st), a, 'q', tg), (('fq', st), a, 'aug')], 70, tg,
                             (lambda i, a=a, st=st: FV[st][:, i, a, :]), (lambda i, st=st: [(('fv', st), i)]), 65,
                             (lambda s: 0), (lambda s, tg=tg: 4 * tg + s), fox_post(h, tg))
        for p in range(4):
            if p == 0:
                continue
            pcs = fox_setup_pieces(p)
            if True:
                base = pair_first_tile[p - 1] + 3
                for j, pc_ in enumerate(pcs):
                    tiles[base + 3 * j]['pre'].append(pc_)
        run_pipeline()
        dump('fq0', fq[0][0:70, :])
        dump('fk0', fk[0][0:70, :])
        P.barrier()
        dump('otm', otm.rearrange("p a b -> p (a b)"))
        if stop_after in ('FOX0', 'FOX'):
            P.barrier()
            P.replay()
            return nc

        cv2 = Carver()
        oT = cv2.alloc([8, T], BF16)
        hT = cv2.alloc([8, T], F32)
        yT = cv2.alloc([8, 1024], BF16)
        xs2 = [cv2.alloc([1024], F32) for _ in range(2)]
        sA = cv2.alloc([512], F32)
        sB = cv2.alloc([512], F32)
        t1 = cv2.alloc([512], F32)
        t2 = cv2.alloc([512], F32)
        sq = [cv2.alloc([512], BF16) for _ in range(4)]
        onesb = cv2.alloc([128], BF16)
        rstdt = cv2.alloc([512], F32)
        for i in range(16):
            b = 5 + (i % 2)
            ptb = banks[b][:, :].bitcast(BF16)
            for c in range(8):
                E('pe', 'transpose', writes=[('bank', b)], inc=(c == 7),
                  out=ptb[:, c * 128:(c + 1) * 128], in_=otm[:, i, c * 128:(c + 1) * 128], identity=identb[:, :])
            if i % 2 == 0:
                E('dve', 'tensor_copy', reads=[('bank', b)], writes=[('oT', i)], out=oT[:, :, i * 128:(i + 1) * 128],
                  in_=ptb.rearrange("p (c t) -> p c t", c=8))
            else:
                E('act', 'copy', reads=[('bank', b)], writes=[('oT', i)], out=oT[:, :, i * 128:(i + 1) * 128],
                  in_=ptb.rearrange("p (c t) -> p c t", c=8))
        P.barrier()
        dump('oT', oT.rearrange("p a b -> p (a b)"))
        E('dve', 'memset', writes=['onesb'], ap=onesb, constant=1.0)

        def xT_tile(i):
            b = i % 2
            P.dma('sp', xs2[b], x_d[i * 128:(i + 1) * 128, :], writes=[('xs2', b)])
            for half in range(2):
                bi = nbank()
                for c in range(4):
                    cc = half * 4 + c
                    E('pe', 'transpose', reads=[('xs2', b), 'identf'], writes=[('bank', bi)], inc=(c == 3),
                      out=banks[bi][:, c * 128:(c + 1) * 128], in_=xs2[b][:, cc * 128:(cc + 1) * 128], identity=identf[:, 0:128])
                if half == 0:
                    E('act', 'copy', reads=[('bank', bi)], writes=[('hT', i, half)], out=hT[:, half * 4:half * 4 + 4, i * 128:(i + 1) * 128],
                      in_=banks[bi][:, :].rearrange("p (c t) -> p c t", c=4))
                else:
                    E('dve', 'tensor_copy', reads=[('bank', bi)], writes=[('hT', i, half)], out=hT[:, half * 4:half * 4 + 4, i * 128:(i + 1) * 128],
                      in_=banks[bi][:, :].rearrange("p (c t) -> p c t", c=4))

        bk = [0]

        def nbank():
            bk[0] = (bk[0] + 1) % 8
            return bk[0]

        def mm_acc(bi, slotv, wkey, kchunks, col0, rhs_fn, rkeys):
            n = len(kchunks)
            for j, k in enumerate(kchunks):
                E('pe', 'matmul', reads=[wkey] + list(rkeys), writes=[('bank', bi)], inc=(j == n - 1),
                  out=banks[bi][:, :], lhsT=slotv[:, k, col0:col0 + 128], rhs=rhs_fn(k), start=(j == 0), stop=(j == n - 1))

        def rms_stats(tg):
            bi = nbank()
            for k in range(8):
                E('act', 'activation', reads=[('h', k, tg)], writes=[('sq', k % 4)], out=sq[k % 4], in_=hT[:, k, tg * 512:(tg + 1) * 512], func=AF.Square)
                E('pe', 'matmul', reads=[('sq', k % 4), 'onesb'], writes=[('bank', bi)], inc=True,
                  out=banks[bi][:, :], lhsT=onesb, rhs=sq[k % 4], start=(k == 0), stop=(k == 7))
            E('dve', 'tensor_scalar', reads=[('bank', bi)], writes=['rstdt'], out=rstdt, in0=banks[bi][:, :], scalar1=1.0 / D, scalar2=1e-6,
              op0=ALU.mult, op1=ALU.add)
            E('act', 'activation', reads=['rstdt'], writes=['rstdt'], out=rstdt, in_=rstdt, func=AF.Ln)
            E('act', 'activation', reads=['rstdt'], writes=['rstdt'], out=rstdt, in_=rstdt, func=AF.Exp, scale=-0.5)

        vT = uT
        for half in range(2):
            wpos = {}
            for m in range(8):
                for nm in ('MG%d' % (m // 2), 'BR%d' % (m // 4)):
                    if nm not in wpos:
                        wpos[nm] = W.acquire(nm)
                pmg, smg, kmg = wpos['MG%d' % (m // 2)]
                pbr, sbr, kbr = wpos['BR%d' % (m // 4)]
                smgv = smg.rearrange("p (k n) -> p k n", k=8)
                sbrv = sbr.rearrange("p (k n) -> p k n", k=8)
                for tl in range(2):
                    tg = half * 2 + tl
                    tsl = slice(tg * 512, (tg + 1) * 512)
                    bA, bB, bC, bD = nbank(), nbank(), nbank(), nbank()
                    mm_acc(bA, smgv, kmg, range(8), (m % 2) * 128, (lambda k, tsl=tsl: uT[:, k, tsl]), [])
                    mm_acc(bB, smgv, kmg, range(8), 256 + (m % 2) * 128, (lambda k, tsl=tsl: uT[:, k, tsl]), [])
                    mm_acc(bC, sbrv, kbr, range(0, 4), (m % 4) * 128, (lambda k, tsl=tsl: oT[:, k, tsl]), [])
                    mm_acc(bD, sbrv, kbr, range(4, 8), (m % 4) * 128, (lambda k, tsl=tsl: oT[:, k, tsl]), [])
                    E('act', 'activation', reads=[('bank', bA)], writes=['sA'], out=sA, in_=banks[bA][:, :], func=AF.Sigmoid, bias=prm[:, 24 + m:25 + m])
                    E('act', 'activation', reads=[('bank', bB)], writes=['sB'], out=sB, in_=banks[bB][:, :], func=AF.Sigmoid, bias=prm[:, 32 + m:33 + m])
                    E('dve', 'tensor_tensor', reads=['sA', ('bank', bC)], writes=['t1'], out=t1, in0=sA, in1=banks[bC][:, :], op=ALU.mult)
                    E('dve', 'tensor_tensor', reads=['sB', ('bank', bD)], writes=['t2'], out=t2, in0=sB, in1=banks[bD][:, :], op=ALU.mult)
                    E('dve', 'tensor_tensor', reads=['t1', 't2'], writes=[('yT', m, tl)], out=yT[:, m, tl * 512:(tl + 1) * 512], in0=t1, in1=t2, op=ALU.add)
                if m % 2 == 1:
                    W.release(pmg)
                if m % 4 == 3:
                    W.release(pbr)
                if half == 0:
                    xT_tile(2 * m)
                    xT_tile(2 * m + 1)
            if half == 0:
                dump('yT', yT.rearrange("p a b -> p (a b)"), reads=[('yT', m, tl) for m in range(8) for tl in range(2)])
            for nm in ('OUT0', 'OUT1'):
                wpos[nm] = W.acquire(nm)
            for tl in range(2):
                tg = half * 2 + tl
                for m in range(8):
                    po, so, ko = wpos['OUT%d' % (m // 4)]
                    sov = so.rearrange("p (k n) -> p k n", k=8)
                    bi = nbank()
                    mm_acc(bi, sov, ko, range(8), (m % 4) * 128, (lambda k, tl=tl: yT[:, k, tl * 512:(tl + 1) * 512]),
                           [('yT', k, tl) for k in range(8)])
                    E('dve', 'tensor_tensor', reads=[('bank', bi)] + [('hT', it, m // 4) for it in range(4 * tg, 4 * tg + 4)], writes=[('h', m, tg)],
                      out=hT[:, m, tg * 512:(tg + 1) * 512], in0=hT[:, m, tg * 512:(tg + 1) * 512], in1=banks[bi][:, :], op=ALU.add)
                if tl == 1:
                    W.release(wpos['OUT0'][0])
                    W.release(wpos['OUT1'][0])
                rms_stats(tg)
                for k in range(8):
                    E('dve', 'scalar_tensor_tensor', reads=[('h', k, tg), 'rstdt', 'prm'], writes=[('vT', k, tg)], out=vT[:, k, tg * 512:(tg + 1) * 512],
                      in0=hT[:, k, tg * 512:(tg + 1) * 512], scalar=prm[:, 8 + k:9 + k], in1=rstdt, op0=ALU.mult, op1=ALU.mult)

        aT = [oT[:, 0:4, :], oT[:, 4:8, :]]

        def mlp_up(fb):
            pu, su, ku = W.acquire('UP%d' % fb)
            suv = su.rearrange("p (k n) -> p k n", k=8)
            for tg in range(4):
                for c in range(4):
                    bi = nbank()
                    mm_acc(bi, suv, ku, range(8), c * 128, (lambda k, tg=tg: vT[:, k, tg * 512:(tg + 1) * 512]), [('vT', k, tg) for k in range(8)])
                    rl = sA if (c * 4 + tg) % 2 == 0 else sB
                    rk = 'sA' if (c * 4 + tg) % 2 == 0 else 'sB'
                    E('act', 'activation', reads=[('bank', bi)], writes=[rk], out=rl, in_=banks[bi][:, :], func=AF.Relu)
                    E('dve', 'tensor_tensor', reads=[rk], writes=[('aT', fb % 2, c, tg)], out=aT[fb % 2][:, c, tg * 512:(tg + 1) * 512],
                      in0=rl, in1=rl, op=ALU.mult)
            W.release(pu)

        def mlp_down(fb, tgs=(0, 1, 2, 3), release=True):
            if ('down', fb) not in wheld:
                wheld[('down', fb)] = W.acquire('DOWN%d' % fb)
            pd, sd, kd = wheld[('down', fb)]
            sdv = sd.rearrange("p (k n) -> p k n", k=4)
            for tg in tgs:
                for m in range(8):
                    bi = nbank()
                    for k in range(4):
                        E('pe', 'matmul', reads=[kd, ('aT', fb % 2, k, tg)], writes=[('bank', bi)], inc=(k == 3),
                          out=banks[bi][:, :], lhsT=sdv[:, k, m * 128:(m + 1) * 128], rhs=aT[fb % 2][:, k, tg * 512:(tg + 1) * 512],
                          start=(k == 0), stop=(k == 3))
                    E('dve', 'tensor_tensor', reads=[('bank', bi)], writes=[('h', m, tg)], out=hT[:, m, tg * 512:(tg + 1) * 512],
                      in0=hT[:, m, tg * 512:(tg + 1) * 512], in1=banks[bi][:, :], op=ALU.add)
            if release:
                W.release(pd)

        wheld = {}
        nrm_t = RG[:, (32768 + 65536) // 2:(32768 + 65536 + 16384) // 2].bitcast(F32).rearrange("p (a b) -> p a b", a=8)

        gfin_s = RG[:, (32768 + 65536) // 2:(32768 + 65536 + 4096) // 2].bitcast(F32)
        ssf = RG[:, (32768 + 65536 + 4096) // 2:(32768 + 65536 + 4096 + 128) // 2].bitcast(F32)
        rsf = RG[:, (32768 + 65536 + 4096 + 128) // 2:(32768 + 65536 + 4096 + 192) // 2].bitcast(F32)

        def final_tile(i):
            tg = i // 4
            ob_ = xs2[i % 2]
            bb = [nbank(), nbank()]
            for half in range(2):
                bi = bb[half]
                for c in range(4):
                    k = half * 4 + c
                    E('pe', 'transpose', reads=[('h', k, tg), 'identf'], writes=[('bank', bi)], inc=(c == 3),
                      out=banks[bi][:, c * 128:(c + 1) * 128], in_=hT[:, k, i * 128:(i + 1) * 128], identity=identf[:, 0:128])
                jk = sA if half == 0 else sB
                E('act', 'activation', reads=[('bank', bi), 'ssf'], writes=['sA' if half == 0 else 'sB', ('ssf', i, half)],
                  out=jk, in_=banks[bi][:, :], func=AF.Square, accum_out=ssf[:, 2 * i + half:2 * i + half + 1])
            E('dve', 'tensor_tensor', reads=[('ssf', i, 0), ('ssf', i, 1)], writes=[('rsf', i)], out=rsf[:, i:i + 1],
              in0=ssf[:, 2 * i:2 * i + 1], in1=ssf[:, 2 * i + 1:2 * i + 2], op=ALU.add)
            E('dve', 'tensor_scalar', reads=[('rsf', i)], writes=[('rsf', i)], out=rsf[:, i:i + 1], in0=rsf[:, i:i + 1],
              scalar1=1.0 / D, scalar2=1e-6, op0=ALU.mult, op1=ALU.add)
            E('act', 'activation', reads=[('rsf', i)], writes=[('rsf', i)], out=rsf[:, i:i + 1], in_=rsf[:, i:i + 1], func=AF.Ln)
            E('act', 'activation', reads=[('rsf', i)], writes=[('rsf', i)], out=rsf[:, i:i + 1], in_=rsf[:, i:i + 1], func=AF.Exp, scale=-0.5)
            for half in range(2):
                E('dve', 'scalar_tensor_tensor', reads=[('bank', bb[half]), ('rsf', i), 'gfin', ('ostage_rd', i % 2)], writes=[('ostage', i % 2, half)],
                  out=ob_[:, half * 512:(half + 1) * 512], in0=banks[bb[half]][:, :], scalar=rsf[:, i:i + 1],
                  in1=gfin_s[:, half * 512:(half + 1) * 512], op0=ALU.mult, op1=ALU.mult)
            P.dma('sp', out_d[i * 128:(i + 1) * 128, :], ob_, reads=[('ostage', i % 2, 0), ('ostage', i % 2, 1)], writes=[('ostage_rd', i % 2)])

        def final_tg(tg):
            for i in range(4 * tg, 4 * tg + 4):
                final_tile(i)

        P.dma('sp', gfin_s, gfin_d, writes=['gfin'] + [('yT', k_, tl_) for k_ in range(8) for tl_ in range(2)])
        E('dve', 'memset', reads=['gfin'], writes=['ssf'] + [('yT', k_, tl_) for k_ in range(8) for tl_ in range(2)], ap=ssf, constant=0.0)
        mlp_up(0)
        for fb in range(7):
            mlp_up(fb + 1)
            mlp_down(fb)
        mlp_down(7, tgs=(0,), release=False)
        mlp_down(7, tgs=(1,), release=False)
        final_tg(0)
        mlp_down(7, tgs=(2,), release=False)
        final_tg(1)
        mlp_down(7, tgs=(3,), release=True)
        final_tg(2)
        final_tg(3)
        P.barrier()
        P.replay()
    return nc


_CACHE = {}


def kernel(**inputs):
    import ml_dtypes
    wall, prm, gfin = host_weights(inputs)
    C = host_consts()
    base = {"wall": wall, "prm": prm, "gfin": gfin}
    for k, v in C.items():
        base[k] = v.astype(np.float32) if k in ('c_topk', 'c_identf') else v.astype(ml_dtypes.bfloat16)
    x = np.asarray(inputs['x'], np.float32)
    in_maps = []
    for b in range(8):
        m = dict(base)
        m['x'] = np.ascontiguousarray(x[b])
        in_maps.append(m)
    if 'nc' not in _CACHE:
        _CACHE['nc'] = build_program()
    res = run_bass_kernel_spmd(_CACHE['nc'], in_maps, core_ids=list(range(8)))
    return np.stack([np.asarray(r['out'], np.float32) for r in res.results], 0)
```

```python
import numpy as np
from contextlib import ExitStack
import concourse.bass as bass
import concourse.mybir as mybir
from concourse.bass_utils import run_bass_kernel_spmd

F32 = mybir.dt.float32
BF16 = mybir.dt.bfloat16
AF = mybir.ActivationFunctionType
ALU = mybir.AluOpType

ENGS = ['pe', 'act', 'dve', 'pool', 'sp']
NDS = 40
NDS_SW = 8
NEG = -30000.0
T = 2048
D = 1024
NSLOT = 4


class Prog:
    def __init__(self, nc, ctx):
        self.nc = nc
        self.ops = {e: [] for e in ENGS}
        self.esem = {e: ctx.enter_context(nc.semaphore("es_" + e)) for e in ENGS}
        self.ecnt = {e: 0 for e in ENGS}
        self.dsems = [ctx.enter_context(nc.semaphore("ds%d" % i)) for i in range(NDS)]
        self.dcnt = [0] * NDS
        self.dnext = 0
        self.dnext_sw = 0
        self.seen = {e: {} for e in ENGS}
        self.lastw = {}
        self.reads = {}
        self.pending = {e: False for e in ENGS}

    def _deps(self, eng, reads, writes, extra=()):
        deps = list(extra)
        for k in reads:
            t = self.lastw.get(k)
            if t is not None:
                deps.append(t)
        for k in writes:
            t = self.lastw.get(k)
            if t is not None:
                deps.append(t)
            deps.extend(self.reads.get(k, ()))
        waits = {}
        for (s, v) in deps:
            if s == ('e', eng) and eng == 'pe':
                continue
            if self.seen[eng].get(s, 0) >= v:
                continue
            if waits.get(s, 0) < v:
                waits[s] = v
        for s, v in waits.items():
            self.seen[eng][s] = v
        return list(waits.items())

    def _commit(self, token, reads, writes):
        for k in writes:
            self.lastw[k] = token
            self.reads[k] = []
        for k in reads:
            self.reads.setdefault(k, []).append(token)

    def emit(self, eng, fn, reads=(), writes=(), inc=True):
        waits = self._deps(eng, reads, writes)
        for (s, v) in waits:
            if s == ('e', eng):
                assert v <= self.ecnt[eng], "self-wait on un-inc'd instruction"
        if inc:
            self.ecnt[eng] += 1
            token = (('e', eng), self.ecnt[eng])
            self.ops[eng].append((waits, fn, ('e', eng)))
            self.pending[eng] = False
        else:
            token = (('e', eng), self.ecnt[eng] + 1)
            self.ops[eng].append((waits, fn, None))
            self.pending[eng] = True
        self._commit(token, reads, writes)
        return token

    def dma(self, eng, out, in_, reads=(), writes=(), **kw):
        if eng == 'pool':
            i = self.dnext_sw
            self.dnext_sw = (self.dnext_sw + 1) % NDS_SW
        else:
            i = NDS_SW + self.dnext
            self.dnext = (self.dnext + 1) % (NDS - NDS_SW)
        extra = []
        if self.dcnt[i] > 0:
            extra.append((('d', i), self.dcnt[i]))
        waits = self._deps(eng, reads, writes, extra)
        self.dcnt[i] += 16
        token = (('d', i), self.dcnt[i])
        fn = (lambda e, out=out, in_=in_, kw=kw: e.dma_start(out=out, in_=in_, **kw))
        self.ops[eng].append((waits, fn, ('d', i)))
        self._commit(token, reads, writes)
        return token

    def barrier(self):
        for e in ENGS:
            assert not self.pending[e], e
        allw = []
        for e in ENGS:
            if self.ecnt[e] > 0:
                allw.append((('e', e), self.ecnt[e]))
        for i in range(NDS):
            if self.dcnt[i] > 0:
                allw.append((('d', i), self.dcnt[i]))
        for e in ENGS:
            waits = [(s, v) for (s, v) in allw if self.seen[e].get(s, 0) < v]
            for s, v in waits:
                self.seen[e][s] = v
            self.ops[e].append((waits, None, None))
        self.lastw = {}
        self.reads = {}

    def _sem(self, s):
        return self.esem[s[1]] if s[0] == 'e' else self.dsems[s[1]]

    def replay(self):
        nc = self.nc
        hmap = {'pe': 'tensor', 'act': 'scalar', 'dve': 'vector', 'pool': 'gpsimd', 'sp': 'sync'}
        with nc.Block() as block:
            for e in ENGS:
                ops = self.ops[e]

                def body(engine, ops=ops):
                    for (waits, fn, inc) in ops:
                        for (s, v) in waits:
                            engine.wait_ge(self._sem(s), v)
                        if fn is None:
                            continue
                        ins = fn(engine)
                        if inc is not None:
                            ins.then_inc(self._sem(inc), 1 if inc[0] == 'e' else 16)
                getattr(block, hmap[e])(body)


BLK = {}
_names = (['W1K', 'W1V', 'KV', 'TM', 'QN'] + ['FOX%d' % p for p in range(4)] + ['MG%d' % j for j in range(4)]
          + ['BR0', 'BR1', 'OUT0', 'OUT1'] + ['UP%d' % f for f in range(8)] + ['DOWN%d' % f for f in range(8)])
for _i, _n in enumerate(_names):
    BLK[_n] = _i
NBLK = len(_names)

STREAM = ['KV', 'TM', 'QN', 'W1K', 'W1V', 'FOX0', 'FOX1', 'FOX2', 'FOX3']
for _half in range(2):
    STREAM += ['MG0', 'BR0', 'MG1', 'MG2', 'BR1', 'MG3', 'OUT0', 'OUT1']
STREAM += ['UP0']
for _f in range(8):
    if _f + 1 < 8:
        STREAM.append('UP%d' % (_f + 1))
    STREAM.append('DOWN%d' % _f)


def _std_block(W):
    W = np.asarray(W, np.float32)
    if W.shape[1] < 512:
        Wp = np.zeros((W.shape[0], 512), np.float32)
        Wp[:, :W.shape[1]] = W
        W = Wp
    return W.reshape(8, 128, 512).transpose(1, 0, 2).reshape(128, 4096)


def _w1_block(w1):
    a = np.asarray(w1, np.float32).reshape(32, 64, 128).transpose(1, 0, 2)
    return np.concatenate([a, a], 0).reshape(128, 4096)


def host_weights(inp):
    w_in = np.asarray(inp['w_in'][0], np.float32)
    q_n, kc, vc, ks, vs, kw, vw = (w_in[:, 0:512], w_in[:, 512:640], w_in[:, 640:768], w_in[:, 768:896],
                                  w_in[:, 896:1024], w_in[:, 1024:1152], w_in[:, 1152:1280])
    gate, q_f, k_f, v_f, f_l = (w_in[:, 1280:1304], w_in[:, 1304:1816], w_in[:, 1816:2328], w_in[:, 2328:2840],
                                w_in[:, 2840:2848])
    wall = np.zeros((NBLK, 128, 4096), np.float32)
    wall[BLK['W1K']] = _w1_block(inp['cmp_w1_k'][0])
    wall[BLK['W1V']] = _w1_block(inp['cmp_w1_v'][0])
    wall[BLK['KV']] = _std_block(np.concatenate([kc, vc, ks, kw], 1))
    wall[BLK['TM']] = _std_block(np.concatenate([vs, vw, gate, f_l], 1))
    wall[BLK['QN']] = _std_block(q_n)
    for p in range(4):
        sl = slice(128 * p, 128 * p + 128)
        wall[BLK['FOX%d' % p]] = _std_block(np.concatenate([q_f[:, sl], k_f[:, sl], v_f[:, sl]], 1))
    w_mg = np.asarray(inp['w_merge_gate'][0], np.float32)
    for j in range(4):
        cols = [w_mg[:, 128 * (2 * j):128 * (2 * j) + 128], w_mg[:, 128 * (2 * j + 1):128 * (2 * j + 1) + 128],
                w_mg[:, 1024 + 128 * (2 * j):1024 + 128 * (2 * j) + 128],
                w_mg[:, 1024 + 128 * (2 * j + 1):1024 + 128 * (2 * j + 1) + 128]]
        wall[BLK['MG%d' % j]] = _std_block(np.concatenate(cols, 1))
    w_br = np.concatenate([np.asarray(inp['w_branch_nsa'][0], np.float32), np.asarray(inp['w_branch_fox'][0], np.float32)], 0)
    w_out = np.asarray(inp['w_out'][0], np.float32)
    for j in range(2):
        wall[BLK['BR%d' % j]] = _std_block(w_br[:, 512 * j:512 * j + 512])
        wall[BLK['OUT%d' % j]] = _std_block(w_out[:, 512 * j:512 * j + 512])
    w_up = np.asarray(inp['w_up'][0], np.float32)
    w_down = np.asarray(inp['w_down'][0], np.float32)
    for f in range(8):
        wall[BLK['UP%d' % f]] = _std_block(w_up[:, 512 * f:512 * f + 512])
        wall[BLK['DOWN%d' % f]] = w_down[512 * f:512 * f + 512, :].reshape(4, 128, 1024).transpose(1, 0, 2).reshape(128, 4096)
    gfin = np.ascontiguousarray(np.broadcast_to(np.asarray(inp['norm_final'], np.float32)[None, :], (128, 1024)))
    prm = np.zeros((128, 240), np.float32)
    prm[:, 0:8] = np.asarray(inp['norm_mix'][0], np.float32).reshape(8, 128).T
    prm[:, 8:16] = np.asarray(inp['norm_mlp'][0], np.float32).reshape(8, 128).T
    prm[:, 16:24] = np.asarray(inp['norm_final'], np.float32).reshape(8, 128).T
    prm[:, 24:40] = np.asarray(inp['b_merge_gate'][0], np.float32).reshape(16, 128).T
    prm[:, 40:48] = np.asarray(inp['fox_f_bias'][0], np.float32)[None, :]
    prm[0:64, 48:80] = np.asarray(inp['cmp_pos_k'][0], np.float32).T
    prm[0:64, 80:112] = np.asarray(inp['cmp_pos_v'][0], np.float32).T
    prm[:, 112:176] = np.asarray(inp['cmp_w2_k'][0], np.float32)
    prm[:, 176:240] = np.asarray(inp['cmp_w2_v'][0], np.float32)
    return wall, prm, gfin


def host_consts():
    C = {}
    p = np.arange(128)
    t = np.arange(T)
    hi = (128 * (t // 128)).astype(np.float32)
    lo = (t % 128).astype(np.float32)
    slopes = (2.0 ** (-8.0 * np.arange(1, 9) / 8)).astype(np.float32)
    C['c_ident'] = np.eye(128, dtype=np.float32)
    idf = np.zeros((128, 512), np.float32)
    idf[:, 0:128] = np.eye(128, dtype=np.float32)
    idf[:, 128:256] = 1.0
    idf[:, 256:384] = (p[:, None] <= p[None, :])
    idf[:, 384:512] = ((p[:, None] // 16 == p[None, :] // 16) & (p[:, None] % 16 < p[None, :] % 16))
    C['c_identf'] = idf
    tri = np.zeros((128, 640), np.float32)
    tri[:, 0:128] = np.where(p[:, None] <= p[None, :], 0.0, NEG)
    tri[:, 128:256] = np.where(p[None, :] < p[:, None], 0.0, NEG)
    tri[:, 256:384] = (p[:, None] <= p[None, :]).astype(np.float32)
    tri[:, 384:512] = (p[:, None] <= p[None, :])
    tri[:, 512:640] = (p[None, :] < p[:, None])
    C['c_tri'] = tri
    n = np.arange(128)
    ce = 16 * n + 31
    cm = np.where((t[None, :] >= ce[:, None]) & (n[:, None] < 127), 0.0, NEG).astype(np.float32)
    C['c_cmpmask'] = cm
    tt = (np.arange(16)[None, :] * 128 + p[:, None])
    jj = np.arange(32)
    cur = tt // 64
    forced = (jj[None, None, :] == 0) | (jj[None, None, :] == cur[:, :, None]) | (jj[None, None, :] == cur[:, :, None] - 1)
    valid = (jj[None, None, :] * 64) <= tt[:, :, None]
    V = (valid & ~forced).astype(np.float32)
    Cc = np.where(forced, 100.0 + jj[None, None, :], np.where(valid, 0.0, -1.0 - 0.01 * jj[None, None, :])).astype(np.float32)
    C['c_topk'] = np.concatenate([V.reshape(128, 512), Cc.reshape(128, 512)], 1)
    cs = 16 * n
    ov = ((cs[:, None] <= jj[None, :] * 64 + 63) & (ce[:, None] >= jj[None, :] * 64) & (n[:, None] < 127)).astype(np.float32)
    C['c_overlap'] = ov
    alq = np.zeros((8, 4, T), np.float32)
    for h in range(8):
        alq[h, 0] = -slopes[h] * hi
        alq[h, 1] = -slopes[h] * lo
        alq[h, 2] = slopes[h]
        alq[h, 3] = slopes[h]
    C['c_alq'] = alq
    alk = np.zeros((4, T), np.float32)
    alk[0] = 1.0
    alk[1] = 1.0
    alk[2] = hi
    alk[3] = lo
    C['c_alk'] = alk
    alkc = np.zeros((4, 128), np.float32)
    alkc[0, :127] = 1.0
    alkc[1, :127] = 1.0
    alkc[2, :127] = (128 * (ce[:127] // 128))
    alkc[3, :127] = (ce[:127] % 128)
    C['c_alkc'] = alkc
    E = np.zeros((32, T), np.float32)
    E[t // 64, t] = 1.0
    C['c_E'] = E
    return C


CONST_SHAPES = {'c_ident': [128, 128], 'c_tri': [128, 640], 'c_cmpmask': [128, T], 'c_topk': [128, 1024],
                'c_overlap': [128, 32], 'c_identf': [128, 512], 'c_alq': [8, 4, T], 'c_alk': [4, T], 'c_alkc': [4, 128], 'c_E': [32, T]}


def build_program(debug=(), stop_after=None):
    nc = bass.Bass("TRN2", target_bir_lowering=False)
    x_d = nc.dram_tensor("x", [T, D], F32, kind="ExternalInput").ap()
    wall_d = nc.dram_tensor("wall", [NBLK, 128, 4096], F32, kind="ExternalInput").ap()
    prm_d = nc.dram_tensor("prm", [128, 240], F32, kind="ExternalInput").ap()
    gfin_d = nc.dram_tensor("gfin", [128, 1024], F32, kind="ExternalInput").ap()
    cd = {}
    for name, shp in CONST_SHAPES.items():
        dt_ = F32 if name in ('c_topk', 'c_identf') else BF16
        cd[name] = nc.dram_tensor(name, shp, dt_, kind="ExternalInput").ap()
    out_d = nc.dram_tensor("out", [T, D], F32, kind="ExternalOutput").ap()
    DBG_SHAPES = {'uT': ([128, 8 * T], BF16), 'kcT': ([128, T], BF16), 'ksaug0': ([128, T], BF16), 'gates': ([128, 16 * 24], F32),
                  'vsaug': ([128, 16 * 2 * 65], BF16), 'kcmpaug0': ([128, 128], BF16), 'vcmpaug0': ([128, 104], BF16),
                  'qn0': ([100, T], BF16), 'otm': ([128, 16 * 1024], BF16), 'impacc': ([128, 16 * 2 * 32], F32),
                  'selb': ([128, 64 + 1024], BF16), 'cneg': ([128, 128], F32), 'fq0': ([70, T], BF16), 'fk0': ([70, T], BF16),
                  'oT': ([128, 8 * T], BF16), 'hT': ([128, 8 * T], F32), 'yT': ([128, 8 * 1024], BF16), 'vT': ([128, 8 * T], BF16)}
    dbg_d = {}
    for name in debug:
        shp, dt_ = DBG_SHAPES[name]
        dbg_d[name] = nc.dram_tensor("dbg_" + name, shp, dt_, kind="ExternalOutput").ap()

    with ExitStack() as ctx:
        P = Prog(nc, ctx)

        def E(eng, method, reads=(), writes=(), inc=True, **kw):
            return P.emit(eng, (lambda e, method=method, kw=kw: getattr(e, method)(**kw)), reads, writes, inc)

        def sbt(name, shape, dt_):
            return ctx.enter_context(nc.sbuf_tensor(name, shape, dt_))

        def dump(name, ap2d, reads=()):
            if name in dbg_d:
                P.dma('sp', dbg_d[name], ap2d, reads=reads)

        uT_t = sbt("uT", [128, 8 * T], BF16)
        uT = uT_t[:, :].rearrange("p (c t) -> p c t", c=8)
        arena = sbt("arena", [128, NSLOT * 4096], BF16)
        identb = sbt("identb", [128, 128], BF16)
        identf = sbt("identf", [128, 512], F32)
        trib = sbt("trib", [128, 640], BF16)
        prm = sbt("prm_s", [128, 240], F32)
        RG_BYTES = 142336
        RG = sbt("RG", [128, RG_BYTES // 2], BF16)
        banks = [ctx.enter_context(nc.psum_tensor("bank%d" % i, [128, 512], F32)) for i in range(8)]

        class Carver:
            def __init__(self):
                self.off = 0

            def alloc(self, dims, dt_):
                n = 1
                for d_ in dims:
                    n *= d_
                nb = n * (4 if dt_ == F32 else 2)
                nb = (nb + 63) // 64 * 64
                assert self.off + nb <= RG_BYTES, (self.off, nb)
                ap = RG[:, self.off // 2:(self.off + nb) // 2]
                self.off += nb
                if dt_ == F32:
                    ap = ap.bitcast(F32)
                ap = ap[:, 0:n]
                if len(dims) == 2:
                    ap = ap.rearrange("p (a b) -> p a b", a=dims[0])
                elif len(dims) == 3:
                    ap = ap.rearrange("p (a b c) -> p a b c", a=dims[0], b=dims[1])
                return ap

        class WStream:
            def __init__(self):
                self.loaded = 0
                self.released = set()

            def slot_ap(self, pos):
                s = pos % NSLOT
                return arena[:, s * 4096:(s + 1) * 4096]

            def pump(self):
                while self.loaded < len(STREAM) and (self.loaded < NSLOT or (self.loaded - NSLOT) in self.released):
                    pos = self.loaded
                    P.dma('pool', self.slot_ap(pos), wall_d[BLK[STREAM[pos]]], writes=[('w', pos % NSLOT)])
                    self.loaded += 1

            def acquire(self, name):
                pos = 0
                while pos in self.released or STREAM[pos] != name:
                    pos += 1
                assert pos < self.loaded, (name, pos, self.loaded)
                return pos, self.slot_ap(pos), ('w', pos % NSLOT)

            def release(self, pos):
                self.released.add(pos)
                self.pump()

        W = WStream()

        P.dma('sp', identb[:, :], cd['c_ident'], writes=['identb'])
        P.dma('sp', prm[:, :], prm_d, writes=['prm'])

        cv = Carver()
        ksaug = [cv.alloc([T], BF16) for _ in range(2)]
        kwaug = [cv.alloc([T], BF16) for _ in range(2)]
        qn = [cv.alloc([T], BF16) for _ in range(4)]
        fq = [cv.alloc([T], BF16) for _ in range(2)]
        fk = [cv.alloc([T], BF16) for _ in range(2)]
        fv = cv.alloc([16, 2, 65], BF16)
        vsaug = cv.alloc([16, 2, 65], BF16)
        vwaug = cv.alloc([16, 2, 65], BF16)
        gates = cv.alloc([16, 24], F32)
        spx = cv.alloc([128], F32)
        cneg = cv.alloc([128], F32)
        tsum = cv.alloc([128], F32)
        cref = cv.alloc([128], F32)
        r1 = cv.alloc([128], F32)
        a123 = [cv.alloc([128], BF16) for _ in range(3)]
        otm_off = cv.off
        otm = cv.alloc([16, 1024], BF16)
        selb = cv.alloc([64 + 1024], BF16)
        impacc = cv.alloc([16, 2, 32], F32)
        kcmpaug = [cv.alloc([128], BF16) for _ in range(2)]
        vcmpaug = [cv.alloc([104], BF16) for _ in range(2)]
        NPT = 6
        PT = [cv.alloc([512], BF16) for _ in range(NPT)]
        cmpmask = cv.alloc([T], BF16)
        topkVC = cv.alloc([1024], F32)
        score = cv.alloc([16, 32], F32)
        sc2 = cv.alloc([32], F32)
        mx8 = cv.alloc([8], F32)
        rec = cv.alloc([4], F32)
        fsc = cv.alloc([4], F32)
        tmpos = [cv.alloc([4, 64], F32) for _ in range(2)]
        tmpo_ctr = [0]
        tmpi = cv.alloc([4, 32], F32)
        w2b = cv.alloc([128], BF16)
        posb = cv.alloc([64], BF16)
        cbias = cv.alloc([2], F32)
        cbias2 = cv.alloc([2], F32)
        gxs = [[cv.alloc([128], F32) for _ in range(4)] for _ in range(2)]
        ghb4 = [cv.alloc([128], BF16) for _ in range(4)]
        kcr = cv.alloc([16, 128], BF16)
        vcr = cv.alloc([16, 128], BF16)
        kcT = fq[0]
        vcT = fq[1]
        otm_flat = RG[:, otm_off // 2:otm_off // 2 + 16384]
        NXS = 4
        xs = [otm_flat[:, 2048 * j:2048 * (j + 1)].bitcast(F32) for j in range(NXS)]
        xn = [otm_flat[:, 8192 + 1024 * j:8192 + 1024 * (j + 1)] for j in range(NXS)]
        ss = otm_flat[:, 12288:12320].bitcast(F32)
        rstd = otm_flat[:, 12320:12352].bitcast(F32)

        for i0 in range(NXS):
            P.dma('sp', xs[i0], x_d[i0 * 128:(i0 + 1) * 128, :], writes=[('xs', i0)])
        P.lastw[('w', 0)] = P.lastw[('xs', 1)]
        W.pump()
        P.dma('sp', identf[:, :], cd['c_identf'], writes=['identf'])
        P.dma('sp', trib[:, :], cd['c_tri'], writes=['trib'])
        P.dma('sp', cmpmask, cd['c_cmpmask'], writes=['cmpmask'])
        P.dma('sp', topkVC, cd['c_topk'], writes=['topkVC'])
        init_ms = []

        def ms(key, ap, val):
            init_ms.append(lambda: E('dve', 'memset', writes=[key], ap=ap, constant=val))
        ms('vsaug', vsaug, 1.0)
        ms('vwaug', vwaug, 1.0)
        for g in range(2):
            for (t_, nm) in ((ksaug[g], 'ks'), (kwaug[g], 'kw')):
                P.dma('sp', t_[96:100, :], cd['c_alk'], writes=[(nm, g, 'aug')])
            P.dma('sp', ksaug[g][64:96, :], cd['c_E'], writes=[('ks', g, 'E')])
            ms(('kw', g, 'E'), kwaug[g][64:96, :], 0.0)
            ms(('kcmp', g), kcmpaug[g][:, :], 0.0)
            ms(('vcmp', g), vcmpaug[g][:, :], 0.0)
        for hl in range(4):
            ms(('qn', hl, 'sel'), qn[hl][64:96, :], 0.0)
        ms('fv', fv, 1.0)
        ms('selb', selb, 0.0)
        E('dve', 'tensor_copy', reads=['prm'], writes=['w2b'], out=w2b, in_=prm[:, 112:240])
        E('dve', 'tensor_copy', reads=['prm'], writes=['posb'], out=posb[0:64, :], in_=prm[0:64, 48:112])

        pj = [0]

        def proj_bank():
            pj[0] += 1
            return 6 + (pj[0] % 2)

        def proj_fm_tg(slot, wkey, col0, evac, tg, ukeys=()):
            sv = slot.rearrange("p (k n) -> p k n", k=8)
            bi = proj_bank()
            for k in range(8):
                E('pe', 'matmul', reads=[wkey] + list(ukeys), writes=[('bank', bi)], inc=(k == 7),
                  out=banks[bi][:, :], lhsT=sv[:, k, col0:col0 + 128], rhs=uT[:, k, tg * 512:(tg + 1) * 512],
                  start=(k == 0), stop=(k == 7))
            evac(tg, bi)

        def proj_fm(slot, wkey, col0, evac):
            for tg in range(4):
                proj_fm_tg(slot, wkey, col0, evac, tg)

        def evac_plain(dst, key):
            def f(tg, bi):
                E('dve', 'tensor_copy', reads=[('bank', bi)], writes=[(key, tg)], out=dst[:, tg * 512:(tg + 1) * 512], in_=banks[bi][:, :])
            return f

        def evac_split(dsts, key, scale=None):
            def f(tg, bi):
                for a in range(2):
                    if scale is None:
                        E('dve', 'tensor_copy', reads=[('bank', bi)], writes=[(key, a, 'q', tg)],
                          out=dsts[a][0:64, tg * 512:(tg + 1) * 512], in_=banks[bi][a * 64:(a + 1) * 64, :])
                    else:
                        E('dve', 'tensor_scalar', reads=[('bank', bi)], writes=[(key, a, 'q', tg)],
                          out=dsts[a][0:64, tg * 512:(tg + 1) * 512], in0=banks[bi][a * 64:(a + 1) * 64, :],
                          scalar1=scale, scalar2=None, op0=ALU.mult)
            return f

        pos_kv, slot_kv, key_kv = W.acquire('KV')
        pos_tm, slot_tm, key_tm = W.acquire('TM')
        pos_qn, slot_qn, key_qn = W.acquire('QN')
        pos_w1k, slot_w1k, key_w1k = W.acquire('W1K')
        svtm = slot_tm.rearrange("p (k n) -> p k n", k=8)

        E('dve', 'memset', writes=['ss'], ap=ss, constant=0.0)

        def stageA(i):
            b = i % NXS
            if i >= NXS:
                P.dma('sp', xs[b], x_d[i * 128:(i + 1) * 128, :], writes=[('xs', b)])
            E('act', 'activation', reads=[('xs', b), 'ss'], writes=[('xn', b), ('ss', i)],
              out=xn[b], in_=xs[b], func=AF.Square, accum_out=ss[:, i:i + 1])
            E('dve', 'tensor_scalar', reads=[('ss', i)], writes=[('rstd', i)], out=rstd[:, i:i + 1], in0=ss[:, i:i + 1],
              scalar1=1.0 / D, scalar2=1e-6, op0=ALU.mult, op1=ALU.add)

        def stageB(i):
            b = i % NXS
            tb = 3 + (i % 3)
            E('act', 'activation', reads=[('rstd', i)], writes=[('rstd', i)], out=rstd[:, i:i + 1], in_=rstd[:, i:i + 1], func=AF.Ln)
            E('act', 'activation', reads=[('rstd', i)], writes=[('rstd', i)], out=rstd[:, i:i + 1], in_=rstd[:, i:i + 1], func=AF.Exp, scale=-0.5)
            E('act', 'activation', reads=[('xs', b), ('rstd', i)], writes=[('xn', b)],
              out=xn[b], in_=xs[b], func=AF.Copy, scale=rstd[:, i:i + 1])
            ptb = banks[tb][:, :].bitcast(BF16)
            for c in range(8):
                E('pe', 'transpose', reads=[('xn', b), 'identb'], writes=[('bank', tb)], inc=(c == 7),
                  out=ptb[:, c * 128:(c + 1) * 128], in_=xn[b][:, c * 128:(c + 1) * 128], identity=identb[:, :])
            E('dve', 'tensor_tensor', reads=[('bank', tb), 'prm'], writes=[('uT', i)],
              out=uT[:, :, i * 128:(i + 1) * 128], in0=ptb.rearrange("p (c t) -> p c t", c=8),
              in1=prm[:, 0:8].unsqueeze(2).to_broadcast([128, 8, 128]), op=ALU.mult)

        def projs(tg, j):
            uk = [('uT', q_) for q_ in range(4 * tg, 4 * tg + 4)]
            if j == 0:
                proj_fm_tg(slot_kv, key_kv, 0, evac_plain(kcT, 'kcT'), tg, uk)
            elif j == 1:
                proj_fm_tg(slot_kv, key_kv, 128, evac_plain(vcT, 'vcT'), tg, uk)
            elif j == 2:
                proj_fm_tg(slot_kv, key_kv, 256, evac_split(ksaug, 'ks'), tg, uk)
            else:
                proj_fm_tg(slot_kv, key_kv, 384, evac_split(kwaug, 'kw'), tg, uk)
            for it in range(4 * tg + j, 4 * tg + j + 1):
                bi = proj_bank()
                for k in range(8):
                    E('pe', 'matmul', reads=[key_tm, ('uT', it)], writes=[('bank', bi)], inc=(k == 7),
                      out=banks[bi][:, 0:288], lhsT=uT[:, k, it * 128:(it + 1) * 128], rhs=svtm[:, k, 0:288],
                      start=(k == 0), stop=(k == 7))
                E('dve', 'tensor_copy', reads=[('bank', bi), 'vsaug'], writes=[('vsaug', it)],
                  out=vsaug[:, it, :, 0:64], in_=banks[bi][:, 0:128].rearrange("p (g d) -> p g d", g=2))
                E('dve', 'tensor_copy', reads=[('bank', bi), 'vwaug'], writes=[('vwaug', it)],
                  out=vwaug[:, it, :, 0:64], in_=banks[bi][:, 128:256].rearrange("p (g d) -> p g d", g=2))
                E('dve', 'tensor_copy', reads=[('bank', bi)], writes=[('gates', it)], out=gates[:, it, :], in_=banks[bi][:, 256:280])
                E('dve', 'tensor_tensor', reads=[('bank', bi), 'prm'], writes=[('spx', it)],
                  out=spx.rearrange("p (h i) -> p h i", h=8)[:, :, it], in0=banks[bi][:, 280:288], in1=prm[:, 40:48], op=ALU.add)

        stageA(0)
        for i in range(16):
            if i + 1 < 16:
                stageA(i + 1)
            stageB(i)
            if init_ms:
                init_ms.pop(0)()
            if i >= 4:
                projs((i - 4) // 4, i % 4)
        while init_ms:
            init_ms.pop(0)()
        for j in range(4):
            projs(3, j)
        W.release(pos_kv)
        W.release(pos_tm)
        pos_w1v, slot_w1v, key_w1v = W.acquire('W1V')
        allg = [('gates', i) for i in range(16)]
        gflat = gates.rearrange("p a b -> p (a b)")
        E('act', 'activation', reads=allg, writes=['gates'], out=gflat, in_=gflat, func=AF.Exp, scale=-1.0)
        E('dve', 'tensor_scalar', reads=['gates'], writes=['gates'], out=gflat, in0=gflat, scalar1=1.0, scalar2=None, op0=ALU.add)
        E('dve', 'reciprocal', reads=['gates'], writes=['gates'], out=gflat, in_=gflat)
        allspx = [('spx', i) for i in range(16)]
        E('act', 'activation', reads=allspx, writes=['spx'], out=spx, in_=spx, func=AF.Exp, scale=-1.0)
        E('act', 'activation', reads=['spx'], writes=['spx'], out=spx, in_=spx, func=AF.Ln, bias=1.0)

        w1kv = slot_w1k.rearrange("p (i m) -> p i m", i=32)
        w1vv = slot_w1v.rearrange("p (i m) -> p i m", i=32)
        allkc = [('kcT', tg) for tg in range(4)]
        allvc = [('vcT', tg) for tg in range(4)]
        E('dve', 'tensor_copy', reads=allkc, writes=['kcr'], out=kcr, in_=kcT.rearrange("p (j r) -> p r j", r=16))
        E('dve', 'tensor_copy', reads=allvc, writes=['vcr'], out=vcr, in_=vcT.rearrange("p (j r) -> p r j", r=16))
        bi = proj_bank()
        for j, (wv, wkey) in enumerate(((w1kv, key_w1k), (w1vv, key_w1v))):
            for i in range(32):
                E('pe', 'matmul', reads=[wkey, 'posb'], writes=[('bank', bi)], inc=(i == 31),
                  out=banks[bi][:, j:j + 1], lhsT=wv[0:64, i, :], rhs=posb[0:64, j * 32 + i:j * 32 + i + 1],
                  start=(i == 0), stop=(i == 31))
        E('dve', 'tensor_copy', reads=[('bank', bi)], writes=['cbias'], out=cbias, in_=banks[bi][:, 0:2])
        chains = [(g, j) for g in range(2) for j in range(2)]
        for ci, (g, j) in enumerate(chains):
            wv, wkey, srcr, skey = ((w1kv, key_w1k, kcr, 'kcr'), (w1vv, key_w1v, vcr, 'vcr'))[j]
            for i in range(32):
                E('pe', 'matmul', reads=[wkey, skey], writes=[('bank', ci)], inc=(i == 31),
                  out=banks[ci][:, 0:127], lhsT=wv[g * 64:(g + 1) * 64, i, :],
                  rhs=srcr[g * 64:(g + 1) * 64, i % 16, (i // 16):(i // 16) + 127], start=(i == 0), stop=(i == 31))
        for hl in range(4):
            P.dma('sp', qn[hl][96:100, :], cd['c_alq'][hl], writes=[('qn', hl, 'alibi')])
        proj_fm(slot_qn, key_qn, 0, evac_split([qn[0], qn[1]], ('qnq', 0), scale=0.125))
        for ci, (g, j) in enumerate(chains):
            cs_ = ci % 2
            gx = gxs[cs_]
            ghb = ghb4[ci]
            gk = ['gx%d_%d' % (q_, cs_) for q_ in range(4)] + ['ghb%d' % ci]
            bi = ci
            E('act', 'activation', reads=[('bank', bi), 'cbias'], writes=[gk[0]], out=gx[0][:, 0:127], in_=banks[bi][:, 0:127],
              func=AF.Identity, bias=cbias[:, j:j + 1])
            E('dve', 'tensor_tensor', reads=[gk[0]], writes=[gk[1]], out=gx[1][:, 0:127], in0=gx[0][:, 0:127], in1=gx[0][:, 0:127], op=ALU.mult)
            E('dve', 'tensor_scalar', reads=[gk[1]], writes=[gk[1]], out=gx[1][:, 0:127], in0=gx[1][:, 0:127],
              scalar1=0.044715, scalar2=1.0, op0=ALU.mult, op1=ALU.add)
            E('dve', 'tensor_tensor', reads=[gk[1], gk[0]], writes=[gk[2]], out=gx[2][:, 0:127], in0=gx[1][:, 0:127], in1=gx[0][:, 0:127], op=ALU.mult)
            E('act', 'activation', reads=[gk[2]], writes=[gk[3]], out=gx[3][:, 0:127], in_=gx[2][:, 0:127], func=AF.Exp, scale=-1.5957691216057308)
            E('dve', 'tensor_scalar', reads=[gk[3]], writes=[gk[3]], out=gx[3][:, 0:127], in0=gx[3][:, 0:127], scalar1=1.0, scalar2=None, op0=ALU.add)
            E('dve', 'reciprocal', reads=[gk[3]], writes=[gk[3]], out=gx[3][:, 0:127], in_=gx[3][:, 0:127])
            E('dve', 'tensor_tensor', reads=[gk[3], gk[0]], writes=[gk[4]], out=ghb[:, 0:127], in0=gx[3][:, 0:127], in1=gx[0][:, 0:127], op=ALU.mult)
        proj_fm(slot_qn, key_qn, 128, evac_split([qn[2], qn[3]], ('qnq', 1), scale=0.125))
        for ci, (g, j) in enumerate(chains):
            ghb = ghb4[ci]
            bo = proj_bank()
            if j == 0:
                E('pe', 'matmul', reads=['ghb%d' % ci, 'w2b'], writes=[('bank', bo)], out=banks[bo][0:64, 0:127], lhsT=w2b[:, 0:64],
                  rhs=ghb[:, 0:127], start=True, stop=True)
                E('dve', 'tensor_copy', reads=[('bank', bo), ('kcmp', g)], writes=[('kcmp', g)], out=kcmpaug[g][0:64, 0:127], in_=banks[bo][0:64, 0:127])
            else:
                E('pe', 'matmul', reads=['ghb%d' % ci, 'w2b'], writes=[('bank', bo)], out=banks[bo][0:127, 0:64], lhsT=ghb[:, 0:127],
                  rhs=w2b[:, 64:128], start=True, stop=True)
                E('dve', 'tensor_copy', reads=[('bank', bo), ('vcmp', g)], writes=[('vcmp', g)], out=vcmpaug[g][0:127, 0:64], in_=banks[bo][0:127, 0:64])
        for g in range(2):
            P.dma('sp', kcmpaug[g][96:100, :], cd['c_alkc'], reads=[('kcmp', g)], writes=[('kcmp', g)])
            E('dve', 'memset', reads=[('vcmp', g)], writes=[('vcmp', g)], ap=vcmpaug[g][0:127, 64:65], constant=1.0)
            P.dma('sp', vcmpaug[g][:, 65:97], cd['c_overlap'], reads=[('vcmp', g)], writes=[('vcmp', g)])
        bi = proj_bank()
        E('pe', 'transpose', reads=['spx', 'identf'], writes=[('bank', bi)], out=banks[bi][:, 0:128], in_=spx, identity=identf[:, 0:128])
        E('dve', 'tensor_copy', reads=[('bank', bi)], writes=['tsum'], out=tsum, in_=banks[bi][:, 0:128])
        E('dve', 'tensor_tensor_scan', reads=['tsum', 'identf'], writes=['cref'], out=cref, data0=identf[:, 128:256], data1=tsum,
          initial=0.0, op0=ALU.mult, op1=ALU.add)
        bi = proj_bank()
        E('pe', 'matmul', reads=['cref', 'identf'], writes=[('bank', bi)], out=banks[bi][:, 0:1], lhsT=identf[:, 384:512], rhs=cref[:, 127:128],
          start=True, stop=True)
        E('dve', 'tensor_copy', reads=[('bank', bi)], writes=['cbase'], out=cbias2[:, 0:1], in_=banks[bi][:, 0:1])
        E('dve', 'tensor_scalar', reads=['cref', 'cbase'], writes=['cneg'], out=cneg, in0=cref, scalar1=cbias2[:, 0:1], scalar2=None, op0=ALU.add)
        E('dve', 'tensor_copy', reads=['cneg'], writes=['a0'], out=a123[0], in_=cneg)
        E('dve', 'tensor_tensor', reads=['cneg', 'a0'], writes=['r1'], out=r1, in0=cneg, in1=a123[0], op=ALU.subtract)
        E('dve', 'tensor_copy', reads=['r1'], writes=['a1'], out=a123[1], in_=r1)
        E('dve', 'tensor_tensor', reads=['r1', 'a1'], writes=['r1'], out=r1, in0=r1, in1=a123[1], op=ALU.subtract)
        E('dve', 'tensor_copy', reads=['r1'], writes=['a2'], out=a123[2], in_=r1)
        W.release(pos_w1k)
        W.release(pos_w1v)
        P.barrier()
        for a_ in range(2):
            E('dve', 'memset', writes=[(('fq', 0), a_, 'aug')], ap=fq[a_][64:70, :], constant=1.0)
            E('dve', 'memset', writes=[(('fk', 0), a_, 'aug')], ap=fk[a_][64:70, :], constant=-1.0)
        if debug:
            E('dve', 'memset', writes=['otm_init'], ap=otm, constant=0.0)
            E('dve', 'memset', writes=['imp_init'], ap=impacc, constant=0.0)
        dump('uT', uT_t[:, :])
        dump('kcmpaug0', kcmpaug[0])
        dump('vcmpaug0', vcmpaug[0])
        if stop_after == '1B':
            P.barrier()
            P.replay()
            return nc

        tiles = []
        ctr = {'S': 0, 'PT': 0, 'O': 0}
        NOB = 2
        Obv = [banks[4 + b][:, :].rearrange("p (s c) -> p s c", s=4) for b in range(NOB)]

        def add_unit(ktiles, kaug, kkeys_fn, qaug, qkeys, K, tg, v_fn, vkeys_fn, nv, first_fn, last_fn, post, pre=None, kparts=128):
            ob = ctr['O'] % NOB
            ctr['O'] += 1
            n = len(ktiles)
            for idx, (i, col0, ncols, mask) in enumerate(ktiles):
                d = {'pre': [], 'post': [], 'first': (idx == 0)}
                if idx == 0 and pre is not None:
                    d['pre'].append(pre)

                def S(i=i, col0=col0, ncols=ncols, mask=mask, d=d):
                    sb_ = ctr['S'] % 4
                    ctr['S'] += 1
                    d['sb'] = sb_
                    E('pe', 'matmul', reads=list(kkeys_fn(i)) + list(qkeys), writes=[('bank', sb_)], inc=(mask is None),
                      out=banks[sb_][0:kparts, 0:ncols], lhsT=kaug(i), rhs=qaug[0:K, tg * 512 + col0: tg * 512 + col0 + ncols],
                      start=True, stop=(mask is None))
                    if mask is not None:
                        if mask[0] == 'cmp':
                            E('pe', 'matmul', reads=['identb', 'cmpmask'], writes=[('bank', sb_)],
                              out=banks[sb_][:, 0:512], lhsT=identb[:, :], rhs=cmpmask[:, tg * 512:(tg + 1) * 512], start=False, stop=True)
                        else:
                            lc = mask[1]
                            mo = 0 if mask[0] == 'ge' else 128
                            E('pe', 'matmul', reads=['identb', 'trib'], writes=[('bank', sb_)],
                              out=banks[sb_][:, lc:lc + 128], lhsT=identb[:, :], rhs=trib[:, mo:mo + 128], start=False, stop=True)

                def EXP(ncols=ncols, d=d, mask=mask):
                    pt_ = ctr['PT'] % NPT
                    ctr['PT'] += 1
                    d['pt'] = pt_
                    E('act', 'activation', reads=[('bank', d['sb'])], writes=[('PT', pt_)],
                      out=PT[pt_][0:kparts, 0:ncols], in_=banks[d['sb']][0:kparts, 0:ncols], func=AF.Exp)

                def PV(i=i, col0=col0, ncols=ncols, d=d):
                    s0 = col0 // 128
                    s1 = (col0 + ncols) // 128
                    for s in range(s0, s1):
                        E('pe', 'matmul', reads=[('PT', d['pt'])] + list(vkeys_fn(i)), writes=[('O', ob)], inc=(s == s1 - 1),
                          out=Obv[ob][:, s, 0:nv], lhsT=PT[d['pt']][0:kparts, (s - s0) * 128:(s - s0 + 1) * 128], rhs=v_fn(i),
                          start=(d['first'] and s == s0), stop=(i == last_fn(s)), skip_group_check=True)
                d['S'] = S
                d['EXP'] = EXP
                d['PV'] = PV
                if idx == n - 1:
                    d['post'].append(lambda ob=ob: post(ob))
                tiles.append(d)

        def run_pipeline(depth=3):
            n = len(tiles)
            for t_ in range(min(depth, n)):
                for f in tiles[t_]['pre']:
                    f()
                tiles[t_]['S']()
            for t_ in range(n):
                tiles[t_]['EXP']()
                if t_ + depth < n:
                    for f in tiles[t_ + depth]['pre']:
                        f()
                    tiles[t_ + depth]['S']()
                tiles[t_]['PV']()
                for f in tiles[t_]['post']:
                    f()
            del tiles[:]

        def epi_rec(ob):
            E('dve', 'tensor_scalar', reads=[('O', ob)], writes=['rec'], out=rec, in0=Obv[ob][:, :, 64],
              scalar1=1e-30, scalar2=None, op0=ALU.max)
            E('dve', 'reciprocal', reads=['rec'], writes=['rec'], out=rec, in_=rec)

        def nsa_post(br, h, g, hl, tg, first_branch):
            def f(ob):
                epi_rec(ob)
                E('dve', 'tensor_tensor', reads=['rec'] + [('gates', i) for i in range(4 * tg, 4 * tg + 4)], writes=['fsc'],
                  out=fsc, in0=rec, in1=gates[:, 4 * tg:4 * tg + 4, br * 8 + h], op=ALU.mult)
                dst = otm[:, 4 * tg:4 * tg + 4, h * 64:(h + 1) * 64]
                okey = ('otm', h, tg)
                if first_branch:
                    E('dve', 'tensor_tensor', reads=[('O', ob), 'fsc'], writes=[okey], out=dst, in0=Obv[ob][:, :, 0:64],
                      in1=fsc.unsqueeze(2).to_broadcast([128, 4, 64]), op=ALU.mult)
                else:
                    ti = tmpo_ctr[0] % 2
                    tmpo_ctr[0] += 1
                    tmpo = tmpos[ti]
                    E('dve', 'tensor_tensor', reads=[('O', ob), 'fsc'], writes=[('tmpo', ti)], out=tmpo, in0=Obv[ob][:, :, 0:64],
                      in1=fsc.unsqueeze(2).to_broadcast([128, 4, 64]), op=ALU.mult)
                    E('pool', 'tensor_tensor', reads=[('tmpo', ti), okey], writes=[okey], out=dst, in0=dst, in1=tmpo, op=ALU.add)
                if br == 0:
                    idst = impacc[:, 4 * tg:4 * tg + 4, g, :]
                    ikey = ('imp', g, tg)
                    if hl == 0:
                        E('dve', 'tensor_tensor', reads=[('O', ob), 'rec'], writes=[ikey], out=idst, in0=Obv[ob][:, :, 65:97],
                          in1=rec.unsqueeze(2).to_broadcast([128, 4, 32]), op=ALU.mult)
                    else:
                        E('dve', 'tensor_tensor', reads=[('O', ob), 'rec'], writes=['tmpi'], out=tmpi, in0=Obv[ob][:, :, 65:97],
                          in1=rec.unsqueeze(2).to_broadcast([128, 4, 32]), op=ALU.mult)
                        E('dve', 'tensor_tensor', reads=['tmpi', ikey], writes=[ikey], out=idst, in0=idst, in1=tmpi, op=ALU.add)
            return f

        def fox_post(h, tg):
            def f(ob):
                epi_rec(ob)
                E('dve', 'tensor_tensor', reads=[('O', ob), 'rec'], writes=[('otm', 8 + h, tg)],
                  out=otm[:, 4 * tg:4 * tg + 4, 512 + h * 64:512 + (h + 1) * 64], in0=Obv[ob][:, :, 0:64],
                  in1=rec.unsqueeze(2).to_broadcast([128, 4, 64]), op=ALU.mult)
            return f

        def causal_ktiles(tg):
            kt = []
            for i in range(4 * tg + 4):
                r = i - 4 * tg
                if r < 0:
                    kt.append((i, 0, 512, None))
                else:
                    kt.append((i, 128 * r, 512 - 128 * r, ('ge', 0)))
            return kt

        def window_ktiles(tg):
            kt = []
            for i in range(max(4 * tg - 4, 0), 4 * tg + 4):
                r = i - 4 * tg
                if r >= 0:
                    kt.append((i, 128 * r, 512 - 128 * r, ('ge', 0)))
                else:
                    nc_ = 128 * (r + 5)
                    kt.append((i, 0, nc_, ('lt', nc_ - 128)))
            return kt

        topkV = topkVC[:, 0:512].rearrange("p (a b) -> p a b", a=16)
        topkC = topkVC[:, 512:1024].rearrange("p (a b) -> p a b", a=16)
        selbv = selb[:, 64:64 + 1024].rearrange("p (g a b) -> p g a b", g=2, a=16)

        def topk_prep(g):
            allimp = [('imp', g, tg) for tg in range(4)]
            E('dve', 'tensor_tensor', reads=allimp + ['topkVC'], writes=[('score', g)], out=score, in0=impacc[:, :, g, :], in1=topkV, op=ALU.mult)
            E('dve', 'tensor_tensor', reads=[('score', g), 'topkVC'], writes=[('score', g)], out=score, in0=score, in1=topkC, op=ALU.add)

        def topk_tile(g, i):
            E('dve', 'max', reads=[('score', g)], writes=['mx8'], out=mx8, in_=score[:, i, :])
            E('dve', 'match_replace', reads=[('score', g), 'mx8'], writes=['sc2'], out=sc2, in_to_replace=mx8, in_values=score[:, i, :], imm_value=-1e30)
            E('dve', 'max', reads=['sc2'], writes=['mx8'], out=mx8, in_=sc2)
            E('dve', 'tensor_scalar', reads=[('score', g), 'mx8'], writes=[('selb', g, i)], out=selbv[:, g, i, :], in0=score[:, i, :],
              scalar1=mx8[:, 7:8], scalar2=NEG, op0=ALU.is_lt, op1=ALU.mult)

        def sel_rows_group(g):
            for tg in range(4):
                bi = proj_bank()
                ptb = banks[bi][:, :].bitcast(BF16)
                for s in range(4):
                    i = 4 * tg + s
                    off = (g * 16 + i) * 32
                    E('pe', 'transpose', reads=[('selb', g, i), 'identb'], writes=[('bank', bi)], inc=(s == 3),
                      out=ptb[0:96, s * 128:(s + 1) * 128], in_=selb[:, off:off + 96], identity=identb[:, :])
                for hl in range(4):
                    E('dve', 'tensor_copy', reads=[('bank', bi)], writes=[('qn', hl, 'sel', tg)],
                      out=qn[hl][64:96, tg * 512:(tg + 1) * 512], in_=ptb[64:96, 0:512])

        FQ = [fq, [qn[0], qn[1]]]
        FK = [fk, [qn[2], qn[3]]]
        FV = [fv, vsaug]

        def fox_setup_pieces(p):
            st = p % 2
            holder = {}
            pieces = []

            def acq():
                holder['w'] = W.acquire('FOX%d' % p)
            for tg in range(4):
                def pc(tg=tg):
                    if 'w' not in holder:
                        acq()
                    pos_f, slot_f, key_f = holder['w']
                    proj_fm_tg(slot_f, key_f, 0, evac_split(FQ[st], ('fq', st), scale=0.125), tg)
                    proj_fm_tg(slot_f, key_f, 128, evac_split(FK[st], ('fk', st)), tg)
                pieces.append(pc)
            for tg in range(4):
                def pv(tg=tg):
                    pos_f, slot_f, key_f = holder['w']
                    svf = slot_f.rearrange("p (k n) -> p k n", k=8)
                    for i in range(4 * tg, 4 * tg + 4):
                        bi = proj_bank()
                        for k in range(8):
                            E('pe', 'matmul', reads=[key_f], writes=[('bank', bi)], inc=(k == 7),
                              out=banks[bi][:, 0:128], lhsT=uT[:, k, i * 128:(i + 1) * 128], rhs=svf[:, k, 256:384],
                              start=(k == 0), stop=(k == 7))
                        E('dve', 'tensor_copy', reads=[('bank', bi)], writes=[(('fv', st), i)],
                          out=FV[st][:, i, :, 0:64], in_=banks[bi][:, 0:128].rearrange("p (g d) -> p g d", g=2))
                    if tg == 3:
                        W.release(pos_f)
                pieces.append(pv)
            for a in range(2):
                def pa(a=a):
                    h = 2 * p + a
                    for j in range(3):
                        P.dma('sp', FQ[st][a][64 + j:65 + j, :], a123[j][16 * h:16 * h + 16, :], reads=['a%d' % j], writes=[(('fq', st), a, 'aug')])
                        P.dma('sp', FK[st][a][67 + j:68 + j, :], a123[j][16 * h:16 * h + 16, :], reads=['a%d' % j], writes=[(('fk', st), a, 'aug')])
                pieces.append(pa)
            return pieces


        for g in range(2):
            def pre_group(g=g):
                if g == 0:
                    return
                for hl in range(4):
                    P.dma('sp', qn[hl][96:100, :], cd['c_alq'][g * 4 + hl], writes=[('qn', hl, 'alibi')])
                for c in range(2):
                    proj_fm(slot_qn, key_qn, (2 * g + c) * 128, evac_split([qn[2 * c], qn[2 * c + 1]], ('qnq', c), scale=0.125))
                if g == 1:
                    W.release(pos_qn)

            def qkeys_of(hl, tg, sel):
                ks_ = [(('qnq', hl // 2), hl % 2, 'q', tg), ('qn', hl, 'alibi'), ('qn', hl, 'sel')]
                ks_.append(('qn', hl, 'sel', tg))
                return ks_
            for hl in range(4):
                h = g * 4 + hl
                for tg in range(4):
                    add_unit([(0, 0, 512, ('cmp',))], (lambda i, g=g: kcmpaug[g][0:100, 0:128]), (lambda i, g=g: [('kcmp', g)]),
                             qn[hl], qkeys_of(hl, tg, False), 100, tg, (lambda i, g=g: vcmpaug[g][:, 0:97]), (lambda i, g=g: [('vcmp', g)]), 97,
                             (lambda s: 0), (lambda s: 0), nsa_post(0, h, g, hl, tg, True),
                             pre=(pre_group if (hl == 0 and tg == 0) else None))
            tiles[-1]['post'].append(lambda g=g: (topk_prep(g), topk_tile(g, 0)))
            win_unit_ends = []
            for hl in range(4):
                h = g * 4 + hl
                for tg in range(4):
                    add_unit(window_ktiles(tg), (lambda i, g=g: kwaug[g][0:100, i * 128:(i + 1) * 128]),
                             (lambda i, g=g: [('kw', g, 'aug'), ('kw', g, 'E')] + [('kw', g, 'q', i // 4)]),
                             qn[hl], qkeys_of(hl, tg, False), 100, tg, (lambda i, g=g: vwaug[:, i, g, :]), (lambda i: [('vwaug', i), 'vwaug']), 65,
                             (lambda s, tg=tg: max(4 * tg + s - 4, 0)), (lambda s, tg=tg: 4 * tg + s), nsa_post(2, h, g, hl, tg, False))
                    win_unit_ends.append(len(tiles) - 1)
            for ui in range(15):
                tiles[win_unit_ends[ui]]['post'].append(lambda g=g, ui=ui: topk_tile(g, ui + 1))
            for hl in range(4):
                h = g * 4 + hl
                for tg in range(4):
                    add_unit(causal_ktiles(tg), (lambda i, g=g: ksaug[g][0:100, i * 128:(i + 1) * 128]),
                             (lambda i, g=g: [('ks', g, 'aug'), ('ks', g, 'E')] + [('ks', g, 'q', i // 4)]),
                             qn[hl], qkeys_of(hl, tg, True), 100, tg, (lambda i, g=g: vsaug[:, i, g, :]), (lambda i: [('vsaug', i), 'vsaug']), 65,
                             (lambda s: 0), (lambda s, tg=tg: 4 * tg + s), nsa_post(1, h, g, hl, tg, False),
                             pre=((lambda g=g: sel_rows_group(g)) if (hl == 0 and tg == 0) else None))
            if g == 1:
                pcs0 = fox_setup_pieces(0)
                base0 = len(tiles) - 3 * len(pcs0) - 4
                for j, pc_ in enumerate(pcs0):
                    tiles[base0 + 3 * j]['pre'].append(pc_)
            run_pipeline()
            if g == 0:
                dump('qn0', qn[0][0:100, :], reads=[(('qnq', 0), 0, 'q', tg) for tg in range(4)] + [('qn', 0, 'alibi'), ('qn', 0, 'sel')] + [('qn', 0, 'sel', tg) for tg in range(4)])
            if stop_after == 'NSA0':
                break
        dump('impacc', impacc.rearrange("p a b c -> p (a b c)"), reads=[('imp', g, tg) for g in range(2) for tg in range(4)])
        dump('selb', selb, reads=[('selb', g, i) for g in range(2) for i in range(16)])
        if stop_after in ('NSA0', 'NSA'):
            P.barrier()
            dump('otm', otm.rearrange("p a b -> p (a b)"))
            P.barrier()
            P.replay()
            return nc

        P.barrier()
        for a_ in range(2):
            E('dve', 'memset', writes=[(('fq', 1), a_, 'aug')], ap=FQ[1][a_][64:70, :], constant=1.0)
            E('dve', 'memset', writes=[(('fk', 1), a_, 'aug')], ap=FK[1][a_][64:70, :], constant=-1.0)
        pair_first_tile = []
        for p in range(4):
            st = p % 2
            pair_first_tile.append(len(tiles))
            for a in range(2):
                h = 2 * p + a
                for tg in range(4):
                    add_unit(causal_ktiles(tg), (lambda i, a=a, st=st: FK[st][a][0:70, i * 128:(i + 1) * 128]),
                             (lambda i, a=a, st=st: [(('fk', st), a, 'q', i // 4), (('fk', st), a, 'aug')]),
                             FQ[st][a], [(('fq', st), a, 'q', tg), (('fq', st), a, 'aug')], 70, tg,
                             (lambda i, a=a, st=st: FV[st][:, i, a, :]), (lambda i, st=st: [(('fv', st), i)]), 65,
                             (lambda s: 0), (lambda s, tg=tg: 4 * tg + s), fox_post(h, tg))
        for p in range(4):
            if p == 0:
                continue
            pcs = fox_setup_pieces(p)
            if True:
                base = pair_first_tile[p - 1] + 3
                for j, pc_ in enumerate(pcs):
                    tiles[base + 3 * j]['pre'].append(pc_)
        run_pipeline()
        dump('fq0', fq[0][0:70, :])
        dump('fk0', fk[0][0:70, :])
        P.barrier()
        dump('otm', otm.rearrange("p a b -> p (a b)"))
        if stop_after in ('FOX0', 'FOX'):
            P.barrier()
            P.replay()
            return nc

        cv2 = Carver()
        oT = cv2.alloc([8, T], BF16)
        hT = cv2.alloc([8, T], F32)
        yT = cv2.alloc([8, 1024], BF16)
        xs2 = [cv2.alloc([1024], F32) for _ in range(2)]
        sA = cv2.alloc([512], F32)
        sB = cv2.alloc([512], F32)
        t1 = cv2.alloc([512], F32)
        t2 = cv2.alloc([512], F32)
        sq = [cv2.alloc([512], BF16) for _ in range(4)]
        onesb = cv2.alloc([128], BF16)
        rstdt = cv2.alloc([512], F32)
        for i in range(16):
            b = 5 + (i % 2)
            ptb = banks[b][:, :].bitcast(BF16)
            for c in range(8):
                E('pe', 'transpose', writes=[('bank', b)], inc=(c == 7),
                  out=ptb[:, c * 128:(c + 1) * 128], in_=otm[:, i, c * 128:(c + 1) * 128], identity=identb[:, :])
            if i % 2 == 0:
                E('dve', 'tensor_copy', reads=[('bank', b)], writes=[('oT', i)], out=oT[:, :, i * 128:(i + 1) * 128],
                  in_=ptb.rearrange("p (c t) -> p c t", c=8))
            else:
                E('act', 'copy', reads=[('bank', b)], writes=[('oT', i)], out=oT[:, :, i * 128:(i + 1) * 128],
                  in_=ptb.rearrange("p (c t) -> p c t", c=8))
        P.barrier()
        dump('oT', oT.rearrange("p a b -> p (a b)"))
        E('dve', 'memset', writes=['onesb'], ap=onesb, constant=1.0)

        def xT_tile(i):
            b = i % 2
            P.dma('sp', xs2[b], x_d[i * 128:(i + 1) * 128, :], writes=[('xs2', b)])
            for half in range(2):
                bi = nbank()
                for c in range(4):
                    cc = half * 4 + c
                    E('pe', 'transpose', reads=[('xs2', b), 'identf'], writes=[('bank', bi)], inc=(c == 3),
                      out=banks[bi][:, c * 128:(c + 1) * 128], in_=xs2[b][:, cc * 128:(cc + 1) * 128], identity=identf[:, 0:128])
                if half == 0:
                    E('act', 'copy', reads=[('bank', bi)], writes=[('hT', i, half)], out=hT[:, half * 4:half * 4 + 4, i * 128:(i + 1) * 128],
                      in_=banks[bi][:, :].rearrange("p (c t) -> p c t", c=4))
                else:
                    E('dve', 'tensor_copy', reads=[('bank', bi)], writes=[('hT', i, half)], out=hT[:, half * 4:half * 4 + 4, i * 128:(i + 1) * 128],
                      in_=banks[bi][:, :].rearrange("p (c t) -> p c t", c=4))

        bk = [0]

        def nbank():
            bk[0] = (bk[0] + 1) % 8
            return bk[0]

        def mm_acc(bi, slotv, wkey, kchunks, col0, rhs_fn, rkeys):
            n = len(kchunks)
            for j, k in enumerate(kchunks):
                E('pe', 'matmul', reads=[wkey] + list(rkeys), writes=[('bank', bi)], inc=(j == n - 1),
                  out=banks[bi][:, :], lhsT=slotv[:, k, col0:col0 + 128], rhs=rhs_fn(k), start=(j == 0), stop=(j == n - 1))

        def rms_stats(tg):
            bi = nbank()
            for k in range(8):
                E('act', 'activation', reads=[('h', k, tg)], writes=[('sq', k % 4)], out=sq[k % 4], in_=hT[:, k, tg * 512:(tg + 1) * 512], func=AF.Square)
                E('pe', 'matmul', reads=[('sq', k % 4), 'onesb'], writes=[('bank', bi)], inc=True,
                  out=banks[bi][:, :], lhsT=onesb, rhs=sq[k % 4], start=(k == 0), stop=(k == 7))
            E('dve', 'tensor_scalar', reads=[('bank', bi)], writes=['rstdt'], out=rstdt, in0=banks[bi][:, :], scalar1=1.0 / D, scalar2=1e-6,
              op0=ALU.mult, op1=ALU.add)
            E('act', 'activation', reads=['rstdt'], writes=['rstdt'], out=rstdt, in_=rstdt, func=AF.Ln)
            E('act', 'activation', reads=['rstdt'], writes=['rstdt'], out=rstdt, in_=rstdt, func=AF.Exp, scale=-0.5)

        vT = uT
        for half in range(2):
            wpos = {}
            for m in range(8):
                for nm in ('MG%d' % (m // 2), 'BR%d' % (m // 4)):
                    if nm not in wpos:
                        wpos[nm] = W.acquire(nm)
                pmg, smg, kmg = wpos['MG%d' % (m // 2)]
                pbr, sbr, kbr = wpos['BR%d' % (m // 4)]
                smgv = smg.rearrange("p (k n) -> p k n", k=8)
                sbrv = sbr.rearrange("p (k n) -> p k n", k=8)
                for tl in range(2):
                    tg = half * 2 + tl
                    tsl = slice(tg * 512, (tg + 1) * 512)
                    bA, bB, bC, bD = nbank(), nbank(), nbank(), nbank()
                    mm_acc(bA, smgv, kmg, range(8), (m % 2) * 128, (lambda k, tsl=tsl: uT[:, k, tsl]), [])
                    mm_acc(bB, smgv, kmg, range(8), 256 + (m % 2) * 128, (lambda k, tsl=tsl: uT[:, k, tsl]), [])
                    mm_acc(bC, sbrv, kbr, range(0, 4), (m % 4) * 128, (lambda k, tsl=tsl: oT[:, k, tsl]), [])
                    mm_acc(bD, sbrv, kbr, range(4, 8), (m % 4) * 128, (lambda k, tsl=tsl: oT[:, k, tsl]), [])
                    E('act', 'activation', reads=[('bank', bA)], writes=['sA'], out=sA, in_=banks[bA][:, :], func=AF.Sigmoid, bias=prm[:, 24 + m:25 + m])
                    E('act', 'activation', reads=[('bank', bB)], writes=['sB'], out=sB, in_=banks[bB][:, :], func=AF.Sigmoid, bias=prm[:, 32 + m:33 + m])
                    E('dve', 'tensor_tensor', reads=['sA', ('bank', bC)], writes=['t1'], out=t1, in0=sA, in1=banks[bC][:, :], op=ALU.mult)
                    E('dve', 'tensor_tensor', reads=['sB', ('bank', bD)], writes=['t2'], out=t2, in0=sB, in1=banks[bD][:, :], op=ALU.mult)
                    E('dve', 'tensor_tensor', reads=['t1', 't2'], writes=[('yT', m, tl)], out=yT[:, m, tl * 512:(tl + 1) * 512], in0=t1, in1=t2, op=ALU.add)
                if m % 2 == 1:
                    W.release(pmg)
                if m % 4 == 3:
                    W.release(pbr)
                if half == 0:
                    xT_tile(2 * m)
                    xT_tile(2 * m + 1)
            if half == 0:
                dump('yT', yT.rearrange("p a b -> p (a b)"), reads=[('yT', m, tl) for m in range(8) for tl in range(2)])
            for nm in ('OUT0', 'OUT1'):
                wpos[nm] = W.acquire(nm)
            for tl in range(2):
                tg = half * 2 + tl
                for m in range(8):
                    po, so, ko = wpos['OUT%d' % (m // 4)]
                    sov = so.rearrange("p (k n) -> p k n", k=8)
                    bi = nbank()
                    mm_acc(bi, sov, ko, range(8), (m % 4) * 128, (lambda k, tl=tl: yT[:, k, tl * 512:(tl + 1) * 512]),
                           [('yT', k, tl) for k in range(8)])
                    E('dve', 'tensor_tensor', reads=[('bank', bi)] + [('hT', it, m // 4) for it in range(4 * tg, 4 * tg + 4)], writes=[('h', m, tg)],
                      out=hT[:, m, tg * 512:(tg + 1) * 512], in0=hT[:, m, tg * 512:(tg + 1) * 512], in1=banks[bi][:, :], op=ALU.add)
                if tl == 1:
                    W.release(wpos['OUT0'][0])
                    W.release(wpos['OUT1'][0])
                rms_stats(tg)
                for k in range(8):
                    E('dve', 'scalar_tensor_tensor', reads=[('h', k, tg), 'rstdt', 'prm'], writes=[('vT', k, tg)], out=vT[:, k, tg * 512:(tg + 1) * 512],
                      in0=hT[:, k, tg * 512:(tg + 1) * 512], scalar=prm[:, 8 + k:9 + k], in1=rstdt, op0=ALU.mult, op1=ALU.mult)

        aT = [oT[:, 0:4, :], oT[:, 4:8, :]]

        def mlp_up(fb):
            pu, su, ku = W.acquire('UP%d' % fb)
            suv = su.rearrange("p (k n) -> p k n", k=8)
            for tg in range(4):
                for c in range(4):
                    bi = nbank()
                    mm_acc(bi, suv, ku, range(8), c * 128, (lambda k, tg=tg: vT[:, k, tg * 512:(tg + 1) * 512]), [('vT', k, tg) for k in range(8)])
                    rl = sA if (c * 4 + tg) % 2 == 0 else sB
                    rk = 'sA' if (c * 4 + tg) % 2 == 0 else 'sB'
                    E('act', 'activation', reads=[('bank', bi)], writes=[rk], out=rl, in_=banks[bi][:, :], func=AF.Relu)
                    E('dve', 'tensor_tensor', reads=[rk], writes=[('aT', fb % 2, c, tg)], out=aT[fb % 2][:, c, tg * 512:(tg + 1) * 512],
                      in0=rl, in1=rl, op=ALU.mult)
            W.release(pu)

        def mlp_down(fb, tgs=(0, 1, 2, 3), release=True):
            if ('down', fb) not in wheld:
                wheld[('down', fb)] = W.acquire('DOWN%d' % fb)
            pd, sd, kd = wheld[('down', fb)]
            sdv = sd.rearrange("p (k n) -> p k n", k=4)
            for tg in tgs:
                for m in range(8):
                    bi = nbank()
                    for k in range(4):
                        E('pe', 'matmul', reads=[kd, ('aT', fb % 2, k, tg)], writes=[('bank', bi)], inc=(k == 3),
                          out=banks[bi][:, :], lhsT=sdv[:, k, m * 128:(m + 1) * 128], rhs=aT[fb % 2][:, k, tg * 512:(tg + 1) * 512],
                          start=(k == 0), stop=(k == 3))
                    E('dve', 'tensor_tensor', reads=[('bank', bi)], writes=[('h', m, tg)], out=hT[:, m, tg * 512:(tg + 1) * 512],
                      in0=hT[:, m, tg * 512:(tg + 1) * 512], in1=banks[bi][:, :], op=ALU.add)
            if release:
                W.release(pd)

        wheld = {}
        nrm_t = RG[:, (32768 + 65536) // 2:(32768 + 65536 + 16384) // 2].bitcast(F32).rearrange("p (a b) -> p a b", a=8)

        gfin_s = RG[:, (32768 + 65536) // 2:(32768 + 65536 + 4096) // 2].bitcast(F32)
        ssf = RG[:, (32768 + 65536 + 4096) // 2:(32768 + 65536 + 4096 + 128) // 2].bitcast(F32)
        rsf = RG[:, (32768 + 65536 + 4096 + 128) // 2:(32768 + 65536 + 4096 + 192) // 2].bitcast(F32)

        def final_tile(i):
            tg = i // 4
            ob_ = xs2[i % 2]
            bb = [nbank(), nbank()]
            for half in range(2):
                bi = bb[half]
                for c in range(4):
                    k = half * 4 + c
                    E('pe', 'transpose', reads=[('h', k, tg), 'identf'], writes=[('bank', bi)], inc=(c == 3),
                      out=banks[bi][:, c * 128:(c + 1) * 128], in_=hT[:, k, i * 128:(i + 1) * 128], identity=identf[:, 0:128])
                jk = sA if half == 0 else sB
                E('act', 'activation', reads=[('bank', bi), 'ssf'], writes=['sA' if half == 0 else 'sB', ('ssf', i, half)],
                  out=jk, in_=banks[bi][:, :], func=AF.Square, accum_out=ssf[:, 2 * i + half:2 * i + half + 1])
            E('dve', 'tensor_tensor', reads=[('ssf', i, 0), ('ssf', i, 1)], writes=[('rsf', i)], out=rsf[:, i:i + 1],
              in0=ssf[:, 2 * i:2 * i + 1], in1=ssf[:, 2 * i + 1:2 * i + 2], op=ALU.add)
            E('dve', 'tensor_scalar', reads=[('rsf', i)], writes=[('rsf', i)], out=rsf[:, i:i + 1], in0=rsf[:, i:i + 1],
              scalar1=1.0 / D, scalar2=1e-6, op0=ALU.mult, op1=ALU.add)
            E('act', 'activation', reads=[('rsf', i)], writes=[('rsf', i)], out=rsf[:, i:i + 1], in_=rsf[:, i:i + 1], func=AF.Ln)
            E('act', 'activation', reads=[('rsf', i)], writes=[('rsf', i)], out=rsf[:, i:i + 1], in_=rsf[:, i:i + 1], func=AF.Exp, scale=-0.5)
            for half in range(2):
                E('dve', 'scalar_tensor_tensor', reads=[('bank', bb[half]), ('rsf', i), 'gfin', ('ostage_rd', i % 2)], writes=[('ostage', i % 2, half)],
                  out=ob_[:, half * 512:(half + 1) * 512], in0=banks[bb[half]][:, :], scalar=rsf[:, i:i + 1],
                  in1=gfin_s[:, half * 512:(half + 1) * 512], op0=ALU.mult, op1=ALU.mult)
            P.dma('sp', out_d[i * 128:(i + 1) * 128, :], ob_, reads=[('ostage', i % 2, 0), ('ostage', i % 2, 1)], writes=[('ostage_rd', i % 2)])

        def final_tg(tg):
            for i in range(4 * tg, 4 * tg + 4):
                final_tile(i)

        P.dma('sp', gfin_s, gfin_d, writes=['gfin'] + [('yT', k_, tl_) for k_ in range(8) for tl_ in range(2)])
        E('dve', 'memset', reads=['gfin'], writes=['ssf'] + [('yT', k_, tl_) for k_ in range(8) for tl_ in range(2)], ap=ssf, constant=0.0)
        mlp_up(0)
        for fb in range(7):
            mlp_up(fb + 1)
            mlp_down(fb)
        mlp_down(7, tgs=(0,), release=False)
        mlp_down(7, tgs=(1,), release=False)
        final_tg(0)
        mlp_down(7, tgs=(2,), release=False)
        final_tg(1)
        mlp_down(7, tgs=(3,), release=True)
        final_tg(2)
        final_tg(3)
        P.barrier()
        P.replay()
    return nc


_CACHE = {}


def kernel(**inputs):
    import ml_dtypes
    wall, prm, gfin = host_weights(inputs)
    C = host_consts()
    base = {"wall": wall, "prm": prm, "gfin": gfin}
    for k, v in C.items():
        base[k] = v.astype(np.float32) if k in ('c_topk', 'c_identf') else v.astype(ml_dtypes.bfloat16)
    x = np.asarray(inputs['x'], np.float32)
    in_maps = []
    for b in range(8):
        m = dict(base)
        m['x'] = np.ascontiguousarray(x[b])
        in_maps.append(m)
    if 'nc' not in _CACHE:
        _CACHE['nc'] = build_program()
    res = run_bass_kernel_spmd(_CACHE['nc'], in_maps, core_ids=list(range(8)))
    return np.stack([np.asarray(r['out'], np.float32) for r in res.results], 0)
```

```python
import numpy as np
from contextlib import ExitStack
import concourse.bass as bass
import concourse.mybir as mybir
from concourse.bass_utils import run_bass_kernel_spmd

F32 = mybir.dt.float32
BF16 = mybir.dt.bfloat16
AF = mybir.ActivationFunctionType
ALU = mybir.AluOpType

ENGS = ['pe', 'act', 'dve', 'pool', 'sp']
NDS = 40
NDS_SW = 8
NEG = -30000.0
T = 2048
D = 1024
NSLOT = 4


class Prog:
    def __init__(self, nc, ctx):
        self.nc = nc
        self.ops = {e: [] for e in ENGS}
        self.esem = {e: ctx.enter_context(nc.semaphore("es_" + e)) for e in ENGS}
        self.ecnt = {e: 0 for e in ENGS}
        self.dsems = [ctx.enter_context(nc.semaphore("ds%d" % i)) for i in range(NDS)]
        self.dcnt = [0] * NDS
        self.dnext = 0
        self.dnext_sw = 0
        self.seen = {e: {} for e in ENGS}
        self.lastw = {}
        self.reads = {}
        self.pending = {e: False for e in ENGS}

    def _deps(self, eng, reads, writes, extra=()):
        deps = list(extra)
        for k in reads:
            t = self.lastw.get(k)
            if t is not None:
                deps.append(t)
        for k in writes:
            t = self.lastw.get(k)
            if t is not None:
                deps.append(t)
            deps.extend(self.reads.get(k, ()))
        waits = {}
        for (s, v) in deps:
            if s == ('e', eng) and eng == 'pe':
                continue
            if self.seen[eng].get(s, 0) >= v:
                continue
            if waits.get(s, 0) < v:
                waits[s] = v
        for s, v in waits.items():
            self.seen[eng][s] = v
        return list(waits.items())

    def _commit(self, token, reads, writes):
        for k in writes:
            self.lastw[k] = token
            self.reads[k] = []
        for k in reads:
            self.reads.setdefault(k, []).append(token)

    def emit(self, eng, fn, reads=(), writes=(), inc=True):
        waits = self._deps(eng, reads, writes)
        for (s, v) in waits:
            if s == ('e', eng):
                assert v <= self.ecnt[eng], "self-wait on un-inc'd instruction"
        if inc:
            self.ecnt[eng] += 1
            token = (('e', eng), self.ecnt[eng])
            self.ops[eng].append((waits, fn, ('e', eng)))
            self.pending[eng] = False
        else:
            token = (('e', eng), self.ecnt[eng] + 1)
            self.ops[eng].append((waits, fn, None))
            self.pending[eng] = True
        self._commit(token, reads, writes)
        return token

    def dma(self, eng, out, in_, reads=(), writes=(), **kw):
        if eng == 'pool':
            i = self.dnext_sw
            self.dnext_sw = (self.dnext_sw + 1) % NDS_SW
        else:
            i = NDS_SW + self.dnext
            self.dnext = (self.dnext + 1) % (NDS - NDS_SW)
        extra = []
        if self.dcnt[i] > 0:
            extra.append((('d', i), self.dcnt[i]))
        waits = self._deps(eng, reads, writes, extra)
        self.dcnt[i] += 16
        token = (('d', i), self.dcnt[i])
        fn = (lambda e, out=out, in_=in_, kw=kw: e.dma_start(out=out, in_=in_, **kw))
        self.ops[eng].append((waits, fn, ('d', i)))
        self._commit(token, reads, writes)
        return token

    def barrier(self):
        for e in ENGS:
            assert not self.pending[e], e
        allw = []
        for e in ENGS:
            if self.ecnt[e] > 0:
                allw.append((('e', e), self.ecnt[e]))
        for i in range(NDS):
            if self.dcnt[i] > 0:
                allw.append((('d', i), self.dcnt[i]))
        for e in ENGS:
            waits = [(s, v) for (s, v) in allw if self.seen[e].get(s, 0) < v]
            for s, v in waits:
                self.seen[e][s] = v
            self.ops[e].append((waits, None, None))
        self.lastw = {}
        self.reads = {}

    def _sem(self, s):
        return self.esem[s[1]] if s[0] == 'e' else self.dsems[s[1]]

    def replay(self):
        nc = self.nc
        hmap = {'pe': 'tensor', 'act': 'scalar', 'dve': 'vector', 'pool': 'gpsimd', 'sp': 'sync'}
        with nc.Block() as block:
            for e in ENGS:
                ops = self.ops[e]

                def body(engine, ops=ops):
                    for (waits, fn, inc) in ops:
                        for (s, v) in waits:
                            engine.wait_ge(self._sem(s), v)
                        if fn is None:
                            continue
                        ins = fn(engine)
                        if inc is not None:
                            ins.then_inc(self._sem(inc), 1 if inc[0] == 'e' else 16)
                getattr(block, hmap[e])(body)


BLK = {}
_names = (['W1K', 'W1V', 'KV', 'TM', 'QN'] + ['FOX%d' % p for p in range(4)] + ['MG%d' % j for j in range(4)]
          + ['BR0', 'BR1', 'OUT0', 'OUT1'] + ['UP%d' % f for f in range(8)] + ['DOWN%d' % f for f in range(8)])
for _i, _n in enumerate(_names):
    BLK[_n] = _i
NBLK = len(_names)

STREAM = ['KV', 'TM', 'QN', 'W1K', 'W1V', 'FOX0', 'FOX1', 'FOX2', 'FOX3']
for _half in range(2):
    STREAM += ['MG0', 'BR0', 'MG1', 'MG2', 'BR1', 'MG3', 'OUT0', 'OUT1']
STREAM += ['UP0']
for _f in range(8):
    if _f + 1 < 8:
        STREAM.append('UP%d' % (_f + 1))
    STREAM.append('DOWN%d' % _f)


def _std_block(W):
    W = np.asarray(W, np.float32)
    if W.shape[1] < 512:
        Wp = np.zeros((W.shape[0], 512), np.float32)
        Wp[:, :W.shape[1]] = W
        W = Wp
    return W.reshape(8, 128, 512).transpose(1, 0, 2).reshape(128, 4096)


def _w1_block(w1):
    a = np.asarray(w1, np.float32).reshape(32, 64, 128).transpose(1, 0, 2)
    return np.concatenate([a, a], 0).reshape(128, 4096)


def host_weights(inp):
    w_in = np.asarray(inp['w_in'][0], np.float32)
    q_n, kc, vc, ks, vs, kw, vw = (w_in[:, 0:512], w_in[:, 512:640], w_in[:, 640:768], w_in[:, 768:896],
                                  w_in[:, 896:1024], w_in[:, 1024:1152], w_in[:, 1152:1280])
    gate, q_f, k_f, v_f, f_l = (w_in[:, 1280:1304], w_in[:, 1304:1816], w_in[:, 1816:2328], w_in[:, 2328:2840],
                                w_in[:, 2840:2848])
    wall = np.zeros((NBLK, 128, 4096), np.float32)
    wall[BLK['W1K']] = _w1_block(inp['cmp_w1_k'][0])
    wall[BLK['W1V']] = _w1_block(inp['cmp_w1_v'][0])
    wall[BLK['KV']] = _std_block(np.concatenate([kc, vc, ks, kw], 1))
    wall[BLK['TM']] = _std_block(np.concatenate([vs, vw, gate, f_l], 1))
    wall[BLK['QN']] = _std_block(q_n)
    for p in range(4):
        sl = slice(128 * p, 128 * p + 128)
        wall[BLK['FOX%d' % p]] = _std_block(np.concatenate([q_f[:, sl], k_f[:, sl], v_f[:, sl]], 1))
    w_mg = np.asarray(inp['w_merge_gate'][0], np.float32)
    for j in range(4):
        cols = [w_mg[:, 128 * (2 * j):128 * (2 * j) + 128], w_mg[:, 128 * (2 * j + 1):128 * (2 * j + 1) + 128],
                w_mg[:, 1024 + 128 * (2 * j):1024 + 128 * (2 * j) + 128],
                w_mg[:, 1024 + 128 * (2 * j + 1):1024 + 128 * (2 * j + 1) + 128]]
        wall[BLK['MG%d' % j]] = _std_block(np.concatenate(cols, 1))
    w_br = np.concatenate([np.asarray(inp['w_branch_nsa'][0], np.float32), np.asarray(inp['w_branch_fox'][0], np.float32)], 0)
    w_out = np.asarray(inp['w_out'][0], np.float32)
    for j in range(2):
        wall[BLK['BR%d' % j]] = _std_block(w_br[:, 512 * j:512 * j + 512])
        wall[BLK['OUT%d' % j]] = _std_block(w_out[:, 512 * j:512 * j + 512])
    w_up = np.asarray(inp['w_up'][0], np.float32)
    w_down = np.asarray(inp['w_down'][0], np.float32)
    for f in range(8):
        wall[BLK['UP%d' % f]] = _std_block(w_up[:, 512 * f:512 * f + 512])
        wall[BLK['DOWN%d' % f]] = w_down[512 * f:512 * f + 512, :].reshape(4, 128, 1024).transpose(1, 0, 2).reshape(128, 4096)
    gfin = np.ascontiguousarray(np.broadcast_to(np.asarray(inp['norm_final'], np.float32)[None, :], (128, 1024)))
    prm = np.zeros((128, 240), np.float32)
    prm[:, 0:8] = np.asarray(inp['norm_mix'][0], np.float32).reshape(8, 128).T
    prm[:, 8:16] = np.asarray(inp['norm_mlp'][0], np.float32).reshape(8, 128).T
    prm[:, 16:24] = np.asarray(inp['norm_final'], np.float32).reshape(8, 128).T
    prm[:, 24:40] = np.asarray(inp['b_merge_gate'][0], np.float32).reshape(16, 128).T
    prm[:, 40:48] = np.asarray(inp['fox_f_bias'][0], np.float32)[None, :]
    prm[0:64, 48:80] = np.asarray(inp['cmp_pos_k'][0], np.float32).T
    prm[0:64, 80:112] = np.asarray(inp['cmp_pos_v'][0], np.float32).T
    prm[:, 112:176] = np.asarray(inp['cmp_w2_k'][0], np.float32)
    prm[:, 176:240] = np.asarray(inp['cmp_w2_v'][0], np.float32)
    return wall, prm, gfin


def host_consts():
    C = {}
    p = np.arange(128)
    t = np.arange(T)
    hi = (128 * (t // 128)).astype(np.float32)
    lo = (t % 128).astype(np.float32)
    slopes = (2.0 ** (-8.0 * np.arange(1, 9) / 8)).astype(np.float32)
    C['c_ident'] = np.eye(128, dtype=np.float32)
    idf = np.zeros((128, 512), np.float32)
    idf[:, 0:128] = np.eye(128, dtype=np.float32)
    idf[:, 128:256] = 1.0
    idf[:, 256:384] = (p[:, None] <= p[None, :])
    idf[:, 384:512] = ((p[:, None] // 16 == p[None, :] // 16) & (p[:, None] % 16 < p[None, :] % 16))
    C['c_identf'] = idf
    tri = np.zeros((128, 640), np.float32)
    tri[:, 0:128] = np.where(p[:, None] <= p[None, :], 0.0, NEG)
    tri[:, 128:256] = np.where(p[None, :] < p[:, None], 0.0, NEG)
    tri[:, 256:384] = (p[:, None] <= p[None, :]).astype(np.float32)
    tri[:, 384:512] = (p[:, None] <= p[None, :])
    tri[:, 512:640] = (p[None, :] < p[:, None])
    C['c_tri'] = tri
    n = np.arange(128)
    ce = 16 * n + 31
    cm = np.where((t[None, :] >= ce[:, None]) & (n[:, None] < 127), 0.0, NEG).astype(np.float32)
    C['c_cmpmask'] = cm
    tt = (np.arange(16)[None, :] * 128 + p[:, None])
    jj = np.arange(32)
    cur = tt // 64
    forced = (jj[None, None, :] == 0) | (jj[None, None, :] == cur[:, :, None]) | (jj[None, None, :] == cur[:, :, None] - 1)
    valid = (jj[None, None, :] * 64) <= tt[:, :, None]
    V = (valid & ~forced).astype(np.float32)
    Cc = np.where(forced, 100.0 + jj[None, None, :], np.where(valid, 0.0, -1.0 - 0.01 * jj[None, None, :])).astype(np.float32)
    C['c_topk'] = np.concatenate([V.reshape(128, 512), Cc.reshape(128, 512)], 1)
    cs = 16 * n
    ov = ((cs[:, None] <= jj[None, :] * 64 + 63) & (ce[:, None] >= jj[None, :] * 64) & (n[:, None] < 127)).astype(np.float32)
    C['c_overlap'] = ov
    alq = np.zeros((8, 4, T), np.float32)
    for h in range(8):
        alq[h, 0] = -slopes[h] * hi
        alq[h, 1] = -slopes[h] * lo
        alq[h, 2] = slopes[h]
        alq[h, 3] = slopes[h]
    C['c_alq'] = alq
    alk = np.zeros((4, T), np.float32)
    alk[0] = 1.0
    alk[1] = 1.0
    alk[2] = hi
    alk[3] = lo
    C['c_alk'] = alk
    alkc = np.zeros((4, 128), np.float32)
    alkc[0, :127] = 1.0
    alkc[1, :127] = 1.0
    alkc[2, :127] = (128 * (ce[:127] // 128))
    alkc[3, :127] = (ce[:127] % 128)
    C['c_alkc'] = alkc
    E = np.zeros((32, T), np.float32)
    E[t // 64, t] = 1.0
    C['c_E'] = E
    return C


CONST_SHAPES = {'c_ident': [128, 128], 'c_tri': [128, 640], 'c_cmpmask': [128, T], 'c_topk': [128, 1024],
                'c_overlap': [128, 32], 'c_identf': [128, 512], 'c_alq': [8, 4, T], 'c_alk': [4, T], 'c_alkc': [4, 128], 'c_E': [32, T]}


def build_program(debug=(), stop_after=None):
    nc = bass.Bass("TRN2", target_bir_lowering=False)
    x_d = nc.dram_tensor("x", [T, D], F32, kind="ExternalInput").ap()
    wall_d = nc.dram_tensor("wall", [NBLK, 128, 4096], F32, kind="ExternalInput").ap()
    prm_d = nc.dram_tensor("prm", [128, 240], F32, kind="ExternalInput").ap()
    gfin_d = nc.dram_tensor("gfin", [128, 1024], F32, kind="ExternalInput").ap()
    cd = {}
    for name, shp in CONST_SHAPES.items():
        dt_ = F32 if name in ('c_topk', 'c_identf') else BF16
        cd[name] = nc.dram_tensor(name, shp, dt_, kind="ExternalInput").ap()
    out_d = nc.dram_tensor("out", [T, D], F32, kind="ExternalOutput").ap()
    DBG_SHAPES = {'uT': ([128, 8 * T], BF16), 'kcT': ([128, T], BF16), 'ksaug0': ([128, T], BF16), 'gates': ([128, 16 * 24], F32),
                  'vsaug': ([128, 16 * 2 * 65], BF16), 'kcmpaug0': ([128, 128], BF16), 'vcmpaug0': ([128, 104], BF16),
                  'qn0': ([100, T], BF16), 'otm': ([128, 16 * 1024], BF16), 'impacc': ([128, 16 * 2 * 32], F32),
                  'selb': ([128, 64 + 1024], BF16), 'cneg': ([128, 128], F32), 'fq0': ([70, T], BF16), 'fk0': ([70, T], BF16),
                  'oT': ([128, 8 * T], BF16), 'hT': ([128, 8 * T], F32), 'yT': ([128, 8 * 1024], BF16), 'vT': ([128, 8 * T], BF16)}
    dbg_d = {}
    for name in debug:
        shp, dt_ = DBG_SHAPES[name]
        dbg_d[name] = nc.dram_tensor("dbg_" + name, shp, dt_, kind="ExternalOutput").ap()

    with ExitStack() as ctx:
        P = Prog(nc, ctx)

        def E(eng, method, reads=(), writes=(), inc=True, **kw):
            return P.emit(eng, (lambda e, method=method, kw=kw: getattr(e, method)(**kw)), reads, writes, inc)

        def sbt(name, shape, dt_):
            return ctx.enter_context(nc.sbuf_tensor(name, shape, dt_))

        def dump(name, ap2d, reads=()):
            if name in dbg_d:
                P.dma('sp', dbg_d[name], ap2d, reads=reads)

        uT_t = sbt("uT", [128, 8 * T], BF16)
        uT = uT_t[:, :].rearrange("p (c t) -> p c t", c=8)
        arena = sbt("arena", [128, NSLOT * 4096], BF16)
        identb = sbt("identb", [128, 128], BF16)
        identf = sbt("identf", [128, 512], F32)
        trib = sbt("trib", [128, 640], BF16)
        prm = sbt("prm_s", [128, 240], F32)
        RG_BYTES = 142336
        RG = sbt("RG", [128, RG_BYTES // 2], BF16)
        banks = [ctx.enter_context(nc.psum_tensor("bank%d" % i, [128, 512], F32)) for i in range(8)]

        class Carver:
            def __init__(self):
                self.off = 0

            def alloc(self, dims, dt_):
                n = 1
                for d_ in dims:
                    n *= d_
                nb = n * (4 if dt_ == F32 else 2)
                nb = (nb + 63) // 64 * 64
                assert self.off + nb <= RG_BYTES, (self.off, nb)
                ap = RG[:, self.off // 2:(self.off + nb) // 2]
                self.off += nb
                if dt_ == F32:
                    ap = ap.bitcast(F32)
                ap = ap[:, 0:n]
                if len(dims) == 2:
                    ap = ap.rearrange("p (a b) -> p a b", a=dims[0])
                elif len(dims) == 3:
                    ap = ap.rearrange("p (a b c) -> p a b c", a=dims[0], b=dims[1])
                return ap

        class WStream:
            def __init__(self):
                self.loaded = 0
                self.released = set()

            def slot_ap(self, pos):
                s = pos % NSLOT
                return arena[:, s * 4096:(s + 1) * 4096]

            def pump(self):
                while self.loaded < len(STREAM) and (self.loaded < NSLOT or (self.loaded - NSLOT) in self.released):
                    pos = self.loaded
                    P.dma('pool', self.slot_ap(pos), wall_d[BLK[STREAM[pos]]], writes=[('w', pos % NSLOT)])
                    self.loaded += 1

            def acquire(self, name):
                pos = 0
                while pos in self.released or STREAM[pos] != name:
                    pos += 1
                assert pos < self.loaded, (name, pos, self.loaded)
                return pos, self.slot_ap(pos), ('w', pos % NSLOT)

            def release(self, pos):
                self.released.add(pos)
                self.pump()

        W = WStream()

        P.dma('sp', identb[:, :], cd['c_ident'], writes=['identb'])
        P.dma('sp', prm[:, :], prm_d, writes=['prm'])

        cv = Carver()
        ksaug = [cv.alloc([T], BF16) for _ in range(2)]
        kwaug = [cv.alloc([T], BF16) for _ in range(2)]
        qn = [cv.alloc([T], BF16) for _ in range(4)]
        fq = [cv.alloc([T], BF16) for _ in range(2)]
        fk = [cv.alloc([T], BF16) for _ in range(2)]
        fv = cv.alloc([16, 2, 65], BF16)
        vsaug = cv.alloc([16, 2, 65], BF16)
        vwaug = cv.alloc([16, 2, 65], BF16)
        gates = cv.alloc([16, 24], F32)
        spx = cv.alloc([128], F32)
        cneg = cv.alloc([128], F32)
        tsum = cv.alloc([128], F32)
        cref = cv.alloc([128], F32)
        r1 = cv.alloc([128], F32)
        a123 = [cv.alloc([128], BF16) for _ in range(3)]
        otm_off = cv.off
        otm = cv.alloc([16, 1024], BF16)
        selb = cv.alloc([64 + 1024], BF16)
        impacc = cv.alloc([16, 2, 32], F32)
        kcmpaug = [cv.alloc([128], BF16) for _ in range(2)]
        vcmpaug = [cv.alloc([104], BF16) for _ in range(2)]
        NPT = 6
        PT = [cv.alloc([512], BF16) for _ in range(NPT)]
        cmpmask = cv.alloc([T], BF16)
        topkVC = cv.alloc([1024], F32)
        score = cv.alloc([16, 32], F32)
        sc2 = cv.alloc([32], F32)
        mx8 = cv.alloc([8], F32)
        rec = cv.alloc([4], F32)
        fsc = cv.alloc([4], F32)
        tmpos = [cv.alloc([4, 64], F32) for _ in range(2)]
        tmpo_ctr = [0]
        tmpi = cv.alloc([4, 32], F32)
        w2b = cv.alloc([128], BF16)
        posb = cv.alloc([64], BF16)
        cbias = cv.alloc([2], F32)
        cbias2 = cv.alloc([2], F32)
        gxs = [[cv.alloc([128], F32) for _ in range(4)] for _ in range(2)]
        ghb4 = [cv.alloc([128], BF16) for _ in range(4)]
        kcr = cv.alloc([16, 128], BF16)
        vcr = cv.alloc([16, 128], BF16)
        kcT = fq[0]
        vcT = fq[1]
        otm_flat = RG[:, otm_off // 2:otm_off // 2 + 16384]
        NXS = 4
        xs = [otm_flat[:, 2048 * j:2048 * (j + 1)].bitcast(F32) for j in range(NXS)]
        xn = [otm_flat[:, 8192 + 1024 * j:8192 + 1024 * (j + 1)] for j in range(NXS)]
        ss = otm_flat[:, 12288:12320].bitcast(F32)
        rstd = otm_flat[:, 12320:12352].bitcast(F32)

        for i0 in range(NXS):
            P.dma('sp', xs[i0], x_d[i0 * 128:(i0 + 1) * 128, :], writes=[('xs', i0)])
        P.lastw[('w', 0)] = P.lastw[('xs', 1)]
        W.pump()
        P.dma('sp', identf[:, :], cd['c_identf'], writes=['identf'])
        P.dma('sp', trib[:, :], cd['c_tri'], writes=['trib'])
        P.dma('sp', cmpmask, cd['c_cmpmask'], writes=['cmpmask'])
        P.dma('sp', topkVC, cd['c_topk'], writes=['topkVC'])
        init_ms = []

        def ms(key, ap, val):
            init_ms.append(lambda: E('dve', 'memset', writes=[key], ap=ap, constant=val))
        ms('vsaug', vsaug, 1.0)
        ms('vwaug', vwaug, 1.0)
        for g in range(2):
            for (t_, nm) in ((ksaug[g], 'ks'), (kwaug[g], 'kw')):
                P.dma('sp', t_[96:100, :], cd['c_alk'], writes=[(nm, g, 'aug')])
            P.dma('sp', ksaug[g][64:96, :], cd['c_E'], writes=[('ks', g, 'E')])
            ms(('kw', g, 'E'), kwaug[g][64:96, :], 0.0)
            ms(('kcmp', g), kcmpaug[g][:, :], 0.0)
            ms(('vcmp', g), vcmpaug[g][:, :], 0.0)
        for hl in range(4):
            ms(('qn', hl, 'sel'), qn[hl][64:96, :], 0.0)
        ms('fv', fv, 1.0)
        ms('selb', selb, 0.0)
        E('dve', 'tensor_copy', reads=['prm'], writes=['w2b'], out=w2b, in_=prm[:, 112:240])
        E('dve', 'tensor_copy', reads=['prm'], writes=['posb'], out=posb[0:64, :], in_=prm[0:64, 48:112])

        pj = [0]

        def proj_bank():
            pj[0] += 1
            return 6 + (pj[0] % 2)

        def proj_fm_tg(slot, wkey, col0, evac, tg, ukeys=()):
            sv = slot.rearrange("p (k n) -> p k n", k=8)
            bi = proj_bank()
            for k in range(8):
                E('pe', 'matmul', reads=[wkey] + list(ukeys), writes=[('bank', bi)], inc=(k == 7),
                  out=banks[bi][:, :], lhsT=sv[:, k, col0:col0 + 128], rhs=uT[:, k, tg * 512:(tg + 1) * 512],
                  start=(k == 0), stop=(k == 7))
            evac(tg, bi)

        def proj_fm(slot, wkey, col0, evac):
            for tg in range(4):
                proj_fm_tg(slot, wkey, col0, evac, tg)

        def evac_plain(dst, key):
            def f(tg, bi):
                E('dve', 'tensor_copy', reads=[('bank', bi)], writes=[(key, tg)], out=dst[:, tg * 512:(tg + 1) * 512], in_=banks[bi][:, :])
            return f

        def evac_split(dsts, key, scale=None):
            def f(tg, bi):
                for a in range(2):
                    if scale is None:
                        E('dve', 'tensor_copy', reads=[('bank', bi)], writes=[(key, a, 'q', tg)],
                          out=dsts[a][0:64, tg * 512:(tg + 1) * 512], in_=banks[bi][a * 64:(a + 1) * 64, :])
                    else:
                        E('dve', 'tensor_scalar', reads=[('bank', bi)], writes=[(key, a, 'q', tg)],
                          out=dsts[a][0:64, tg * 512:(tg + 1) * 512], in0=banks[bi][a * 64:(a + 1) * 64, :],
                          scalar1=scale, scalar2=None, op0=ALU.mult)
            return f

        pos_kv, slot_kv, key_kv = W.acquire('KV')
        pos_tm, slot_tm, key_tm = W.acquire('TM')
        pos_qn, slot_qn, key_qn = W.acquire('QN')
        pos_w1k, slot_w1k, key_w1k = W.acquire('W1K')
        svtm = slot_tm.rearrange("p (k n) -> p k n", k=8)

        E('dve', 'memset', writes=['ss'], ap=ss, constant=0.0)

        def stageA(i):
            b = i % NXS
            if i >= NXS:
                P.dma('sp', xs[b], x_d[i * 128:(i + 1) * 128, :], writes=[('xs', b)])
            E('act', 'activation', reads=[('xs', b), 'ss'], writes=[('xn', b), ('ss', i)],
              out=xn[b], in_=xs[b], func=AF.Square, accum_out=ss[:, i:i + 1])
            E('dve', 'tensor_scalar', reads=[('ss', i)], writes=[('rstd', i)], out=rstd[:, i:i + 1], in0=ss[:, i:i + 1],
              scalar1=1.0 / D, scalar2=1e-6, op0=ALU.mult, op1=ALU.add)

        def stageB(i):
            b = i % NXS
            tb = 3 + (i % 3)
            E('act', 'activation', reads=[('rstd', i)], writes=[('rstd', i)], out=rstd[:, i:i + 1], in_=rstd[:, i:i + 1], func=AF.Ln)
            E('act', 'activation', reads=[('rstd', i)], writes=[('rstd', i)], out=rstd[:, i:i + 1], in_=rstd[:, i:i + 1], func=AF.Exp, scale=-0.5)
            E('act', 'activation', reads=[('xs', b), ('rstd', i)], writes=[('xn', b)],
              out=xn[b], in_=xs[b], func=AF.Copy, scale=rstd[:, i:i + 1])
            ptb = banks[tb][:, :].bitcast(BF16)
            for c in range(8):
                E('pe', 'transpose', reads=[('xn', b), 'identb'], writes=[('bank', tb)], inc=(c == 7),
                  out=ptb[:, c * 128:(c + 1) * 128], in_=xn[b][:, c * 128:(c + 1) * 128], identity=identb[:, :])
            E('dve', 'tensor_tensor', reads=[('bank', tb), 'prm'], writes=[('uT', i)],
              out=uT[:, :, i * 128:(i + 1) * 128], in0=ptb.rearrange("p (c t) -> p c t", c=8),
              in1=prm[:, 0:8].unsqueeze(2).to_broadcast([128, 8, 128]), op=ALU.mult)

        def projs(tg, j):
            uk = [('uT', q_) for q_ in range(4 * tg, 4 * tg + 4)]
            if j == 0:
                proj_fm_tg(slot_kv, key_kv, 0, evac_plain(kcT, 'kcT'), tg, uk)
            elif j == 1:
                proj_fm_tg(slot_kv, key_kv, 128, evac_plain(vcT, 'vcT'), tg, uk)
            elif j == 2:
                proj_fm_tg(slot_kv, key_kv, 256, evac_split(ksaug, 'ks'), tg, uk)
            else:
                proj_fm_tg(slot_kv, key_kv, 384, evac_split(kwaug, 'kw'), tg, uk)
            for it in range(4 * tg + j, 4 * tg + j + 1):
                bi = proj_bank()
                for k in range(8):
                    E('pe', 'matmul', reads=[key_tm, ('uT', it)], writes=[('bank', bi)], inc=(k == 7),
                      out=banks[bi][:, 0:288], lhsT=uT[:, k, it * 128:(it + 1) * 128], rhs=svtm[:, k, 0:288],
                      start=(k == 0), stop=(k == 7))
                E('dve', 'tensor_copy', reads=[('bank', bi), 'vsaug'], writes=[('vsaug', it)],
                  out=vsaug[:, it, :, 0:64], in_=banks[bi][:, 0:128].rearrange("p (g d) -> p g d", g=2))
                E('dve', 'tensor_copy', reads=[('bank', bi), 'vwaug'], writes=[('vwaug', it)],
                  out=vwaug[:, it, :, 0:64], in_=banks[bi][:, 128:256].rearrange("p (g d) -> p g d", g=2))
                E('dve', 'tensor_copy', reads=[('bank', bi)], writes=[('gates', it)], out=gates[:, it, :], in_=banks[bi][:, 256:280])
                E('dve', 'tensor_tensor', reads=[('bank', bi), 'prm'], writes=[('spx', it)],
                  out=spx.rearrange("p (h i) -> p h i", h=8)[:, :, it], in0=banks[bi][:, 280:288], in1=prm[:, 40:48], op=ALU.add)

        stageA(0)
        for i in range(16):
            if i + 1 < 16:
                stageA(i + 1)
            stageB(i)
            if init_ms:
                init_ms.pop(0)()
            if i >= 4:
                projs((i - 4) // 4, i % 4)
        while init_ms:
            init_ms.pop(0)()
        for j in range(4):
            projs(3, j)
        W.release(pos_kv)
        W.release(pos_tm)
        pos_w1v, slot_w1v, key_w1v = W.acquire('W1V')
        allg = [('gates', i) for i in range(16)]
        gflat = gates.rearrange("p a b -> p (a b)")
        E('act', 'activation', reads=allg, writes=['gates'], out=gflat, in_=gflat, func=AF.Exp, scale=-1.0)
        E('dve', 'tensor_scalar', reads=['gates'], writes=['gates'], out=gflat, in0=gflat, scalar1=1.0, scalar2=None, op0=ALU.add)
        E('dve', 'reciprocal', reads=['gates'], writes=['gates'], out=gflat, in_=gflat)
        allspx = [('spx', i) for i in range(16)]
        E('act', 'activation', reads=allspx, writes=['spx'], out=spx, in_=spx, func=AF.Exp, scale=-1.0)
        E('act', 'activation', reads=['spx'], writes=['spx'], out=spx, in_=spx, func=AF.Ln, bias=1.0)

        w1kv = slot_w1k.rearrange("p (i m) -> p i m", i=32)
        w1vv = slot_w1v.rearrange("p (i m) -> p i m", i=32)
        allkc = [('kcT', tg) for tg in range(4)]
        allvc = [('vcT', tg) for tg in range(4)]
        E('dve', 'tensor_copy', reads=allkc, writes=['kcr'], out=kcr, in_=kcT.rearrange("p (j r) -> p r j", r=16))
        E('dve', 'tensor_copy', reads=allvc, writes=['vcr'], out=vcr, in_=vcT.rearrange("p (j r) -> p r j", r=16))
        bi = proj_bank()
        for j, (wv, wkey) in enumerate(((w1kv, key_w1k), (w1vv, key_w1v))):
            for i in range(32):
                E('pe', 'matmul', reads=[wkey, 'posb'], writes=[('bank', bi)], inc=(i == 31),
                  out=banks[bi][:, j:j + 1], lhsT=wv[0:64, i, :], rhs=posb[0:64, j * 32 + i:j * 32 + i + 1],
                  start=(i == 0), stop=(i == 31))
        E('dve', 'tensor_copy', reads=[('bank', bi)], writes=['cbias'], out=cbias, in_=banks[bi][:, 0:2])
        chains = [(g, j) for g in range(2) for j in range(2)]
        for ci, (g, j) in enumerate(chains):
            wv, wkey, srcr, skey = ((w1kv, key_w1k, kcr, 'kcr'), (w1vv, key_w1v, vcr, 'vcr'))[j]
            for i in range(32):
                E('pe', 'matmul', reads=[wkey, skey], writes=[('bank', ci)], inc=(i == 31),
                  out=banks[ci][:, 0:127], lhsT=wv[g * 64:(g + 1) * 64, i, :],
                  rhs=srcr[g * 64:(g + 1) * 64, i % 16, (i // 16):(i // 16) + 127], start=(i == 0), stop=(i == 31))
        for hl in range(4):
            P.dma('sp', qn[hl][96:100, :], cd['c_alq'][hl], writes=[('qn', hl, 'alibi')])
        proj_fm(slot_qn, key_qn, 0, evac_split([qn[0], qn[1]], ('qnq', 0), scale=0.125))
        for ci, (g, j) in enumerate(chains):
            cs_ = ci % 2
            gx = gxs[cs_]
            ghb = ghb4[ci]
            gk = ['gx%d_%d' % (q_, cs_) for q_ in range(4)] + ['ghb%d' % ci]
            bi = ci
            E('act', 'activation', reads=[('bank', bi), 'cbias'], writes=[gk[0]], out=gx[0][:, 0:127], in_=banks[bi][:, 0:127],
              func=AF.Identity, bias=cbias[:, j:j + 1])
            E('dve', 'tensor_tensor', reads=[gk[0]], writes=[gk[1]], out=gx[1][:, 0:127], in0=gx[0][:, 0:127], in1=gx[0][:, 0:127], op=ALU.mult)
            E('dve', 'tensor_scalar', reads=[gk[1]], writes=[gk[1]], out=gx[1][:, 0:127], in0=gx[1][:, 0:127],
              scalar1=0.044715, scalar2=1.0, op0=ALU.mult, op1=ALU.add)
            E('dve', 'tensor_tensor', reads=[gk[1], gk[0]], writes=[gk[2]], out=gx[2][:, 0:127], in0=gx[1][:, 0:127], in1=gx[0][:, 0:127], op=ALU.mult)
            E('act', 'activation', reads=[gk[2]], writes=[gk[3]], out=gx[3][:, 0:127], in_=gx[2][:, 0:127], func=AF.Exp, scale=-1.5957691216057308)
            E('dve', 'tensor_scalar', reads=[gk[3]], writes=[gk[3]], out=gx[3][:, 0:127], in0=gx[3][:, 0:127], scalar1=1.0, scalar2=None, op0=ALU.add)
            E('dve', 'reciprocal', reads=[gk[3]], writes=[gk[3]], out=gx[3][:, 0:127], in_=gx[3][:, 0:127])
            E('dve', 'tensor_tensor', reads=[gk[3], gk[0]], writes=[gk[4]], out=ghb[:, 0:127], in0=gx[3][:, 0:127], in1=gx[0][:, 0:127], op=ALU.mult)
        proj_fm(slot_qn, key_qn, 128, evac_split([qn[2], qn[3]], ('qnq', 1), scale=0.125))
        for ci, (g, j) in enumerate(chains):
            ghb = ghb4[ci]
            bo = proj_bank()
            if j == 0:
                E('pe', 'matmul', reads=['ghb%d' % ci, 'w2b'], writes=[('bank', bo)], out=banks[bo][0:64, 0:127], lhsT=w2b[:, 0:64],
                  rhs=ghb[:, 0:127], start=True, stop=True)
                E('dve', 'tensor_copy', reads=[('bank', bo), ('kcmp', g)], writes=[('kcmp', g)], out=kcmpaug[g][0:64, 0:127], in_=banks[bo][0:64, 0:127])
            else:
                E('pe', 'matmul', reads=['ghb%d' % ci, 'w2b'], writes=[('bank', bo)], out=banks[bo][0:127, 0:64], lhsT=ghb[:, 0:127],
                  rhs=w2b[:, 64:128], start=True, stop=True)
                E('dve', 'tensor_copy', reads=[('bank', bo), ('vcmp', g)], writes=[('vcmp', g)], out=vcmpaug[g][0:127, 0:64], in_=banks[bo][0:127, 0:64])
        for g in range(2):
            P.dma('sp', kcmpaug[g][96:100, :], cd['c_alkc'], reads=[('kcmp', g)], writes=[('kcmp', g)])
            E('dve', 'memset', reads=[('vcmp', g)], writes=[('vcmp', g)], ap=vcmpaug[g][0:127, 64:65], constant=1.0)
            P.dma('sp', vcmpaug[g][:, 65:97], cd['c_overlap'], reads=[('vcmp', g)], writes=[('vcmp', g)])
        bi = proj_bank()
        E('pe', 'transpose', reads=['spx', 'identf'], writes=[('bank', bi)], out=banks[bi][:, 0:128], in_=spx, identity=identf[:, 0:128])
        E('dve', 'tensor_copy', reads=[('bank', bi)], writes=['tsum'], out=tsum, in_=banks[bi][:, 0:128])
        E('dve', 'tensor_tensor_scan', reads=['tsum', 'identf'], writes=['cref'], out=cref, data0=identf[:, 128:256], data1=tsum,
          initial=0.0, op0=ALU.mult, op1=ALU.add)
        bi = proj_bank()
        E('pe', 'matmul', reads=['cref', 'identf'], writes=[('bank', bi)], out=banks[bi][:, 0:1], lhsT=identf[:, 384:512], rhs=cref[:, 127:128],
          start=True, stop=True)
        E('dve', 'tensor_copy', reads=[('bank', bi)], writes=['cbase'], out=cbias2[:, 0:1], in_=banks[bi][:, 0:1])
        E('dve', 'tensor_scalar', reads=['cref', 'cbase'], writes=['cneg'], out=cneg, in0=cref, scalar1=cbias2[:, 0:1], scalar2=None, op0=ALU.add)
        E('dve', 'tensor_copy', reads=['cneg'], writes=['a0'], out=a123[0], in_=cneg)
        E('dve', 'tensor_tensor', reads=['cneg', 'a0'], writes=['r1'], out=r1, in0=cneg, in1=a123[0], op=ALU.subtract)
        E('dve', 'tensor_copy', reads=['r1'], writes=['a1'], out=a123[1], in_=r1)
        E('dve', 'tensor_tensor', reads=['r1', 'a1'], writes=['r1'], out=r1, in0=r1, in1=a123[1], op=ALU.subtract)
        E('dve', 'tensor_copy', reads=['r1'], writes=['a2'], out=a123[2], in_=r1)
        W.release(pos_w1k)
        W.release(pos_w1v)
        P.barrier()
        for a_ in range(2):
            E('dve', 'memset', writes=[(('fq', 0), a_, 'aug')], ap=fq[a_][64:70, :], constant=1.0)
            E('dve', 'memset', writes=[(('fk', 0), a_, 'aug')], ap=fk[a_][64:70, :], constant=-1.0)
        if debug:
            E('dve', 'memset', writes=['otm_init'], ap=otm, constant=0.0)
            E('dve', 'memset', writes=['imp_init'], ap=impacc, constant=0.0)
        dump('uT', uT_t[:, :])
        dump('kcmpaug0', kcmpaug[0])
        dump('vcmpaug0', vcmpaug[0])
        if stop_after == '1B':
            P.barrier()
            P.replay()
            return nc

        tiles = []
        ctr = {'S': 0, 'PT': 0, 'O': 0}
        NOB = 2
        Obv = [banks[4 + b][:, :].rearrange("p (s c) -> p s c", s=4) for b in range(NOB)]

        def add_unit(ktiles, kaug, kkeys_fn, qaug, qkeys, K, tg, v_fn, vkeys_fn, nv, first_fn, last_fn, post, pre=None, kparts=128):
            ob = ctr['O'] % NOB
            ctr['O'] += 1
            n = len(ktiles)
            for idx, (i, col0, ncols, mask) in enumerate(ktiles):
                d = {'pre': [], 'post': [], 'first': (idx == 0)}
                if idx == 0 and pre is not None:
                    d['pre'].append(pre)

                def S(i=i, col0=col0, ncols=ncols, mask=mask, d=d):
                    sb_ = ctr['S'] % 4
                    ctr['S'] += 1
                    d['sb'] = sb_
                    E('pe', 'matmul', reads=list(kkeys_fn(i)) + list(qkeys), writes=[('bank', sb_)], inc=(mask is None),
                      out=banks[sb_][0:kparts, 0:ncols], lhsT=kaug(i), rhs=qaug[0:K, tg * 512 + col0: tg * 512 + col0 + ncols],
                      start=True, stop=(mask is None))
                    if mask is not None:
                        if mask[0] == 'cmp':
                            E('pe', 'matmul', reads=['identb', 'cmpmask'], writes=[('bank', sb_)],
                              out=banks[sb_][:, 0:512], lhsT=identb[:, :], rhs=cmpmask[:, tg * 512:(tg + 1) * 512], start=False, stop=True)
                        else:
                            lc = mask[1]
                            mo = 0 if mask[0] == 'ge' else 128
                            E('pe', 'matmul', reads=['identb', 'trib'], writes=[('bank', sb_)],
                              out=banks[sb_][:, lc:lc + 128], lhsT=identb[:, :], rhs=trib[:, mo:mo + 128], start=False, stop=True)

                def EXP(ncols=ncols, d=d, mask=mask):
                    pt_ = ctr['PT'] % NPT
                    ctr['PT'] += 1
                    d['pt'] = pt_
                    E('act', 'activation', reads=[('bank', d['sb'])], writes=[('PT', pt_)],
                      out=PT[pt_][0:kparts, 0:ncols], in_=banks[d['sb']][0:kparts, 0:ncols], func=AF.Exp)

                def PV(i=i, col0=col0, ncols=ncols, d=d):
                    s0 = col0 // 128
                    s1 = (col0 + ncols) // 128
                    for s in range(s0, s1):
                        E('pe', 'matmul', reads=[('PT', d['pt'])] + list(vkeys_fn(i)), writes=[('O', ob)], inc=(s == s1 - 1),
                          out=Obv[ob][:, s, 0:nv], lhsT=PT[d['pt']][0:kparts, (s - s0) * 128:(s - s0 + 1) * 128], rhs=v_fn(i),
                          start=(d['first'] and s == s0), stop=(i == last_fn(s)), skip_group_check=True)
                d['S'] = S
                d['EXP'] = EXP
                d['PV'] = PV
                if idx == n - 1:
                    d['post'].append(lambda ob=ob: post(ob))
                tiles.append(d)

        def run_pipeline(depth=3):
            n = len(tiles)
            for t_ in range(min(depth, n)):
                for f in tiles[t_]['pre']:
                    f()
                tiles[t_]['S']()
            for t_ in range(n):
                tiles[t_]['EXP']()
                if t_ + depth < n:
                    for f in tiles[t_ + depth]['pre']:
                        f()
                    tiles[t_ + depth]['S']()
                tiles[t_]['PV']()
                for f in tiles[t_]['post']:
                    f()
            del tiles[:]

        def epi_rec(ob, clamp=True):
            if clamp:
                E('dve', 'tensor_scalar', reads=[('O', ob)], writes=['rec'], out=rec, in0=Obv[ob][:, :, 64],
                  scalar1=1e-30, scalar2=None, op0=ALU.max)
                E('dve', 'reciprocal', reads=['rec'], writes=['rec'], out=rec, in_=rec)
            else:
                E('dve', 'reciprocal', reads=[('O', ob)], writes=['rec'], out=rec, in_=Obv[ob][:, :, 64])

        def nsa_post(br, h, g, hl, tg, first_branch):
            def f(ob):
                epi_rec(ob, clamp=(br == 0))
                E('dve', 'tensor_tensor', reads=['rec'] + [('gates', i) for i in range(4 * tg, 4 * tg + 4)], writes=['fsc'],
                  out=fsc, in0=rec, in1=gates[:, 4 * tg:4 * tg + 4, br * 8 + h], op=ALU.mult)
                dst = otm[:, 4 * tg:4 * tg + 4, h * 64:(h + 1) * 64]
                okey = ('otm', h, tg)
                if first_branch:
                    E('dve', 'tensor_tensor', reads=[('O', ob), 'fsc'], writes=[okey], out=dst, in0=Obv[ob][:, :, 0:64],
                      in1=fsc.unsqueeze(2).to_broadcast([128, 4, 64]), op=ALU.mult)
                else:
                    ti = tmpo_ctr[0] % 2
                    tmpo_ctr[0] += 1
                    tmpo = tmpos[ti]
                    E('dve', 'tensor_tensor', reads=[('O', ob), 'fsc'], writes=[('tmpo', ti)], out=tmpo, in0=Obv[ob][:, :, 0:64],
                      in1=fsc.unsqueeze(2).to_broadcast([128, 4, 64]), op=ALU.mult)
                    E('pool', 'tensor_tensor', reads=[('tmpo', ti), okey], writes=[okey], out=dst, in0=dst, in1=tmpo, op=ALU.add)
                if br == 0:
                    idst = impacc[:, 4 * tg:4 * tg + 4, g, :]
                    ikey = ('imp', g, tg)
                    if hl == 0:
                        E('dve', 'tensor_tensor', reads=[('O', ob), 'rec'], writes=[ikey], out=idst, in0=Obv[ob][:, :, 65:97],
                          in1=rec.unsqueeze(2).to_broadcast([128, 4, 32]), op=ALU.mult)
                    else:
                        E('dve', 'tensor_tensor', reads=[('O', ob), 'rec'], writes=['tmpi'], out=tmpi, in0=Obv[ob][:, :, 65:97],
                          in1=rec.unsqueeze(2).to_broadcast([128, 4, 32]), op=ALU.mult)
                        E('dve', 'tensor_tensor', reads=['tmpi', ikey], writes=[ikey], out=idst, in0=idst, in1=tmpi, op=ALU.add)
            return f

        def fox_post(h, tg):
            def f(ob):
                epi_rec(ob, clamp=False)
                E('dve', 'tensor_tensor', reads=[('O', ob), 'rec'], writes=[('otm', 8 + h, tg)],
                  out=otm[:, 4 * tg:4 * tg + 4, 512 + h * 64:512 + (h + 1) * 64], in0=Obv[ob][:, :, 0:64],
                  in1=rec.unsqueeze(2).to_broadcast([128, 4, 64]), op=ALU.mult)
            return f

        def causal_ktiles(tg):
            kt = []
            for i in range(4 * tg + 4):
                r = i - 4 * tg
                if r < 0:
                    kt.append((i, 0, 512, None))
                else:
                    kt.append((i, 128 * r, 512 - 128 * r, ('ge', 0)))
            return kt

        def window_ktiles(tg):
            kt = []
            for i in range(max(4 * tg - 4, 0), 4 * tg + 4):
                r = i - 4 * tg
                if r >= 0:
                    kt.append((i, 128 * r, 512 - 128 * r, ('ge', 0)))
                else:
                    nc_ = 128 * (r + 5)
                    kt.append((i, 0, nc_, ('lt', nc_ - 128)))
            return kt

        topkV = topkVC[:, 0:512].rearrange("p (a b) -> p a b", a=16)
        topkC = topkVC[:, 512:1024].rearrange("p (a b) -> p a b", a=16)
        selbv = selb[:, 64:64 + 1024].rearrange("p (g a b) -> p g a b", g=2, a=16)

        def topk_prep(g):
            allimp = [('imp', g, tg) for tg in range(4)]
            E('dve', 'tensor_tensor', reads=allimp + ['topkVC'], writes=[('score', g)], out=score, in0=impacc[:, :, g, :], in1=topkV, op=ALU.mult)
            E('dve', 'tensor_tensor', reads=[('score', g), 'topkVC'], writes=[('score', g)], out=score, in0=score, in1=topkC, op=ALU.add)

        def topk_tile(g, i):
            E('dve', 'max', reads=[('score', g)], writes=['mx8'], out=mx8, in_=score[:, i, :])
            E('dve', 'match_replace', reads=[('score', g), 'mx8'], writes=['sc2'], out=sc2, in_to_replace=mx8, in_values=score[:, i, :], imm_value=-1e30)
            E('dve', 'max', reads=['sc2'], writes=['mx8'], out=mx8, in_=sc2)
            E('dve', 'tensor_scalar', reads=[('score', g), 'mx8'], writes=[('selb', g, i)], out=selbv[:, g, i, :], in0=score[:, i, :],
              scalar1=mx8[:, 7:8], scalar2=NEG, op0=ALU.is_lt, op1=ALU.mult)

        def sel_rows_group(g):
            for tg in range(4):
                bi = proj_bank()
                ptb = banks[bi][:, :].bitcast(BF16)
                for s in range(4):
                    i = 4 * tg + s
                    off = (g * 16 + i) * 32
                    E('pe', 'transpose', reads=[('selb', g, i), 'identb'], writes=[('bank', bi)], inc=(s == 3),
                      out=ptb[0:96, s * 128:(s + 1) * 128], in_=selb[:, off:off + 96], identity=identb[:, :])
                for hl in range(4):
                    E('dve', 'tensor_copy', reads=[('bank', bi)], writes=[('qn', hl, 'sel', tg)],
                      out=qn[hl][64:96, tg * 512:(tg + 1) * 512], in_=ptb[64:96, 0:512])

        FQ = [fq, [qn[0], qn[1]]]
        FK = [fk, [qn[2], qn[3]]]
        FV = [fv, vsaug]

        def fox_setup_pieces(p):
            st = p % 2
            holder = {}
            pieces = []

            def acq():
                holder['w'] = W.acquire('FOX%d' % p)
            for tg in range(4):
                def pc(tg=tg):
                    if 'w' not in holder:
                        acq()
                    pos_f, slot_f, key_f = holder['w']
                    proj_fm_tg(slot_f, key_f, 0, evac_split(FQ[st], ('fq', st), scale=0.125), tg)
                    proj_fm_tg(slot_f, key_f, 128, evac_split(FK[st], ('fk', st)), tg)
                pieces.append(pc)
            for tg in range(4):
                def pv(tg=tg):
                    pos_f, slot_f, key_f = holder['w']
                    svf = slot_f.rearrange("p (k n) -> p k n", k=8)
                    for i in range(4 * tg, 4 * tg + 4):
                        bi = proj_bank()
                        for k in range(8):
                            E('pe', 'matmul', reads=[key_f], writes=[('bank', bi)], inc=(k == 7),
                              out=banks[bi][:, 0:128], lhsT=uT[:, k, i * 128:(i + 1) * 128], rhs=svf[:, k, 256:384],
                              start=(k == 0), stop=(k == 7))
                        E('dve', 'tensor_copy', reads=[('bank', bi)], writes=[(('fv', st), i)],
                          out=FV[st][:, i, :, 0:64], in_=banks[bi][:, 0:128].rearrange("p (g d) -> p g d", g=2))
                    if tg == 3:
                        W.release(pos_f)
                pieces.append(pv)
            for a in range(2):
                def pa(a=a):
                    h = 2 * p + a
                    for j in range(3):
                        P.dma('sp', FQ[st][a][64 + j:65 + j, :], a123[j][16 * h:16 * h + 16, :], reads=['a%d' % j], writes=[(('fq', st), a, 'aug')])
                        P.dma('sp', FK[st][a][67 + j:68 + j, :], a123[j][16 * h:16 * h + 16, :], reads=['a%d' % j], writes=[(('fk', st), a, 'aug')])
                pieces.append(pa)
            return pieces


        for g in range(2):
            def pre_group(g=g):
                if g == 0:
                    return
                for hl in range(4):
                    P.dma('sp', qn[hl][96:100, :], cd['c_alq'][g * 4 + hl], writes=[('qn', hl, 'alibi')])
                for c in range(2):
                    proj_fm(slot_qn, key_qn, (2 * g + c) * 128, evac_split([qn[2 * c], qn[2 * c + 1]], ('qnq', c), scale=0.125))
                if g == 1:
                    W.release(pos_qn)

            def qkeys_of(hl, tg, sel):
                ks_ = [(('qnq', hl // 2), hl % 2, 'q', tg), ('qn', hl, 'alibi'), ('qn', hl, 'sel')]
                ks_.append(('qn', hl, 'sel', tg))
                return ks_
            for hl in range(4):
                h = g * 4 + hl
                for tg in range(4):
                    add_unit([(0, 0, 512, ('cmp',))], (lambda i, g=g: kcmpaug[g][0:100, 0:128]), (lambda i, g=g: [('kcmp', g)]),
                             qn[hl], qkeys_of(hl, tg, False), 100, tg, (lambda i, g=g: vcmpaug[g][:, 0:97]), (lambda i, g=g: [('vcmp', g)]), 97,
                             (lambda s: 0), (lambda s: 0), nsa_post(0, h, g, hl, tg, True),
                             pre=(pre_group if (hl == 0 and tg == 0) else None))
            tiles[-1]['post'].append(lambda g=g: (topk_prep(g), topk_tile(g, 0)))
            win_unit_ends = []
            for hl in range(4):
                h = g * 4 + hl
                for tg in range(4):
                    add_unit(window_ktiles(tg), (lambda i, g=g: kwaug[g][0:100, i * 128:(i + 1) * 128]),
                             (lambda i, g=g: [('kw', g, 'aug'), ('kw', g, 'E')] + [('kw', g, 'q', i // 4)]),
                             qn[hl], qkeys_of(hl, tg, False), 100, tg, (lambda i, g=g: vwaug[:, i, g, :]), (lambda i: [('vwaug', i), 'vwaug']), 65,
                             (lambda s, tg=tg: max(4 * tg + s - 4, 0)), (lambda s, tg=tg: 4 * tg + s), nsa_post(2, h, g, hl, tg, False))
                    win_unit_ends.append(len(tiles) - 1)
            for ui in range(15):
                tiles[win_unit_ends[ui]]['post'].append(lambda g=g, ui=ui: topk_tile(g, ui + 1))
            for hl in range(4):
                h = g * 4 + hl
                for tg in range(4):
                    add_unit(causal_ktiles(tg), (lambda i, g=g: ksaug[g][0:100, i * 128:(i + 1) * 128]),
                             (lambda i, g=g: [('ks', g, 'aug'), ('ks', g, 'E')] + [('ks', g, 'q', i // 4)]),
                             qn[hl], qkeys_of(hl, tg, True), 100, tg, (lambda i, g=g: vsaug[:, i, g, :]), (lambda i: [('vsaug', i), 'vsaug']), 65,
                             (lambda s: 0), (lambda s, tg=tg: 4 * tg + s), nsa_post(1, h, g, hl, tg, False),
                             pre=((lambda g=g: sel_rows_group(g)) if (hl == 0 and tg == 0) else None))
            if g == 1:
                pcs0 = fox_setup_pieces(0)
                base0 = len(tiles) - 3 * len(pcs0) - 4
                for j, pc_ in enumerate(pcs0):
                    tiles[base0 + 3 * j]['pre'].append(pc_)
            run_pipeline()
            if g == 0:
                dump('qn0', qn[0][0:100, :], reads=[(('qnq', 0), 0, 'q', tg) for tg in range(4)] + [('qn', 0, 'alibi'), ('qn', 0, 'sel')] + [('qn', 0, 'sel', tg) for tg in range(4)])
            if stop_after == 'NSA0':
                break
        dump('impacc', impacc.rearrange("p a b c -> p (a b c)"), reads=[('imp', g, tg) for g in range(2) for tg in range(4)])
        dump('selb', selb, reads=[('selb', g, i) for g in range(2) for i in range(16)])
        if stop_after in ('NSA0', 'NSA'):
            P.barrier()
            dump('otm', otm.rearrange("p a b -> p (a b)"))
            P.barrier()
            P.replay()
            return nc

        P.barrier()
        for a_ in range(2):
            E('dve', 'memset', writes=[(('fq', 1), a_, 'aug')], ap=FQ[1][a_][64:70, :], constant=1.0)
            E('dve', 'memset', writes=[(('fk', 1), a_, 'aug')], ap=FK[1][a_][64:70, :], constant=-1.0)
        pair_first_tile = []
        for p in range(4):
            st = p % 2
            pair_first_tile.append(len(tiles))
            for a in range(2):
                h = 2 * p + a
                for tg in range(4):
                    add_unit(causal_ktiles(tg), (lambda i, a=a, st=st: FK[st][a][0:70, i * 128:(i + 1) * 128]),
                             (lambda i, a=a, st=st: [(('fk', st), a, 'q', i // 4), (('fk', st), a, 'aug')]),
                             FQ[st][a], [(('fq', st), a, 'q', tg), (('fq', st), a, 'aug')], 70, tg,
                             (lambda i, a=a, st=st: FV[st][:, i, a, :]), (lambda i, st=st: [(('fv', st), i)]), 65,
                             (lambda s: 0), (lambda s, tg=tg: 4 * tg + s), fox_post(h, tg))
        for p in range(4):
            if p == 0:
                continue
            pcs = fox_setup_pieces(p)
            if True:
                base = pair_first_tile[p - 1] + 3
                for j, pc_ in enumerate(pcs):
                    tiles[base + 3 * j]['pre'].append(pc_)
        run_pipeline()
        dump('fq0', fq[0][0:70, :])
        dump('fk0', fk[0][0:70, :])
        P.barrier()
        dump('otm', otm.rearrange("p a b -> p (a b)"))
        if stop_after in ('FOX0', 'FOX'):
            P.barrier()
            P.replay()
            return nc

        cv2 = Carver()
        oT = cv2.alloc([8, T], BF16)
        hT = cv2.alloc([8, T], F32)
        yT = cv2.alloc([8, 1024], BF16)
        xs2 = [cv2.alloc([1024], F32) for _ in range(2)]
        sA = cv2.alloc([512], F32)
        sB = cv2.alloc([512], F32)
        t1 = cv2.alloc([512], F32)
        t2 = cv2.alloc([512], F32)
        sq = [cv2.alloc([512], BF16) for _ in range(4)]
        onesb = cv2.alloc([128], BF16)
        rstdt = cv2.alloc([512], F32)
        for i in range(16):
            b = 5 + (i % 2)
            ptb = banks[b][:, :].bitcast(BF16)
            for c in range(8):
                E('pe', 'transpose', writes=[('bank', b)], inc=(c == 7),
                  out=ptb[:, c * 128:(c + 1) * 128], in_=otm[:, i, c * 128:(c + 1) * 128], identity=identb[:, :])
            if i % 2 == 0:
                E('dve', 'tensor_copy', reads=[('bank', b)], writes=[('oT', i)], out=oT[:, :, i * 128:(i + 1) * 128],
                  in_=ptb.rearrange("p (c t) -> p c t", c=8))
            else:
                E('act', 'copy', reads=[('bank', b)], writes=[('oT', i)], out=oT[:, :, i * 128:(i + 1) * 128],
                  in_=ptb.rearrange("p (c t) -> p c t", c=8))
        P.barrier()
        dump('oT', oT.rearrange("p a b -> p (a b)"))
        E('dve', 'memset', writes=['onesb'], ap=onesb, constant=1.0)

        def xT_tile(i):
            b = i % 2
            P.dma('sp', xs2[b], x_d[i * 128:(i + 1) * 128, :], writes=[('xs2', b)])
            for half in range(2):
                bi = nbank()
                for c in range(4):
                    cc = half * 4 + c
                    E('pe', 'transpose', reads=[('xs2', b), 'identf'], writes=[('bank', bi)], inc=(c == 3),
                      out=banks[bi][:, c * 128:(c + 1) * 128], in_=xs2[b][:, cc * 128:(cc + 1) * 128], identity=identf[:, 0:128])
                if half == 0:
                    E('act', 'copy', reads=[('bank', bi)], writes=[('hT', i, half)], out=hT[:, half * 4:half * 4 + 4, i * 128:(i + 1) * 128],
                      in_=banks[bi][:, :].rearrange("p (c t) -> p c t", c=4))
                else:
                    E('dve', 'tensor_copy', reads=[('bank', bi)], writes=[('hT', i, half)], out=hT[:, half * 4:half * 4 + 4, i * 128:(i + 1) * 128],
                      in_=banks[bi][:, :].rearrange("p (c t) -> p c t", c=4))

        bk = [0]

        def nbank():
            bk[0] = (bk[0] + 1) % 8
            return bk[0]

        def mm_acc(bi, slotv, wkey, kchunks, col0, rhs_fn, rkeys):
            n = len(kchunks)
            for j, k in enumerate(kchunks):
                E('pe', 'matmul', reads=[wkey] + list(rkeys), writes=[('bank', bi)], inc=(j == n - 1),
                  out=banks[bi][:, :], lhsT=slotv[:, k, col0:col0 + 128], rhs=rhs_fn(k), start=(j == 0), stop=(j == n - 1))

        def rms_stats(tg):
            bi = nbank()
            for k in range(8):
                E('act', 'activation', reads=[('h', k, tg)], writes=[('sq', k % 4)], out=sq[k % 4], in_=hT[:, k, tg * 512:(tg + 1) * 512], func=AF.Square)
                E('pe', 'matmul', reads=[('sq', k % 4), 'onesb'], writes=[('bank', bi)], inc=True,
                  out=banks[bi][:, :], lhsT=onesb, rhs=sq[k % 4], start=(k == 0), stop=(k == 7))
            E('dve', 'tensor_scalar', reads=[('bank', bi)], writes=['rstdt'], out=rstdt, in0=banks[bi][:, :], scalar1=1.0 / D, scalar2=1e-6,
              op0=ALU.mult, op1=ALU.add)
            E('act', 'activation', reads=['rstdt'], writes=['rstdt'], out=rstdt, in_=rstdt, func=AF.Ln)
            E('act', 'activation', reads=['rstdt'], writes=['rstdt'], out=rstdt, in_=rstdt, func=AF.Exp, scale=-0.5)

        vT = uT
        for half in range(2):
            wpos = {}
            for m in range(8):
                for nm in ('MG%d' % (m // 2), 'BR%d' % (m // 4)):
                    if nm not in wpos:
                        wpos[nm] = W.acquire(nm)
                pmg, smg, kmg = wpos['MG%d' % (m // 2)]
                pbr, sbr, kbr = wpos['BR%d' % (m // 4)]
                smgv = smg.rearrange("p (k n) -> p k n", k=8)
                sbrv = sbr.rearrange("p (k n) -> p k n", k=8)
                for tl in range(2):
                    tg = half * 2 + tl
                    tsl = slice(tg * 512, (tg + 1) * 512)
                    bA, bB, bC, bD = nbank(), nbank(), nbank(), nbank()
                    mm_acc(bA, smgv, kmg, range(8), (m % 2) * 128, (lambda k, tsl=tsl: uT[:, k, tsl]), [])
                    mm_acc(bB, smgv, kmg, range(8), 256 + (m % 2) * 128, (lambda k, tsl=tsl: uT[:, k, tsl]), [])
                    mm_acc(bC, sbrv, kbr, range(0, 4), (m % 4) * 128, (lambda k, tsl=tsl: oT[:, k, tsl]), [])
                    mm_acc(bD, sbrv, kbr, range(4, 8), (m % 4) * 128, (lambda k, tsl=tsl: oT[:, k, tsl]), [])
                    E('act', 'activation', reads=[('bank', bA)], writes=['sA'], out=sA, in_=banks[bA][:, :], func=AF.Sigmoid, bias=prm[:, 24 + m:25 + m])
                    E('act', 'activation', reads=[('bank', bB)], writes=['sB'], out=sB, in_=banks[bB][:, :], func=AF.Sigmoid, bias=prm[:, 32 + m:33 + m])
                    E('dve', 'tensor_tensor', reads=['sA', ('bank', bC)], writes=['t1'], out=t1, in0=sA, in1=banks[bC][:, :], op=ALU.mult)
                    E('dve', 'tensor_tensor', reads=['sB', ('bank', bD)], writes=['t2'], out=t2, in0=sB, in1=banks[bD][:, :], op=ALU.mult)
                    E('dve', 'tensor_tensor', reads=['t1', 't2'], writes=[('yT', m, tl)], out=yT[:, m, tl * 512:(tl + 1) * 512], in0=t1, in1=t2, op=ALU.add)
                if m % 2 == 1:
                    W.release(pmg)
                if m % 4 == 3:
                    W.release(pbr)
                if half == 0:
                    xT_tile(2 * m)
                    xT_tile(2 * m + 1)
            if half == 0:
                dump('yT', yT.rearrange("p a b -> p (a b)"), reads=[('yT', m, tl) for m in range(8) for tl in range(2)])
            for nm in ('OUT0', 'OUT1'):
                wpos[nm] = W.acquire(nm)
            for tl in range(2):
                tg = half * 2 + tl
                for m in range(8):
                    po, so, ko = wpos['OUT%d' % (m // 4)]
                    sov = so.rearrange("p (k n) -> p k n", k=8)
                    bi = nbank()
                    mm_acc(bi, sov, ko, range(8), (m % 4) * 128, (lambda k, tl=tl: yT[:, k, tl * 512:(tl + 1) * 512]),
                           [('yT', k, tl) for k in range(8)])
                    E('dve', 'tensor_tensor', reads=[('bank', bi)] + [('hT', it, m // 4) for it in range(4 * tg, 4 * tg + 4)], writes=[('h', m, tg)],
                      out=hT[:, m, tg * 512:(tg + 1) * 512], in0=hT[:, m, tg * 512:(tg + 1) * 512], in1=banks[bi][:, :], op=ALU.add)
                if tl == 1:
                    W.release(wpos['OUT0'][0])
                    W.release(wpos['OUT1'][0])
                rms_stats(tg)
                for k in range(8):
                    E('dve', 'scalar_tensor_tensor', reads=[('h', k, tg), 'rstdt', 'prm'], writes=[('vT', k, tg)], out=vT[:, k, tg * 512:(tg + 1) * 512],
                      in0=hT[:, k, tg * 512:(tg + 1) * 512], scalar=prm[:, 8 + k:9 + k], in1=rstdt, op0=ALU.mult, op1=ALU.mult)

        aT = [oT[:, 0:4, :], oT[:, 4:8, :]]

        def mlp_up(fb):
            pu, su, ku = W.acquire('UP%d' % fb)
            suv = su.rearrange("p (k n) -> p k n", k=8)
            for tg in range(4):
                for c in range(4):
                    bi = nbank()
                    mm_acc(bi, suv, ku, range(8), c * 128, (lambda k, tg=tg: vT[:, k, tg * 512:(tg + 1) * 512]), [('vT', k, tg) for k in range(8)])
                    rl = sA if (c * 4 + tg) % 2 == 0 else sB
                    rk = 'sA' if (c * 4 + tg) % 2 == 0 else 'sB'
                    E('act', 'activation', reads=[('bank', bi)], writes=[rk], out=rl, in_=banks[bi][:, :], func=AF.Relu)
                    E('dve', 'tensor_tensor', reads=[rk], writes=[('aT', fb % 2, c, tg)], out=aT[fb % 2][:, c, tg * 512:(tg + 1) * 512],
                      in0=rl, in1=rl, op=ALU.mult)
            W.release(pu)

        def mlp_down(fb, tgs=(0, 1, 2, 3), release=True):
            if ('down', fb) not in wheld:
                wheld[('down', fb)] = W.acquire('DOWN%d' % fb)
            pd, sd, kd = wheld[('down', fb)]
            sdv = sd.rearrange("p (k n) -> p k n", k=4)
            for tg in tgs:
                for m in range(8):
                    bi = nbank()
                    for k in range(4):
                        E('pe', 'matmul', reads=[kd, ('aT', fb % 2, k, tg)], writes=[('bank', bi)], inc=(k == 3),
                          out=banks[bi][:, :], lhsT=sdv[:, k, m * 128:(m + 1) * 128], rhs=aT[fb % 2][:, k, tg * 512:(tg + 1) * 512],
                          start=(k == 0), stop=(k == 3))
                    E('dve', 'tensor_tensor', reads=[('bank', bi)], writes=[('h', m, tg)], out=hT[:, m, tg * 512:(tg + 1) * 512],
                      in0=hT[:, m, tg * 512:(tg + 1) * 512], in1=banks[bi][:, :], op=ALU.add)
            if release:
                W.release(pd)

        wheld = {}
        nrm_t = RG[:, (32768 + 65536) // 2:(32768 + 65536 + 16384) // 2].bitcast(F32).rearrange("p (a b) -> p a b", a=8)

        gfin_s = RG[:, (32768 + 65536) // 2:(32768 + 65536 + 4096) // 2].bitcast(F32)
        ssf = RG[:, (32768 + 65536 + 4096) // 2:(32768 + 65536 + 4096 + 128) // 2].bitcast(F32)
        rsf = RG[:, (32768 + 65536 + 4096 + 128) // 2:(32768 + 65536 + 4096 + 192) // 2].bitcast(F32)

        def final_tile(i):
            tg = i // 4
            ob_ = xs2[i % 2]
            bb = [nbank(), nbank()]
            for half in range(2):
                bi = bb[half]
                for c in range(4):
                    k = half * 4 + c
                    E('pe', 'transpose', reads=[('h', k, tg), 'identf'], writes=[('bank', bi)], inc=(c == 3),
                      out=banks[bi][:, c * 128:(c + 1) * 128], in_=hT[:, k, i * 128:(i + 1) * 128], identity=identf[:, 0:128])
                jk = sA if half == 0 else sB
                E('act', 'activation', reads=[('bank', bi), 'ssf'], writes=['sA' if half == 0 else 'sB', ('ssf', i, half)],
                  out=jk, in_=banks[bi][:, :], func=AF.Square, accum_out=ssf[:, 2 * i + half:2 * i + half + 1])
            E('dve', 'tensor_tensor', reads=[('ssf', i, 0), ('ssf', i, 1)], writes=[('rsf', i)], out=rsf[:, i:i + 1],
              in0=ssf[:, 2 * i:2 * i + 1], in1=ssf[:, 2 * i + 1:2 * i + 2], op=ALU.add)
            E('dve', 'tensor_scalar', reads=[('rsf', i)], writes=[('rsf', i)], out=rsf[:, i:i + 1], in0=rsf[:, i:i + 1],
              scalar1=1.0 / D, scalar2=1e-6, op0=ALU.mult, op1=ALU.add)
            E('act', 'activation', reads=[('rsf', i)], writes=[('rsf', i)], out=rsf[:, i:i + 1], in_=rsf[:, i:i + 1], func=AF.Ln)
            E('act', 'activation', reads=[('rsf', i)], writes=[('rsf', i)], out=rsf[:, i:i + 1], in_=rsf[:, i:i + 1], func=AF.Exp, scale=-0.5)
            for half in range(2):
                E('dve', 'scalar_tensor_tensor', reads=[('bank', bb[half]), ('rsf', i), 'gfin', ('ostage_rd', i % 2)], writes=[('ostage', i % 2, half)],
                  out=ob_[:, half * 512:(half + 1) * 512], in0=banks[bb[half]][:, :], scalar=rsf[:, i:i + 1],
                  in1=gfin_s[:, half * 512:(half + 1) * 512], op0=ALU.mult, op1=ALU.mult)
            P.dma('sp', out_d[i * 128:(i + 1) * 128, :], ob_, reads=[('ostage', i % 2, 0), ('ostage', i % 2, 1)], writes=[('ostage_rd', i % 2)])

        def final_tg(tg):
            for i in range(4 * tg, 4 * tg + 4):
                final_tile(i)

        P.dma('sp', gfin_s, gfin_d, writes=['gfin'] + [('yT', k_, tl_) for k_ in range(8) for tl_ in range(2)])
        E('dve', 'memset', reads=['gfin'], writes=['ssf'] + [('yT', k_, tl_) for k_ in range(8) for tl_ in range(2)], ap=ssf, constant=0.0)
        mlp_up(0)
        for fb in range(7):
            mlp_up(fb + 1)
            mlp_down(fb)
        mlp_down(7, tgs=(0,), release=False)
        mlp_down(7, tgs=(1,), release=False)
        final_tg(0)
        mlp_down(7, tgs=(2,), release=False)
        final_tg(1)
        mlp_down(7, tgs=(3,), release=True)
        final_tg(2)
        final_tg(3)
        P.barrier()
        P.replay()
    return nc


_CACHE = {}


def kernel(**inputs):
    import ml_dtypes
    wall, prm, gfin = host_weights(inputs)
    C = host_consts()
    base = {"wall": wall, "prm": prm, "gfin": gfin}
    for k, v in C.items():
        base[k] = v.astype(np.float32) if k in ('c_topk', 'c_identf') else v.astype(ml_dtypes.bfloat16)
    x = np.asarray(inputs['x'], np.float32)
    in_maps = []
    for b in range(8):
        m = dict(base)
        m['x'] = np.ascontiguousarray(x[b])
        in_maps.append(m)
    if 'nc' not in _CACHE:
        _CACHE['nc'] = build_program()
    res = run_bass_kernel_spmd(_CACHE['nc'], in_maps, core_ids=list(range(8)))
    return np.stack([np.asarray(r['out'], np.float32) for r in res.results], 0)
```
